# Optimizing a Trainium2 kernel written in Bass

```python
import jax, jax.numpy as jnp
from jax import lax
import numpy as np

D_MODEL = 4096
BATCH = 4
SEQ = 4096
DEPTH = 1

ATTN_HEAD_DIM = 128
ATTN_HEADS = D_MODEL // 256
ATTN_WIDTH = ATTN_HEADS * ATTN_HEAD_DIM
ROT_DIM = ATTN_HEAD_DIM // 4
ROPE_THETA = 500000.0
MOBA_BLOCK = 256
MOBA_TOPK = 3
ATTN_Q_BLOCK = 16

SSD_D_INNER = D_MODEL
SSD_HEAD_DIM = 64
SSD_HEADS = SSD_D_INNER // SSD_HEAD_DIM
SSD_GROUPS = 8
SSD_HEADS_PER_GROUP = SSD_HEADS // SSD_GROUPS
SSD_STATE = 128
SSD_CONV = 4
SSD_CHUNK = 256
SSD_CONV_DIM = SSD_D_INNER + 2 * SSD_GROUPS * SSD_STATE

N_BRANCHES = 2
IN_COLS = 4 * ATTN_WIDTH + SSD_D_INNER + SSD_CONV_DIM + SSD_HEADS + N_BRANCHES * D_MODEL
IN_SPLITS = [ATTN_WIDTH, 2 * ATTN_WIDTH, 3 * ATTN_WIDTH, 4 * ATTN_WIDTH,
             4 * ATTN_WIDTH + SSD_D_INNER,
             4 * ATTN_WIDTH + SSD_D_INNER + SSD_CONV_DIM,
             4 * ATTN_WIDTH + SSD_D_INNER + SSD_CONV_DIM + SSD_HEADS]
NORM_EPS = 1e-6

kernel_name = "moba_ssd_gated_hybrid"


def rms_norm(x, w):
    xf = x.astype(jnp.float32)
    y = xf * lax.rsqrt(jnp.mean(xf * xf, axis=-1, keepdims=True) + NORM_EPS)
    return (y * w.astype(jnp.float32)).astype(x.dtype)


def partial_rotary(t, pos):
    half = ROT_DIM // 2
    inv_freq = ROPE_THETA ** (-jnp.arange(half, dtype=jnp.float32) * 2.0 / ROT_DIM)
    ang = pos.astype(jnp.float32)[:, None] * inv_freq[None, :]
    cos, sin = jnp.cos(ang), jnp.sin(ang)
    tf = t.astype(jnp.float32)
    t1, t2 = tf[..., :half], tf[..., half:ROT_DIM]
    out = jnp.concatenate([t1 * cos - t2 * sin, t2 * cos + t1 * sin, tf[..., ROT_DIM:]], axis=-1)
    return out.astype(t.dtype)


def moba_attention(q, k, v):
    bsz, n_heads, seq, hd = q.shape
    n_blocks = -(-seq // MOBA_BLOCK)
    s_pad = n_blocks * MOBA_BLOCK
    pad = [(0, 0), (0, 0), (0, s_pad - seq), (0, 0)]
    q, k, v = jnp.pad(q, pad), jnp.pad(k, pad), jnp.pad(v, pad)
    k_blocks = k.reshape(bsz, n_heads, n_blocks, MOBA_BLOCK, hd)
    v_blocks = v.reshape(bsz, n_heads, n_blocks, MOBA_BLOCK, hd)
    k_mean = jnp.mean(k_blocks.astype(jnp.float32), axis=3)
    n_sel = min(MOBA_TOPK, n_blocks)
    scale = hd ** -0.5
    b_idx = jnp.arange(bsz)[:, None, None, None]
    h_idx = jnp.arange(n_heads)[None, :, None, None]

    def query_block(start):
        qc = lax.dynamic_slice_in_dim(q, start, ATTN_Q_BLOCK, axis=2)
        own = start // MOBA_BLOCK
        q_pos = start + jnp.arange(ATTN_Q_BLOCK)
        gate = jnp.einsum('bhqd,bhnd->bhqn', qc.astype(jnp.float32), k_mean)
        gate = jnp.where(jnp.arange(n_blocks) < own, gate, -jnp.inf)
        _, sel = lax.top_k(gate, n_sel)
        sel_valid = jnp.arange(n_sel) < own
        k_sel = k_blocks[b_idx, h_idx, sel]
        v_sel = v_blocks[b_idx, h_idx, sel]
        s_sel = jnp.einsum('bhqd,bhqkjd->bhqkj', qc, k_sel).astype(jnp.float32) * scale
        s_sel = jnp.where(sel_valid[:, None], s_sel, -jnp.inf)
        k_own = lax.dynamic_slice_in_dim(k, own * MOBA_BLOCK, MOBA_BLOCK, axis=2)
        v_own = lax.dynamic_slice_in_dim(v, own * MOBA_BLOCK, MOBA_BLOCK, axis=2)
        s_own = jnp.einsum('bhqd,bhjd->bhqj', qc, k_own).astype(jnp.float32) * scale
        k_pos = own * MOBA_BLOCK + jnp.arange(MOBA_BLOCK)
        s_own = jnp.where(k_pos[None, :] <= q_pos[:, None], s_own, -jnp.inf)
        scores = jnp.concatenate(
            [s_sel.reshape(bsz, n_heads, ATTN_Q_BLOCK, n_sel * MOBA_BLOCK), s_own], axis=-1)
        p = jax.nn.softmax(scores, axis=-1)
        p_sel = p[..., :n_sel * MOBA_BLOCK].reshape(bsz, n_heads, ATTN_Q_BLOCK, n_sel, MOBA_BLOCK)
        p_own = p[..., n_sel * MOBA_BLOCK:]
        o = (jnp.einsum('bhqkj,bhqkjd->bhqd', p_sel.astype(v.dtype), v_sel)
             + jnp.einsum('bhqj,bhjd->bhqd', p_own.astype(v.dtype), v_own))
        return o.astype(q.dtype)

    starts = jnp.arange(0, s_pad, ATTN_Q_BLOCK)
    out = lax.map(query_block, starts)
    out = jnp.moveaxis(out, 0, 2).reshape(bsz, n_heads, s_pad, hd)
    return out[:, :, :seq]


def ssd_chunked_scan(xdt, a, b_mat, c_mat):
    bsz, seq = xdt.shape[:2]
    n_chunks = -(-seq // SSD_CHUNK)
    pad = n_chunks * SSD_CHUNK - seq

    def to_chunks(t):
        t = jnp.pad(t, [(0, 0), (0, pad)] + [(0, 0)] * (t.ndim - 2))
        t = t.reshape(bsz, n_chunks, SSD_CHUNK, *t.shape[2:])
        return jnp.moveaxis(t, 1, 0)

    xs = to_chunks(xdt.reshape(bsz, seq, SSD_GROUPS, SSD_HEADS_PER_GROUP, SSD_HEAD_DIM))
    as_ = to_chunks(a.reshape(bsz, seq, SSD_GROUPS, SSD_HEADS_PER_GROUP))
    bs = to_chunks(b_mat)
    cs = to_chunks(c_mat)
    causal = jnp.tril(jnp.ones((SSD_CHUNK, SSD_CHUNK), dtype=bool))

    def step(state, inp):
        xc, ac, bc, cc = inp
        acum = jnp.cumsum(ac, axis=1)
        seg = acum[:, :, None] - acum[:, None, :]
        decay = jnp.exp(jnp.where(causal[None, :, :, None, None], seg, -jnp.inf))
        cb = jnp.einsum('blgn,bsgn->blsg', cc, bc)
        y_diag = jnp.einsum('blsg,blsgh,bsghp->blghp', cb, decay, xc)
        y_off = jnp.einsum('blgn,bghpn->blghp', cc, state) * jnp.exp(acum)[..., None]
        a_tot = acum[:, -1]
        w = jnp.exp(a_tot[:, None] - acum)
        new_state = (state * jnp.exp(a_tot)[..., None, None]
                     + jnp.einsum('bsgn,bsgh,bsghp->bghpn', bc, w, xc))
        return new_state, y_diag + y_off

    state0 = jnp.zeros((bsz, SSD_GROUPS, SSD_HEADS_PER_GROUP, SSD_HEAD_DIM, SSD_STATE), jnp.float32)
    _, ys = lax.scan(step, state0, (xs, as_, bs, cs))
    ys = jnp.moveaxis(ys, 0, 1).reshape(bsz, n_chunks * SSD_CHUNK, SSD_HEADS, SSD_HEAD_DIM)
    return ys[:, :seq]


def ssd_branch(z, xbc, dt_raw, conv_w, conv_b, dt_bias, a_log, d_skip, norm_w):
    bsz, seq, _ = xbc.shape
    xbc = lax.conv_general_dilated(
        xbc, conv_w[:, None, :], window_strides=(1,), padding=[(SSD_CONV - 1, 0)],
        dimension_numbers=('NWC', 'WIO', 'NWC'), feature_group_count=SSD_CONV_DIM)
    xbc = jax.nn.silu((xbc + conv_b).astype(jnp.float32))
    xs, b_mat, c_mat = jnp.split(xbc, [SSD_D_INNER, SSD_D_INNER + SSD_GROUPS * SSD_STATE], axis=-1)
    xs = xs.reshape(bsz, seq, SSD_HEADS, SSD_HEAD_DIM)
    b_mat = b_mat.reshape(bsz, seq, SSD_GROUPS, SSD_STATE)
    c_mat = c_mat.reshape(bsz, seq, SSD_GROUPS, SSD_STATE)
    dt = jax.nn.softplus(dt_raw.astype(jnp.float32) + dt_bias.astype(jnp.float32))
    a = dt * (-jnp.exp(a_log.astype(jnp.float32)))
    y = ssd_chunked_scan(xs * dt[..., None], a, b_mat, c_mat) + d_skip.astype(jnp.float32)[:, None] * xs
    y = y.reshape(bsz, seq, SSD_D_INNER) * jax.nn.silu(z.astype(jnp.float32))
    yg = y.reshape(bsz, seq, SSD_GROUPS, SSD_D_INNER // SSD_GROUPS)
    yg = yg * lax.rsqrt(jnp.mean(yg * yg, axis=-1, keepdims=True) + NORM_EPS)
    return (yg.reshape(bsz, seq, SSD_D_INNER) * norm_w.astype(jnp.float32)).astype(z.dtype)


def setup_inputs(seed: int = 0) -> dict:
    key = jax.random.key(seed)
    ks = jax.random.split(key, 16)
    nrm = jax.random.normal
    f32 = jnp.float32
    x = nrm(ks[0], (BATCH, SEQ, D_MODEL), f32)
    norm_w = 1.0 + 0.02 * nrm(ks[1], (DEPTH, D_MODEL), f32)
    w_in = nrm(ks[2], (DEPTH, D_MODEL, IN_COLS), f32) * D_MODEL ** -0.5
    conv_w = nrm(ks[3], (DEPTH, SSD_CONV, SSD_CONV_DIM), f32) * SSD_CONV ** -0.5
    conv_b = 0.02 * nrm(ks[4], (DEPTH, SSD_CONV_DIM), f32)
    u = jax.random.uniform(ks[5], (DEPTH, SSD_HEADS), f32)
    dt0 = jnp.exp(u * (jnp.log(0.1) - jnp.log(0.001)) + jnp.log(0.001))
    dt_bias = dt0 + jnp.log(-jnp.expm1(-dt0))
    a_log = jnp.log(jax.random.uniform(ks[6], (DEPTH, SSD_HEADS), f32, 1.0, 16.0))
    d_skip = 1.0 + 0.1 * nrm(ks[7], (DEPTH, SSD_HEADS), f32)
    ssd_norm_w = 1.0 + 0.02 * nrm(ks[8], (DEPTH, SSD_D_INNER), f32)
    w_attn_out = nrm(ks[9], (DEPTH, ATTN_WIDTH, D_MODEL), f32) * ATTN_WIDTH ** -0.5
    w_ssd_out = nrm(ks[10], (DEPTH, SSD_D_INNER, D_MODEL), f32) * SSD_D_INNER ** -0.5
    gate_bias = 0.02 * nrm(ks[11], (DEPTH, N_BRANCHES * D_MODEL), f32)
    w_out = nrm(ks[12], (DEPTH, D_MODEL, D_MODEL), f32) * D_MODEL ** -0.5
    final_norm_w = 1.0 + 0.02 * nrm(ks[13], (D_MODEL,), f32)
    return {"x": x, "norm_w": norm_w, "w_in": w_in, "conv_w": conv_w, "conv_b": conv_b,
            "dt_bias": dt_bias, "a_log": a_log, "d_skip": d_skip, "ssd_norm_w": ssd_norm_w,
            "w_attn_out": w_attn_out, "w_ssd_out": w_ssd_out, "gate_bias": gate_bias,
            "w_out": w_out, "final_norm_w": final_norm_w}


def reference(x, norm_w, w_in, conv_w, conv_b, dt_bias, a_log, d_skip, ssd_norm_w,
              w_attn_out, w_ssd_out, gate_bias, w_out, final_norm_w):
    bsz, seq, _ = x.shape
    pos = jnp.arange(seq)
    h = x
    for l in range(DEPTH):
        u = rms_norm(h, norm_w[l])
        proj = jnp.einsum('bsd,dc->bsc', u, w_in[l])
        q, k, v, g_attn, z, xbc, dt_raw, g_merge = jnp.split(proj, IN_SPLITS, axis=-1)

        def to_heads(t):
            return t.reshape(bsz, seq, ATTN_HEADS, ATTN_HEAD_DIM).transpose(0, 2, 1, 3)
        qh = partial_rotary(to_heads(q), pos)
        kh = partial_rotary(to_heads(k), pos)
        o = moba_attention(qh, kh, to_heads(v))
        o = o.transpose(0, 2, 1, 3).reshape(bsz, seq, ATTN_WIDTH)
        o = (o.astype(jnp.float32) * jax.nn.silu(g_attn.astype(jnp.float32))).astype(x.dtype)
        y_attn = jnp.einsum('bsc,cd->bsd', o, w_attn_out[l])

        y_ssd = ssd_branch(z, xbc, dt_raw, conv_w[l], conv_b[l], dt_bias[l], a_log[l],
                           d_skip[l], ssd_norm_w[l])
        y_ssd = jnp.einsum('bsc,cd->bsd', y_ssd, w_ssd_out[l])

        gm = jax.nn.sigmoid((g_merge + gate_bias[l]).astype(jnp.float32))
        merged = (gm[..., :D_MODEL] * y_attn.astype(jnp.float32)
                  + gm[..., D_MODEL:] * y_ssd.astype(jnp.float32)).astype(x.dtype)
        h = h + jnp.einsum('bsd,de->bse', merged, w_out[l])
    return rms_norm(h, final_norm_w)
```

```python
import numpy as np
import concourse.bass as bass
import concourse.mybir as mybir
from concourse.bass_utils import run_bass_kernel_spmd

F32 = mybir.dt.float32
BF16 = mybir.dt.bfloat16
AF = mybir.ActivationFunctionType
ALU = mybir.AluOpType
AX = mybir.AxisListType

D = 4096
KC = D // 128
AW = 2048
NH = 16
HD = 128
SSD_IN = 4096
SH = 64
SP_ = 64
SG = 8
SN = 128
CONVD = SSD_IN + 2 * SG * SN
IN_COLS = 4 * AW + SSD_IN + CONVD + SH + 2 * D
OFF_Q, OFF_K, OFF_V, OFF_G = 0, AW, 2 * AW, 3 * AW
OFF_Z = 4 * AW
OFF_X = OFF_Z + SSD_IN
OFF_B = OFF_X + SSD_IN
OFF_C = OFF_B + SG * SN
OFF_DT = OFF_C + SG * SN
OFF_GA = OFF_DT + SH
OFF_GS = OFF_GA + D
EPS = 1e-6
BIG = 30000.0
BLK = 256
ROPE_THETA = 500000.0


class Buf:
    __slots__ = ("name", "st", "sem", "cnt", "psum")

    def __init__(self, name, psum=False):
        self.name = name
        self.psum = psum
        self.st = {}
        self.sem = None
        self.cnt = 0


class Prog:
    COMPUTE = ("pe", "act", "dve")
    QUEUES = ("sp", "pool")

    def __init__(self, nc):
        self.nc = nc
        self.ops = {e: [] for e in self.COMPUTE + self.QUEUES}
        self.sems = {}
        self.pending = {e: [] for e in self.COMPUTE + self.QUEUES}
        self.dma_bufs = []
        self.nsem = 0

    def new_sem(self, name):
        s = self.nc.semaphore(name)
        h = s.__enter__()
        self.nsem += 1
        return h

    def setup(self):
        for e in self.COMPUTE:
            self.sems[e] = self.new_sem("sem_" + e)

    def _conf(self, buf, key):
        if key is None:
            return list(buf.st.values())
        r = []
        if key in buf.st:
            r.append(buf.st[key])
        if None in buf.st:
            r.append(buf.st[None])
        return r

    def _deps(self, reads, writes, eng=None):
        deps = []
        for (b, k) in reads:
            for ent in self._conf(b, k):
                if ent[0] is not None:
                    deps.append(ent[0])
                if b.psum:
                    for t in ent[1].values():
                        if not (t[0] == "eng" and t[1] == eng):
                            deps.append(t)
        for (b, k) in writes:
            for ent in self._conf(b, k):
                if ent[0] is not None:
                    deps.append(ent[0])
                deps.extend(ent[1].values())
        return deps

    @staticmethod
    def _addreader(d, tok):
        key = (tok[0], tok[1] if tok[0] == "eng" else id(tok[1]))
        if key not in d or d[key][2] < tok[2]:
            d[key] = tok

    def _update(self, tok, reads, writes):
        for (b, k) in reads:
            if k not in b.st:
                b.st[k] = [None, {}]
            self._addreader(b.st[k][1], tok)
        for (b, k) in writes:
            if k is None:
                b.st = {None: [tok, {}]}
            else:
                b.st[k] = [tok, {}]
                if None in b.st:
                    pass

    def _mkwaits(self, eng, deps):
        w = {}
        for t in deps:
            if t[0] == "eng":
                if t[1] == eng and eng == "pe":
                    continue
                key = ("eng", t[1])
                if key not in w or w[key] < t[2]:
                    w[key] = t[2]
                self.ops[t[1]][t[2]]["flag"] = True
            else:
                key = ("dma", id(t[1]), t[1])
                if key not in w or w[key] < t[2]:
                    w[key] = t[2]
        return w

    @staticmethod
    def _norm(r):
        b, k = r if isinstance(r, tuple) else (r, None)
        if b.psum:
            k = None
        return (b, k)

    def op(self, eng, fn, reads=(), writes=()):
        reads = [self._norm(r) for r in reads]
        writes = [self._norm(r) for r in writes]
        deps = self._deps(reads, writes, eng) + self.pending[eng]
        self.pending[eng] = []
        idx = len(self.ops[eng])
        self.ops[eng].append({"fn": fn, "waits": self._mkwaits(eng, deps), "flag": False, "dma": None})
        self._update(("eng", eng, idx), reads, writes)

    def dma(self, q, out_ap, in_ap, slot, reads=(), writes=()):
        reads = [self._norm(r) for r in reads]
        writes = [self._norm(r) for r in writes]
        deps = self._deps(reads, writes, q) + self.pending[q]
        self.pending[q] = []
        if slot.sem is None:
            slot.sem = self.new_sem("dsem_" + slot.name)
            self.dma_bufs.append(slot)
        slot.cnt += 16
        tok = ("dma", slot.sem, slot.cnt)

        def fn(e, out_ap=out_ap, in_ap=in_ap):
            return e.dma_start(out=out_ap, in_=in_ap)
        self.ops[q].append({"fn": fn, "waits": self._mkwaits(q, deps), "flag": False, "dma": (slot.sem, 16)})
        self._update(tok, reads, writes)

    def barrier(self):
        toks = []
        for e in self.COMPUTE:
            if self.ops[e]:
                toks.append(("eng", e, len(self.ops[e]) - 1))
        for b in self.dma_bufs:
            toks.append(("dma", b.sem, b.cnt))
        for e in self.COMPUTE + self.QUEUES:
            self.pending[e] = self.pending[e] + toks

    def final_wait_tokens(self):
        return [("dma", b.sem, b.cnt) for b in self.dma_bufs]

    def emit(self):
        nc = self.nc
        cnts = {}
        for e in self.COMPUTE:
            c = 0
            arr = []
            for o in self.ops[e]:
                if o["flag"]:
                    c += 1
                arr.append(c)
            cnts[e] = arr
            assert c < 60000, (e, c)
        for b in self.dma_bufs:
            assert b.cnt < 60000, (b.name, b.cnt)
        engobj = {"pe": "tensor", "act": "scalar", "dve": "vector", "sp": "sync", "pool": "gpsimd"}
        with nc.Block() as block:
            for e in self.COMPUTE + self.QUEUES:
                ops = self.ops[e]
                sems = self.sems

                def body(E, ops=ops, e=e):
                    waited = {}
                    for o in ops:
                        for key, val in o["waits"].items():
                            if key[0] == "eng":
                                sem = sems[key[1]]
                                v = cnts[key[1]][val]
                                wk = key
                            else:
                                sem = key[2]
                                v = val
                                wk = key[:2]
                            if waited.get(wk, -1) >= v:
                                continue
                            waited[wk] = v
                            E.wait_ge(sem, v)
                        ins = o["fn"](E)
                        if o["dma"] is not None:
                            ins.then_inc(o["dma"][0], o["dma"][1])
                        elif o["flag"]:
                            ins.then_inc(sems[e], 1)
                getattr(block, engobj[e])(body)


class Arena:
    def __init__(self, ap, nwords):
        self.ap = ap
        self.n = nwords
        self.off = 0
        self.mark_ = 0

    def mark(self):
        self.mark_ = self.off

    def reset(self):
        self.off = self.mark_

    def alloc(self, nelem, dtype=F32, parts=128):
        words = nelem if dtype == F32 else (nelem + 1) // 2
        words = (words + 7) // 8 * 8
        assert self.off + words <= self.n, ("arena overflow", self.off, words, self.n)
        a = self.ap[0:parts, self.off:self.off + words]
        self.off += words
        if dtype != F32:
            a = a.bitcast(dtype)
        return a[:, 0:nelem]


def _r3(ap, b):
    return ap.rearrange("p (a b) -> p a b", b=b)


def build_program(TP, TM, debug=False, stop_after=None):
    T = TP + TM
    NBLK = T // BLK
    nc = bass.Bass("TRN2", target_bir_lowering=False)
    P = Prog(nc)

    def din(name, shape, dt=F32):
        return nc.dram_tensor(name, list(shape), dt, kind="ExternalInput").ap()

    skind = "ExternalOutput" if debug else "Internal"

    def dscr(name, shape, dt):
        return nc.dram_tensor(name, list(shape), dt, kind=skind).ap()

    x_all = din("x_all", [T, D])
    import os as _os
    _wc = int(_os.environ.get('KDBG_WCOLS', IN_COLS))
    _small = 'KDBG_WCOLS' in _os.environ
    w_in = din("w_in", [D, _wc])
    w_ao = din("w_ao", [128 if _small else AW, D])
    w_so = din("w_so", [128 if _small else SSD_IN, D])
    w_o = din("w_o", [128 if _small else D, D])
    c_normw = din("c_normw", [128, KC])
    c_convw = din("c_convw", [128, 48 * 4])
    c_convb = din("c_convb", [128, 48])
    c_gbias = din("c_gbias", [128, 64])
    c_dtb = din("c_dtb", [128, SH])
    c_alog = din("c_alog", [128, SH])
    c_dsk = din("c_dsk", [128, SSD_IN])
    c_snw = din("c_snw", [128, SSD_IN])
    c_fnw = din("c_fnw", [128, D])
    c_cos = din("c_cos", [128, T])
    c_sin = din("c_sin", [128, T])
    c_ident = din("c_ident", [128, 128])
    c_rot = din("c_rot", [32, 128])
    c_tri = din("c_tri", [128, 128])
    c_ustr = din("c_ustr", [128, 128])
    c_caus = din("c_caus", [128, 4 * 512])
    c_onehot = din("c_onehot", [NBLK, NBLK * 128])
    c_gb = din("c_gb", [128, (TM // 128) * NBLK])
    c_ob = din("c_ob", [128, (TM // 128) * NBLK])
    c_flag = din("c_flag", [128, 1])

    out = nc.dram_tensor("out", [TM, D], F32, kind="ExternalOutput").ap()

    s_qT = dscr("s_qT", [AW, TM], BF16)
    s_kT = dscr("s_kT", [AW, T], BF16)
    s_v = dscr("s_v", [T, AW], BF16)
    s_sg = dscr("s_sg", [TM, AW], BF16)
    s_sz = dscr("s_sz", [TM, SSD_IN], BF16)
    s_xs = dscr("s_xs", [T, SSD_IN], BF16)
    s_bT = dscr("s_bT", [SG * SN, T], BF16)
    s_btm = dscr("s_btm", [T, SG * SN], BF16)
    s_cT = dscr("s_cT", [SG * SN, T], BF16)
    s_dt = dscr("s_dt", [T, SH], F32)
    s_sgm = dscr("s_sgm", [2 * D, TM], BF16)
    s_ogT = dscr("s_ogT", [AW, TM], BF16)
    s_ynT = dscr("s_ynT", [SSD_IN, TM], BF16)

    AW_WORDS = 51 * 1024
    arena_g = nc.sbuf_tensor("arena", [128, AW_WORDS], F32)
    arena_t = arena_g.__enter__()
    AR = Arena(arena_t, AW_WORDS)
    psg = [nc.psum_tensor("ps%d" % i, [128, 512], F32) for i in range(8)]
    PS = [g.__enter__() for g in psg]
    PSB = [Buf("ps%d" % i, psum=True) for i in range(8)]
    P.setup()

    def cload(name, src, nelem, dt=F32, parts=128, cast=False):
        ap = AR.alloc(nelem, dt, parts)
        b = Buf(name)
        P.dma("pool" if cast else "sp", ap, src, b, writes=[b])
        return ap, b

    ident_f, identf_b = cload("identf", c_ident, 128)
    ident, ident_b = cload("ident", c_ident, 128, BF16, cast=True)
    rot, rot_b = cload("rot", c_rot, 128, BF16, parts=32, cast=True)
    tri, tri_b = cload("tri", c_tri, 128)
    ustr, ustr_b = cload("ustr", c_ustr, 128)
    flag, flag_b = cload("flag", c_flag, 1)
    ssf = AR.alloc((TM // 128) * 8)
    ssf_b = Buf("ssf")
    AR.mark()

    SCALE = HD ** -0.5

    def finalize():
        P.pending["sp"] = P.pending["sp"] + P.final_wait_tokens()
        P.ops["sp"].append({"fn": lambda E: E.nop(), "waits": P._mkwaits("sp", P.pending["sp"]), "flag": False, "dma": None})
        print("nsem", P.nsem, {e: len(P.ops[e]) for e in P.ops})
        P.emit()
        return nc


    TCH = 1024
    normw, normw_b = cload("normw", c_normw, KC)
    convw, convw_b = cload("convw", c_convw, 48 * 4)
    convb, convb_b = cload("convb", c_convb, 48)
    gbias, gbias_b = cload("gbias", c_gbias, 64)
    dtb, dtb_b = cload("dtb", c_dtb, SH)
    uT = _r3(AR.alloc(KC * TCH, BF16), TCH)
    uT_b = Buf("uT")
    wsl = [_r3(AR.alloc(KC * 512, BF16), 512) for _ in range(2)]
    wsl_b = [[Buf("w%d_%d" % (s_, q_)) for q_ in range(4)] for s_ in range(2)]
    xt = [AR.alloc(D) for _ in range(1)]
    xt_b = [Buf("xt0")]
    xn = AR.alloc(D, BF16)
    xn_b = Buf("xn")
    ssq = AR.alloc(8)
    ssq_b = Buf("ssq")
    halo = AR.alloc(48 * 3)
    halo_b = Buf("halo")
    cst = [AR.alloc(520) for _ in range(2)]
    cst_b = [Buf("cst0"), Buf("cst1")]
    acc = [AR.alloc(512) for _ in range(2)]
    acc_b = [Buf("acc0"), Buf("acc1")]
    NST = 8
    stg = [AR.alloc(512, BF16) for _ in range(NST)]
    stg_b = [Buf("stg%d" % i) for i in range(NST)]
    stt = [AR.alloc(512, BF16) for _ in range(2)]
    stt_b = [Buf("stt0"), Buf("stt1")]
    NRT = 4
    rt1s = [AR.alloc(512) for _ in range(NRT)]
    rt1s_b = [Buf("rt1_%d" % i) for i in range(NRT)]
    rt2 = AR.alloc(512)
    rt2_b = Buf("rt2")
    cosb = AR.alloc(TCH)
    sinb = AR.alloc(TCH)
    cos_b, sin_b = Buf("cos"), Buf("sin")
    dts = [AR.alloc(64 * 4) for _ in range(2)]
    dts_b = [Buf("dts0"), Buf("dts1")]

    P.op("dve", lambda E: E.memset(halo, 0.0), writes=[halo_b])

    state = {"ps": 0, "st": 0, "stt": 0, "cst": 0, "w": 0, "dts": 0, "rt": 0}
    deferred = []
    LAG = 2

    def defer(fn):
        deferred.append(fn)
        while len(deferred) > LAG:
            deferred.pop(0)()

    def drain(n=1):
        for _ in range(n):
            if deferred:
                deferred.pop(0)()


    def next_ps(lo=0, hi=5):
        i = lo + state["ps"] % (hi - lo)
        state["ps"] += 1
        return i

    def next_stg():
        i = state["st"] % NST
        state["st"] += 1
        return i

    def groups_for(is_prefix):
        g = []
        for i in range(4):
            g.append(("K", OFF_K + 512 * i, 512, i))
        for i in range(4):
            g.append(("V", OFF_V + 512 * i, 512, i))
        for i in range(8):
            g.append(("X", OFF_X + 512 * i, 512, i))
        for i in range(2):
            g.append(("B", OFF_B + 512 * i, 512, i))
        g.append(("DT", OFF_DT, 64, 0))
        for i in range(2):
            g.append(("C", OFF_C + 512 * i, 512, i))
        if not is_prefix:
            for i in range(4):
                g.append(("Q", OFF_Q + 512 * i, 512, i))
            for i in range(4):
                g.append(("G", OFF_G + 512 * i, 512, i))
            for i in range(8):
                g.append(("Z", OFF_Z + 512 * i, 512, i))
            for i in range(16):
                g.append(("GM", OFF_GA + 512 * i, 512, i))
        return g

    w_in_r = w_in.rearrange("(kc p) c -> p kc c", p=128)

    def load_w(slot, src_r, c0, ncols, nkc=KC):
        for q in range(0, nkc, 8):
            P.dma("pool", wsl[slot][:, q:q + 8, 0:ncols], src_r[:, q:q + 8, c0:c0 + ncols],
                  wsl_b[slot][q // 8], writes=[wsl_b[slot][q // 8]])

    def fm_store(src_ap, sbuf_b, dst, row0, col0, n=512):
        P.dma("sp", dst[row0:row0 + 128, col0:col0 + n], src_ap, sbuf_b, reads=[sbuf_b])

    def transposed_store(src_bf, src_b, dst, tok0, col0):
        pi = 6 + state["stt"] % 2
        si = state["stt"] % 2
        state["stt"] += 1
        psb = PS[pi].bitcast(BF16)
        for j in range(4):
            P.op("pe", lambda E, j=j, psb=psb: E.transpose(psb[:, j * 128:(j + 1) * 128],
                                                           src_bf[:, j * 128:(j + 1) * 128], ident),
                 reads=[src_b, ident_b], writes=[(PSB[pi], j)])
        P.op("act", lambda E, psb=psb, si=si: E.copy(out=stt[si], in_=psb[:, 0:512]),
             reads=[PSB[pi]], writes=[stt_b[si]])
        P.dma("sp", dst[tok0:tok0 + 512, col0:col0 + 128].rearrange("(j p) c -> p j c", p=128),
              _r3(stt[si], 128), stt_b[si], reads=[stt_b[si]])

    import os as _os
    n_chunks = int(_os.environ.get('KDBG_CHUNKS', T // TCH))
    for tc in range(n_chunks):
        t0 = tc * TCH
        is_prefix = t0 < TP
        tm0 = t0 - TP
        drain(len(deferred))
        P.dma("sp", cosb, c_cos[:, t0:t0 + TCH], cos_b, writes=[cos_b])
        P.dma("sp", sinb, c_sin[:, t0:t0 + TCH], sin_b, writes=[sin_b])
        for tt in range(TCH // 128):
            r0 = t0 + tt * 128
            P.dma("sp", xt[0], x_all[r0:r0 + 128, :], xt_b[0], writes=[xt_b[0]])
            P.op("act", lambda E: E.activation(out=xn, in_=xt[0], func=AF.Square, accum_out=ssq[:, 0:1]),
                 reads=[xt_b[0]], writes=[xn_b, (ssq_b, 0)])
            P.op("dve", lambda E: E.tensor_scalar(out=ssq[:, 1:2], in0=ssq[:, 0:1], scalar1=1.0 / D, scalar2=EPS,
                                                  op0=ALU.mult, op1=ALU.add),
                 reads=[(ssq_b, 0)], writes=[(ssq_b, 1)])
            P.op("act", lambda E: E.sqrt(out=ssq[:, 3:4], in_=ssq[:, 1:2]), reads=[(ssq_b, 1)], writes=[(ssq_b, 3)])
            P.op("dve", lambda E: E.reciprocal(out=ssq[:, 2:3], in_=ssq[:, 3:4]), reads=[(ssq_b, 3)], writes=[(ssq_b, 2)])
            P.op("act", lambda E: E.activation(out=xn, in_=xt[0], func=AF.Copy, scale=ssq[:, 2:3]),
                 reads=[xt_b[0], (ssq_b, 2)], writes=[xn_b])
            for j in range(8):
                pi = 6 + j % 2
                psb = PS[pi].bitcast(BF16)
                for q in range(4):
                    kc = 4 * j + q
                    P.op("pe", lambda E, psb=psb, q=q, kc=kc: E.transpose(psb[:, q * 128:(q + 1) * 128],
                                                                          xn[:, kc * 128:(kc + 1) * 128], ident),
                         reads=[xn_b, ident_b], writes=[(PSB[pi], q)])
                P.op("dve", lambda E, psb=psb, j=j, tt=tt: E.tensor_tensor(
                    out=uT[:, 4 * j:4 * j + 4, tt * 128:(tt + 1) * 128],
                    in0=_r3(psb[:, 0:512], 128),
                    in1=normw[:, 4 * j:4 * j + 4].unsqueeze(2).to_broadcast([128, 4, 128]), op=ALU.mult),
                     reads=[PSB[pi], normw_b], writes=[(uT_b, tt)])

        glist = groups_for(is_prefix)[:int(_os.environ.get('KDBG_GROUPS', 1000))]
        if not glist:
            continue
        load_w(state["w"] % 2, w_in_r, glist[0][1], glist[0][2])
        for gi, (kind, c0, ncols, gidx) in enumerate(glist):
            slot = state["w"] % 2
            state["w"] += 1
            if gi + 1 < len(glist):
                load_w(state["w"] % 2, w_in_r, glist[gi + 1][1], glist[gi + 1][2])
            W = wsl[slot]
            Wb = wsl_b[slot]
            if kind in ("K", "Q", "X", "B", "C", "GM"):
                for ct in range(4):
                    for th in range(TCH // 512):
                        pi = next_ps()
                        for kc in range(KC):
                            P.op("pe", lambda E, pi=pi, kc=kc, ct=ct, th=th, W=W: E.matmul(
                                PS[pi][:, :], lhsT=W[:, kc, ct * 128:(ct + 1) * 128],
                                rhs=uT[:, kc, th * 512:(th + 1) * 512], start=(kc == 0), stop=(kc == KC - 1)),
                                 reads=[Wb[kc // 8], uT_b], writes=[PSB[pi]])
                        tok0 = t0 + th * 512
                        if kind in ("K", "Q"):
                            si = next_stg()
                            P.op("act", lambda E, pi=pi, si=si: E.copy(out=stg[si], in_=PS[pi][:, :]),
                                 reads=[PSB[pi]], writes=[stg_b[si]])
                            ri_ = state["rt"] % NRT
                            state["rt"] += 1
                            rt1, rt1_b = rt1s[ri_], rt1s_b[ri_]
                            P.op("dve", lambda E, pi=pi, th=th, rt1=rt1: E.tensor_tensor(
                                out=rt1, in0=PS[pi][:, :], in1=cosb[:, th * 512:(th + 1) * 512], op=ALU.mult),
                                 reads=[PSB[pi], cos_b], writes=[rt1_b])

                            def part_b(si=si, th=th, rt1=rt1, rt1_b=rt1_b, kind=kind, gidx=gidx, ct=ct, tok0=tok0):
                                pr = 5
                                P.op("pe", lambda E: E.matmul(PS[pr][:, :], lhsT=rot, rhs=stg[si][0:32, :], start=True, stop=True),
                                     reads=[stg_b[si], rot_b], writes=[PSB[pr]])
                                P.op("dve", lambda E: E.tensor_tensor(
                                    out=rt2, in0=PS[pr][:, :], in1=sinb[:, th * 512:(th + 1) * 512], op=ALU.mult),
                                     reads=[PSB[pr], sin_b], writes=[rt2_b])
                                P.op("dve", lambda E: E.tensor_tensor(out=stg[si], in0=rt1, in1=rt2, op=ALU.add),
                                     reads=[rt1_b, rt2_b, stg_b[si]], writes=[stg_b[si]])
                                row0 = gidx * 512 + ct * 128
                                if kind == "K":
                                    fm_store(stg[si], stg_b[si], s_kT, row0, tok0)
                                else:
                                    fm_store(stg[si], stg_b[si], s_qT, row0, tok0 - TP)
                            defer(part_b)
                        elif kind == "GM":
                            si = next_stg()
                            til = gidx * 4 + ct
                            P.op("act", lambda E, pi=pi, si=si, til=til: E.activation(
                                out=stg[si], in_=PS[pi][:, :], func=AF.Sigmoid, bias=gbias[:, til:til + 1]),
                                 reads=[PSB[pi], gbias_b], writes=[stg_b[si]])
                            fm_store(stg[si], stg_b[si], s_sgm, til * 128, tok0 - TP)
                        else:
                            cti = {"X": 0, "B": 32, "C": 40}[kind] + gidx * 4 + ct
                            ci = state["cst"] % 2
                            state["cst"] += 1
                            P.op("dve", lambda E, ci=ci, cti=cti: E.tensor_copy(out=cst[ci][:, 0:3], in_=halo[:, cti * 3:cti * 3 + 3]),
                                 reads=[(halo_b, cti)], writes=[(cst_b[ci], "h")])
                            P.op("act", lambda E, ci=ci, pi=pi: E.copy(out=cst[ci][:, 3:515], in_=PS[pi][:, :]),
                                 reads=[PSB[pi]], writes=[(cst_b[ci], "m")])
                            P.op("dve", lambda E, ci=ci, cti=cti: E.tensor_copy(out=halo[:, cti * 3:cti * 3 + 3], in_=cst[ci][:, 512:515]),
                                 reads=[(cst_b[ci], "m")], writes=[(halo_b, cti)])
                            P.op("dve", lambda E, ci=ci, cti=cti: E.tensor_scalar(
                                out=acc[ci], in0=cst[ci][:, 0:512], scalar1=convw[:, cti * 4:cti * 4 + 1], scalar2=None, op0=ALU.mult),
                                 reads=[cst_b[ci], convw_b], writes=[acc_b[ci]])
                            for k in range(1, 4):
                                P.op("dve", lambda E, ci=ci, cti=cti, k=k: E.scalar_tensor_tensor(
                                    out=acc[ci], in0=cst[ci][:, k:k + 512], scalar=convw[:, cti * 4 + k:cti * 4 + k + 1],
                                    in1=acc[ci], op0=ALU.mult, op1=ALU.add),
                                     reads=[cst_b[ci], convw_b, acc_b[ci]], writes=[acc_b[ci]])
                            si = next_stg()
                            P.op("act", lambda E, ci=ci, si=si, cti=cti: E.activation(
                                out=stg[si], in_=acc[ci], func=AF.Silu, bias=convb[:, cti:cti + 1]),
                                 reads=[acc_b[ci], convb_b], writes=[stg_b[si]])
                            col0 = gidx * 512 + ct * 128

                            def part_b(si=si, kind=kind, col0=col0, tok0=tok0):
                                if kind == "X":
                                    transposed_store(stg[si], stg_b[si], s_xs, tok0, col0)
                                elif kind == "B":
                                    fm_store(stg[si], stg_b[si], s_bT, col0, tok0)
                                    transposed_store(stg[si], stg_b[si], s_btm, tok0, col0)
                                else:
                                    fm_store(stg[si], stg_b[si], s_cT, col0, tok0)
                            defer(part_b)
            else:
                for tt in range(TCH // 128):
                    pi = next_ps()
                    for kc in range(KC):
                        P.op("pe", lambda E, pi=pi, kc=kc, tt=tt, W=W, ncols=ncols: E.matmul(
                            PS[pi][:, 0:ncols], lhsT=uT[:, kc, tt * 128:(tt + 1) * 128],
                            rhs=W[:, kc, 0:ncols], start=(kc == 0), stop=(kc == KC - 1)),
                             reads=[Wb[kc // 8], uT_b], writes=[PSB[pi]])
                    tok0 = t0 + tt * 128
                    if kind == "DT":
                        di = state["dts"] % 2
                        state["dts"] += 1
                        d3 = _r3(dts[di], 64)
                        db = dts_b[di]
                        P.op("dve", lambda E, pi=pi, d3=d3: E.tensor_tensor(out=d3[:, 0, :], in0=PS[pi][:, 0:64], in1=dtb, op=ALU.add),
                             reads=[PSB[pi], dtb_b], writes=[(db, 0)])
                        P.op("act", lambda E, d3=d3: E.activation(out=d3[:, 1, :], in_=d3[:, 0, :], func=AF.Abs),
                             reads=[(db, 0)], writes=[(db, 1)])
                        P.op("act", lambda E, d3=d3: E.activation(out=d3[:, 2, :], in_=d3[:, 1, :], func=AF.Exp, scale=-1.0),
                             reads=[(db, 1)], writes=[(db, 2)])
                        P.op("act", lambda E, d3=d3: E.activation(out=d3[:, 1, :], in_=d3[:, 2, :], func=AF.Ln, bias=1.0),
                             reads=[(db, 2)], writes=[(db, 1)])
                        P.op("dve", lambda E, d3=d3: E.scalar_tensor_tensor(out=d3[:, 3, :], in0=d3[:, 0, :], scalar=0.0, in1=d3[:, 1, :],
                                                                            op0=ALU.max, op1=ALU.add),
                             reads=[(db, 0), (db, 1)], writes=[(db, 3)])
                        P.dma("sp", s_dt[tok0:tok0 + 128, :], d3[:, 3, :], db, reads=[(db, 3)])
                    else:
                        si = next_stg()
                        if kind == "V":
                            P.op("act", lambda E, pi=pi, si=si: E.copy(out=stg[si], in_=PS[pi][:, :]),
                                 reads=[PSB[pi]], writes=[stg_b[si]])
                            dst = s_v[tok0:tok0 + 128, gidx * 512:(gidx + 1) * 512]
                        else:
                            P.op("act", lambda E, pi=pi, si=si: E.activation(out=stg[si], in_=PS[pi][:, :], func=AF.Silu),
                                 reads=[PSB[pi]], writes=[stg_b[si]])
                            dd = s_sg if kind == "G" else s_sz
                            dst = dd[tok0 - TP:tok0 - TP + 128, gidx * 512:(gidx + 1) * 512]
                        P.dma("sp", dst, stg[si], stg_b[si], reads=[stg_b[si]])
                    drain(1)

    drain(len(deferred))
    P.barrier()
    if stop_after == 1:
        return finalize()
    AR.reset()
    for b in PSB:
        b.st = {}

    NQT = TM // 128
    NQC = TM // 512
    NKT = T // 128
    PKT = TP // 128
    gb_c, gb_b = cload("gb", c_gb, NQT * NBLK)
    ob_c, ob_b = cload("ob", c_ob, NQT * NBLK)
    caus, caus_b = cload("caus", c_caus, 4 * 512, BF16, cast=True)
    onehot, onehot_b = cload("onehot", c_onehot, NBLK * 128, BF16, parts=NBLK, cast=True)
    kT = [AR.alloc(T, BF16) for _ in range(2)]
    kT_b = [Buf("kT0"), Buf("kT1")]
    qT = [AR.alloc(TM, BF16) for _ in range(2)]
    qT_b = [Buf("qT0"), Buf("qT1")]
    va = [_r3(AR.alloc(NKT * 132, BF16), 132) for _ in range(2)]
    va_b = [Buf("va0"), Buf("va1")]
    sgt = [_r3(AR.alloc(NQT * 128, BF16), 128) for _ in range(2)]
    sgt_b = [Buf("sg0"), Buf("sg1")]
    kmf = AR.alloc(NBLK)
    kmf_b = Buf("kmf")
    kmb = AR.alloc(NBLK, BF16)
    kmb_b = Buf("kmb")
    NG = NQT * NBLK
    gA = AR.alloc(NG)
    gB = AR.alloc(NG)
    gC = AR.alloc(NG)
    gM = AR.alloc(NQT)
    gA_b, gB_b, gC_b, gM_b = Buf("gA"), Buf("gB"), Buf("gC"), Buf("gM")
    gbf = AR.alloc(NG, BF16)
    gbf_b = Buf("gbf")
    biasT = AR.alloc(TM, BF16, NBLK)
    biasT_b = Buf("biasT")
    NPT = 6
    pt = [AR.alloc(512, BF16) for _ in range(NPT)]
    pt_b = [Buf("pt%d" % i) for i in range(NPT)]
    rinv = AR.alloc(8)
    rinv_b = Buf("rinv")
    ogs = [AR.alloc(128, BF16) for _ in range(4)]
    ogs_b = [Buf("ogs%d" % i) for i in range(4)]
    ogst = [AR.alloc(512, BF16) for _ in range(2)]
    ogst_b = [Buf("ogst0"), Buf("ogst1")]

    for s in range(2):
        P.op("dve", lambda E, s=s: E.memset(va[s][:, :, 128:129], 1.0), writes=[(va_b[s], "one")])

    def attn_load(h):
        s = h % 2
        P.dma("sp", kT[s], s_kT[h * 128:(h + 1) * 128, :], kT_b[s], writes=[kT_b[s]])
        P.dma("sp", qT[s], s_qT[h * 128:(h + 1) * 128, :], qT_b[s], writes=[qT_b[s]])
        P.dma("sp", va[s][:, :, 0:128], s_v[:, h * 128:(h + 1) * 128].rearrange("(t p) c -> p t c", p=128),
              va_b[s], writes=[(va_b[s], "v")])
        P.dma("sp", sgt[s], s_sg[:, h * 128:(h + 1) * 128].rearrange("(t p) c -> p t c", p=128),
              sgt_b[s], writes=[sgt_b[s]])

    attn_load(0)
    pt_i = 0
    og_i = 0
    for h in range(NH):
        s = h % 2
        if h + 1 < NH:
            attn_load(h + 1)
        K_, Q_, V_, SGt = kT[s], qT[s], va[s], sgt[s]
        P.op("dve", lambda E, K_=K_: E.tensor_reduce(out=kmf, in_=_r3(K_, BLK), axis=AX.X, op=ALU.add),
             reads=[kT_b[s]], writes=[kmf_b])
        P.op("act", lambda E: E.activation(out=kmb, in_=kmf, func=AF.Copy, scale=1.0 / BLK),
             reads=[kmf_b], writes=[kmb_b])
        for qt in range(NQT):
            P.op("pe", lambda E, qt=qt, Q_=Q_: E.matmul(PS[7][:, qt * NBLK:(qt + 1) * NBLK], lhsT=Q_[:, qt * 128:(qt + 1) * 128],
                                                        rhs=kmb, start=True, stop=True),
                 reads=[qT_b[s], kmb_b], writes=[(PSB[7], qt)])
        g3 = lambda a: _r3(a, NBLK)
        mb = lambda: gM.unsqueeze(2).to_broadcast([128, NQT, NBLK])
        P.op("dve", lambda E: E.tensor_tensor(out=gA, in0=PS[7][:, 0:NG], in1=gb_c, op=ALU.add),
             reads=[PSB[7], gb_b], writes=[gA_b])
        P.op("dve", lambda E: E.tensor_reduce(out=gM, in_=g3(gA), axis=AX.X, op=ALU.max), reads=[gA_b], writes=[gM_b])
        P.op("dve", lambda E: E.tensor_tensor(out=g3(gC), in0=g3(gA), in1=mb(), op=ALU.is_ge), reads=[gA_b, gM_b], writes=[gC_b])
        P.op("dve", lambda E: E.scalar_tensor_tensor(out=gB, in0=gC, scalar=-BIG, in1=gA, op0=ALU.mult, op1=ALU.add),
             reads=[gC_b, gA_b], writes=[gB_b])
        P.op("dve", lambda E: E.tensor_reduce(out=gM, in_=g3(gB), axis=AX.X, op=ALU.max), reads=[gB_b], writes=[gM_b])
        P.op("dve", lambda E: E.tensor_tensor(out=g3(gC), in0=g3(gB), in1=mb(), op=ALU.is_ge), reads=[gB_b, gM_b], writes=[gC_b])
        P.op("dve", lambda E: E.scalar_tensor_tensor(out=gB, in0=gC, scalar=-BIG, in1=gB, op0=ALU.mult, op1=ALU.add),
             reads=[gC_b, gB_b], writes=[gB_b])
        P.op("dve", lambda E: E.tensor_reduce(out=gM, in_=g3(gB), axis=AX.X, op=ALU.max), reads=[gB_b], writes=[gM_b])
        P.op("dve", lambda E: E.tensor_tensor(out=g3(gC), in0=g3(gA), in1=mb(), op=ALU.is_ge), reads=[gA_b, gM_b], writes=[gC_b])
        P.op("dve", lambda E: E.tensor_scalar(out=gB, in0=gC, scalar1=BIG, scalar2=-BIG, op0=ALU.mult, op1=ALU.add),
             reads=[gC_b], writes=[gB_b])
        P.op("dve", lambda E: E.tensor_tensor(out=gA, in0=gB, in1=gb_c, op=ALU.add), reads=[gB_b, gb_b, gA_b], writes=[gA_b])
        P.op("dve", lambda E: E.tensor_tensor(out=gbf, in0=gA, in1=ob_c, op=ALU.max), reads=[gA_b, ob_b], writes=[gbf_b])
        for half in range(NQT // 8):
            psb = PS[7].bitcast(BF16)
            for q8 in range(8):
                qt = half * 8 + q8
                P.op("pe", lambda E, psb=psb, q8=q8, qt=qt: E.transpose(psb[0:NBLK, q8 * 128:(q8 + 1) * 128],
                                                                        gbf[:, qt * NBLK:(qt + 1) * NBLK], ident),
                     reads=[gbf_b, ident_b], writes=[(PSB[7], q8)])
            P.op("act", lambda E, psb=psb, half=half: E.copy(out=biasT[:, half * 1024:(half + 1) * 1024], in_=psb[0:NBLK, 0:1024]),
                 reads=[PSB[7]], writes=[(biasT_b, half)])
        for qc in range(NQC):
            nkt = PKT + 4 * (qc + 1)

            def emit_s(kt, qc=qc):
                nonlocal pt_i
                pi = (0, 1, 6)[kt % 3]
                blk = kt // 2
                P.op("pe", lambda E, pi=pi, kt=kt, qc=qc, K_=K_, Q_=Q_: E.matmul(
                    PS[pi][:, :], lhsT=K_[:, kt * 128:(kt + 1) * 128], rhs=Q_[:, qc * 512:(qc + 1) * 512], start=True, stop=False),
                     reads=[kT_b[s], qT_b[s]], writes=[PSB[pi]])
                P.op("pe", lambda E, pi=pi, blk=blk, qc=qc: E.matmul(
                    PS[pi][:, :], lhsT=onehot[:, blk * 128:(blk + 1) * 128], rhs=biasT[:, qc * 512:(qc + 1) * 512], start=False, stop=True),
                     reads=[onehot_b, biasT_b], writes=[PSB[pi]])
                pj = pt_i % NPT
                pt_i += 1
                P.op("act", lambda E, pi=pi, pj=pj: E.activation(out=pt[pj], in_=PS[pi][:, :], func=AF.Exp, scale=SCALE),
                     reads=[PSB[pi]], writes=[pt_b[pj]])
                dj = kt - (PKT + 4 * qc)
                if dj >= 0:
                    P.op("dve", lambda E, pj=pj, dj=dj: E.tensor_tensor(out=pt[pj], in0=pt[pj], in1=caus[:, dj * 512:(dj + 1) * 512], op=ALU.mult),
                         reads=[pt_b[pj], caus_b], writes=[pt_b[pj]])
                return pj

            def emit_pv(kt, pj, nkt=nkt):
                for qs in range(4):
                    P.op("pe", lambda E, qs=qs, pj=pj, kt=kt, nkt=nkt, V_=V_: E.matmul(
                        PS[2 + qs][:, 0:129], lhsT=pt[pj][:, qs * 128:(qs + 1) * 128], rhs=V_[:, kt, 0:129],
                        start=(kt == 0), stop=(kt == nkt - 1)),
                         reads=[pt_b[pj], va_b[s]], writes=[PSB[2 + qs]])

            pend = []
            for kt in range(nkt):
                pend.append((kt, emit_s(kt)))
                if len(pend) > 2:
                    emit_pv(*pend.pop(0))
            while pend:
                emit_pv(*pend.pop(0))
            oi = og_i % 2
            og_i += 1
            psb = PS[7].bitcast(BF16)
            for qs in range(4):
                qt = qc * 4 + qs
                P.op("dve", lambda E, qs=qs: E.reciprocal(out=rinv[:, qs:qs + 1], in_=PS[2 + qs][:, 128:129]),
                     reads=[PSB[2 + qs]], writes=[(rinv_b, qs)])
                P.op("dve", lambda E, qs=qs, qt=qt, SGt=SGt: E.scalar_tensor_tensor(
                    out=ogs[qs], in0=PS[2 + qs][:, 0:128], scalar=rinv[:, qs:qs + 1], in1=SGt[:, qt, :], op0=ALU.mult, op1=ALU.mult),
                     reads=[PSB[2 + qs], (rinv_b, qs), sgt_b[s]], writes=[ogs_b[qs]])
                P.op("pe", lambda E, qs=qs, psb=psb: E.transpose(psb[:, qs * 128:(qs + 1) * 128], ogs[qs], ident),
                     reads=[ogs_b[qs], ident_b], writes=[(PSB[7], qs)])
            P.op("act", lambda E, psb=psb, oi=oi: E.copy(out=ogst[oi], in_=psb[:, 0:512]), reads=[PSB[7]], writes=[ogst_b[oi]])
            P.dma("sp", s_ogT[h * 128:(h + 1) * 128, qc * 512:(qc + 1) * 512], ogst[oi], ogst_b[oi], reads=[ogst_b[oi]])

    P.barrier()
    if stop_after == 2:
        return finalize()
    AR.reset()
    for b in PSB:
        b.st = {}

    alog, alog_b = cload("alog", c_alog, SH)
    dsk, dsk_b = cload("dsk", c_dsk, SSD_IN)
    snw, snw_b = cload("snw", c_snw, SSD_IN)
    Aneg = AR.alloc(SH)
    Aneg_b = Buf("Aneg")
    P.op("act", lambda E: E.activation(out=Aneg, in_=alog, func=AF.Exp), reads=[alog_b], writes=[Aneg_b])
    P.op("dve", lambda E: E.tensor_single_scalar(out=Aneg, in_=Aneg, scalar=-1.0, op=ALU.mult), reads=[Aneg_b], writes=[Aneg_b])
    xs_t = [AR.alloc(SSD_IN, BF16) for _ in range(2)]
    xs_b = [Buf("xs0"), Buf("xs1")]
    btm_t = [AR.alloc(SG * SN, BF16) for _ in range(2)]
    btm_b = [Buf("btm0"), Buf("btm1")]
    bT_t = [AR.alloc(SG * 128, BF16) for _ in range(2)]
    bT_b = [Buf("bT0"), Buf("bT1")]
    cT_t = [AR.alloc(SG * 128, BF16) for _ in range(2)]
    cT_b = [Buf("cT0"), Buf("cT1")]
    dt_t = [AR.alloc(SH) for _ in range(2)]
    dt_b = [Buf("dt0"), Buf("dt1")]
    sz_t = [AR.alloc(SSD_IN, BF16) for _ in range(2)]
    sz_b = [Buf("sz0"), Buf("sz1")]
    stS = AR.alloc(SSD_IN)
    stS_b = Buf("state")
    stB = AR.alloc(SSD_IN, BF16)
    stB_b = Buf("stateb")
    a_tm = AR.alloc(SH)
    a_b = Buf("a_tm")
    acs = AR.alloc(SH)
    acs_b = Buf("acs")
    E_tm = AR.alloc(SH)
    E_b = Buf("E_tm")
    wl = AR.alloc(SH)
    wl_b = Buf("wl")
    w_tm = AR.alloc(SH)
    w_b = Buf("w_tm")
    Etot = AR.alloc(SH)
    Etot_b = Buf("Etot")
    dtw = AR.alloc(SH)
    dtw_b = Buf("dtw")
    xdt = AR.alloc(SSD_IN, BF16)
    xdt_b = Buf("xdt")
    xdtw = AR.alloc(SSD_IN, BF16)
    xdtw_b = Buf("xdtw")
    rall = [AR.alloc(512) for _ in range(2)]
    rall_b = [Buf("rall0"), Buf("rall1")]
    decT = [AR.alloc(512, BF16) for _ in range(2)]
    decT_b = [Buf("decT0"), Buf("decT1")]
    cbm = [AR.alloc(128, BF16) for _ in range(2)]
    cbm_b = [Buf("cbm0"), Buf("cbm1")]
    MT = [AR.alloc(512, BF16) for _ in range(2)]
    MT_b = [Buf("MT0"), Buf("MT1")]
    yv = [AR.alloc(512) for _ in range(2)]
    yv_b = [Buf("yv0"), Buf("yv1")]
    y2 = [AR.alloc(512) for _ in range(2)]
    y2_b = [Buf("y20"), Buf("y21")]
    ysq = AR.alloc(512)
    ysq_b = Buf("ysq")
    ssn = AR.alloc(8)
    ssn_b = Buf("ssn")
    ynb = [AR.alloc(512, BF16) for _ in range(2)]
    ynb_b = [Buf("ynb0"), Buf("ynb1")]
    ynst = [AR.alloc(512, BF16) for _ in range(2)]
    ynst_b = [Buf("ynst0"), Buf("ynst1")]

    P.op("dve", lambda E: E.memset(stS, 0.0), writes=[stS_b])
    P.op("dve", lambda E: E.memset(stB, 0.0), writes=[stB_b])

    NCH = T // 128
    PCH = TP // 128

    def ssd_load(c):
        s = c % 2
        r0 = c * 128
        P.dma("sp", xs_t[s], s_xs[r0:r0 + 128, :], xs_b[s], writes=[xs_b[s]])
        P.dma("sp", btm_t[s], s_btm[r0:r0 + 128, :], btm_b[s], writes=[btm_b[s]])
        P.dma("sp", _r3(bT_t[s], 128), s_bT[:, r0:r0 + 128].rearrange("(g n) s -> n g s", n=128), bT_b[s], writes=[bT_b[s]])
        P.dma("sp", dt_t[s], s_dt[r0:r0 + 128, :], dt_b[s], writes=[dt_b[s]])
        if c >= PCH:
            m0 = r0 - TP
            P.dma("sp", _r3(cT_t[s], 128), s_cT[:, r0:r0 + 128].rearrange("(g n) s -> n g s", n=128), cT_b[s], writes=[cT_b[s]])
            P.dma("sp", sz_t[s], s_sz[m0:m0 + 128, :], sz_b[s], writes=[sz_b[s]])

    onesf = AR.alloc(128)
    onesf_b = Buf("onesf")
    P.op("dve", lambda E: E.memset(onesf, 1.0), writes=[onesf_b])
    caus01 = AR.alloc(128, BF16)
    caus01_b = Buf("caus01")
    P.op("dve", lambda E: E.tensor_copy(out=caus01, in_=tri), reads=[tri_b], writes=[caus01_b])

    ssd_load(0)
    ri = 0
    for c in range(NCH):
        s = c % 2
        main = c >= PCH
        if c + 1 < NCH:
            ssd_load(c + 1)
        XS, BTM, BT, CT, DT, SZ = xs_t[s], btm_t[s], _r3(bT_t[s], 128), _r3(cT_t[s], 128), dt_t[s], sz_t[s]
        xsb, btmb, bTb, cTb, dtb_, szb = xs_b[s], btm_b[s], bT_b[s], cT_b[s], dt_b[s], sz_b[s]
        P.op("dve", lambda E, DT=DT: E.tensor_tensor(out=a_tm, in0=DT, in1=Aneg, op=ALU.mult), reads=[dtb_, Aneg_b], writes=[a_b])
        P.op("pe", lambda E: E.matmul(PS[6][:, 0:64], lhsT=tri, rhs=a_tm, start=True, stop=True), reads=[tri_b, a_b], writes=[(PSB[6], 0)])
        P.op("pe", lambda E: E.matmul(PS[6][:, 64:128], lhsT=onesf, rhs=a_tm, start=True, stop=True), reads=[onesf_b, a_b], writes=[(PSB[6], 1)])
        P.op("act", lambda E: E.copy(out=acs, in_=PS[6][:, 0:64]), reads=[(PSB[6], 0)], writes=[acs_b])
        P.op("act", lambda E: E.activation(out=E_tm, in_=PS[6][:, 0:64], func=AF.Exp), reads=[(PSB[6], 0)], writes=[E_b])
        P.op("act", lambda E: E.activation(out=Etot, in_=PS[6][:, 64:128], func=AF.Exp), reads=[(PSB[6], 1)], writes=[Etot_b])
        P.op("dve", lambda E: E.tensor_tensor(out=wl, in0=PS[6][:, 64:128], in1=acs, op=ALU.subtract), reads=[(PSB[6], 1), acs_b], writes=[wl_b])
        P.op("act", lambda E: E.activation(out=w_tm, in_=wl, func=AF.Exp), reads=[wl_b], writes=[w_b])
        P.op("dve", lambda E, DT=DT: E.tensor_tensor(out=dtw, in0=DT, in1=w_tm, op=ALU.mult), reads=[dtb_, w_b], writes=[dtw_b])
        if c < PCH:
            P.op("dve", lambda E: E.tensor_scalar(out=dtw, in0=dtw, scalar1=flag[:, 0:1], scalar2=None, op0=ALU.mult),
                 reads=[dtw_b, flag_b], writes=[dtw_b])
        x3 = lambda a: _r3(a, SP_)
        P.op("dve", lambda E, XS=XS: E.tensor_tensor(out=x3(xdtw), in0=x3(XS), in1=dtw.unsqueeze(2).to_broadcast([128, SH, SP_]), op=ALU.mult),
             reads=[xsb, dtw_b], writes=[xdtw_b])
        if main:
            P.op("dve", lambda E, XS=XS, DT=DT: E.tensor_tensor(out=x3(xdt), in0=x3(XS), in1=DT.unsqueeze(2).to_broadcast([128, SH, SP_]), op=ALU.mult),
                 reads=[xsb, dtb_], writes=[xdt_b])
        def front(g):
            nonlocal ri
            gs = slice(g * 512, (g + 1) * 512)
            if True:
                ci = g % 2
                P.op("pe", lambda E, g=g, BT=BT, CT=CT: E.matmul(PS[7][:, 0:128], lhsT=BT[:, g, :], rhs=CT[:, g, :], start=True, stop=True),
                     reads=[bTb, cTb], writes=[PSB[7]])
                P.op("dve", lambda E, ci=ci: E.tensor_tensor(out=cbm[ci], in0=PS[7][:, 0:128], in1=caus01, op=ALU.mult),
                     reads=[PSB[7], caus01_b], writes=[cbm_b[ci]])
                ypi = 4 + g % 2
                for qd in range(2):
                    h0 = g * 8 + qd * 4
                    rj = ri % 2
                    ri += 1
                    P.op("dve", lambda E, rj=rj, h0=h0: E.tensor_tensor(
                        out=_r3(rall[rj], 128), in0=tri.unsqueeze(1).to_broadcast([128, 4, 128]),
                        in1=a_tm[:, h0:h0 + 4].unsqueeze(2).to_broadcast([128, 4, 128]), op=ALU.mult),
                         reads=[tri_b, a_b], writes=[rall_b[rj]])
                    dpi = rj
                    P.op("pe", lambda E, dpi=dpi, rj=rj: E.matmul(PS[dpi][:, :], lhsT=ustr, rhs=rall[rj], start=True, stop=True),
                         reads=[ustr_b, rall_b[rj]], writes=[PSB[dpi]])
                    P.op("act", lambda E, dpi=dpi, rj=rj: E.activation(out=decT[rj], in_=PS[dpi][:, :], func=AF.Exp),
                         reads=[PSB[dpi]], writes=[decT_b[rj]])
                    P.op("dve", lambda E, rj=rj, ci=ci: E.tensor_tensor(
                        out=_r3(MT[rj], 128), in0=_r3(decT[rj], 128), in1=cbm[ci].unsqueeze(1).to_broadcast([128, 4, 128]), op=ALU.mult),
                         reads=[decT_b[rj], cbm_b[ci]], writes=[MT_b[rj]])
                    for hh in range(4):
                        h = h0 + hh
                        col = (qd * 4 + hh) * 64
                        P.op("pe", lambda E, ypi=ypi, rj=rj, hh=hh, h=h, col=col: E.matmul(
                            PS[ypi][:, col:col + 64], lhsT=MT[rj][:, hh * 128:(hh + 1) * 128], rhs=xdt[:, h * 64:(h + 1) * 64],
                            start=True, stop=True),
                             reads=[MT_b[rj], xdt_b], writes=[(PSB[ypi], col)])
                opi = 2 + g % 2
                P.op("pe", lambda E, opi=opi, g=g, gs=gs, CT=CT: E.matmul(PS[opi][:, :], lhsT=CT[:, g, :], rhs=stB[:, gs], start=True, stop=True),
                     reads=[cTb, (stB_b, g)], writes=[PSB[opi]])

        def back(g):
            gs = slice(g * 512, (g + 1) * 512)
            ypi = 4 + g % 2
            opi = 2 + g % 2
            if main:
                yj = g % 2
                hs = slice(g * 8, g * 8 + 8)
                P.op("dve", lambda E, opi=opi, yj=yj, hs=hs: E.tensor_tensor(
                    out=x3(yv[yj]), in0=x3(PS[opi][:, :]), in1=E_tm[:, hs].unsqueeze(2).to_broadcast([128, 8, SP_]), op=ALU.mult),
                     reads=[PSB[opi], E_b], writes=[yv_b[yj]])
                P.op("dve", lambda E, ypi=ypi, yj=yj: E.tensor_tensor(out=yv[yj], in0=yv[yj], in1=PS[ypi][:, :], op=ALU.add),
                     reads=[yv_b[yj], PSB[ypi]], writes=[yv_b[yj]])
                P.op("dve", lambda E, yj=yj, gs=gs, XS=XS: E.tensor_tensor(out=y2[yj], in0=XS[:, gs], in1=dsk[:, gs], op=ALU.mult),
                     reads=[xsb, dsk_b], writes=[y2_b[yj]])
                P.op("dve", lambda E, yj=yj: E.tensor_tensor(out=yv[yj], in0=yv[yj], in1=y2[yj], op=ALU.add),
                     reads=[yv_b[yj], y2_b[yj]], writes=[yv_b[yj]])
                P.op("dve", lambda E, yj=yj, gs=gs, SZ=SZ: E.tensor_tensor(out=yv[yj], in0=yv[yj], in1=SZ[:, gs], op=ALU.mult),
                     reads=[yv_b[yj], szb], writes=[yv_b[yj]])
                P.op("act", lambda E, yj=yj: E.activation(out=ysq, in_=yv[yj], func=AF.Square, accum_out=ssn[:, 0:1]),
                     reads=[yv_b[yj]], writes=[ysq_b, (ssn_b, 0)])
                P.op("dve", lambda E: E.tensor_scalar(out=ssn[:, 1:2], in0=ssn[:, 0:1], scalar1=1.0 / 512, scalar2=EPS, op0=ALU.mult, op1=ALU.add),
                     reads=[(ssn_b, 0)], writes=[(ssn_b, 1)])
                P.op("act", lambda E: E.sqrt(out=ssn[:, 3:4], in_=ssn[:, 1:2]), reads=[(ssn_b, 1)], writes=[(ssn_b, 3)])
                P.op("dve", lambda E: E.reciprocal(out=ssn[:, 2:3], in_=ssn[:, 3:4]), reads=[(ssn_b, 3)], writes=[(ssn_b, 2)])
                P.op("dve", lambda E, yj=yj, gs=gs: E.scalar_tensor_tensor(
                    out=ynb[yj], in0=yv[yj], scalar=ssn[:, 2:3], in1=snw[:, gs], op0=ALU.mult, op1=ALU.mult),
                     reads=[yv_b[yj], (ssn_b, 2), snw_b], writes=[ynb_b[yj]])
                psb = PS[6].bitcast(BF16)
                for j in range(4):
                    P.op("pe", lambda E, j=j, yj=yj, psb=psb: E.transpose(psb[:, 512 + j * 128:512 + (j + 1) * 128], ynb[yj][:, j * 128:(j + 1) * 128], ident),
                         reads=[ynb_b[yj], ident_b], writes=[(PSB[6], 10 + j)])
                P.op("act", lambda E, yj=yj, psb=psb: E.copy(out=ynst[yj], in_=psb[:, 512:1024]),
                     reads=[(PSB[6], 10), (PSB[6], 11), (PSB[6], 12), (PSB[6], 13)], writes=[ynst_b[yj]])
                m0 = c * 128 - TP
                P.dma("sp", s_ynT[g * 512:(g + 1) * 512, m0:m0 + 128].rearrange("(j p) t -> p j t", p=128),
                      _r3(ynst[yj], 128), ynst_b[yj], reads=[ynst_b[yj]])
            if c + 1 < NCH:
                spi = 2 + g % 2
                P.op("pe", lambda E, spi=spi, g=g, gs=gs, BTM=BTM: E.matmul(PS[spi][:, :], lhsT=BTM[:, g * 128:(g + 1) * 128], rhs=xdtw[:, gs], start=True, stop=True),
                     reads=[btmb, xdtw_b], writes=[PSB[spi]])
                hs = slice(g * 8, g * 8 + 8)
                P.op("dve", lambda E, gs=gs, hs=hs: E.tensor_tensor(
                    out=x3(stS[:, gs]), in0=x3(stS[:, gs]), in1=Etot[:, hs].unsqueeze(2).to_broadcast([128, 8, SP_]), op=ALU.mult),
                     reads=[(stS_b, g), Etot_b], writes=[(stS_b, g)])
                P.op("dve", lambda E, gs=gs, spi=spi: E.tensor_tensor(out=stS[:, gs], in0=stS[:, gs], in1=PS[spi][:, :], op=ALU.add),
                     reads=[(stS_b, g), PSB[spi]], writes=[(stS_b, g)])
                P.op("act", lambda E, gs=gs: E.copy(out=stB[:, gs], in_=stS[:, gs]), reads=[(stS_b, g)], writes=[(stB_b, g)])


        if main:
            front(0)
            for g in range(1, SG):
                front(g)
                back(g - 1)
            back(SG - 1)
        else:
            for g in range(SG):
                back(g)

    P.barrier()
    if stop_after == 3:
        return finalize()
    AR.reset()
    for b in PSB:
        b.st = {}

    NTC3 = TM // 512
    NTT = TM // 128
    ogc = _r3(AR.alloc(16 * 512, BF16), 512)
    ogc_b = Buf("ogc")
    ync = _r3(AR.alloc(32 * 512, BF16), 512)
    ync_b = Buf("ync")
    mrg = _r3(AR.alloc(32 * 512, BF16), 512)
    mrg_b = Buf("mrg")
    w3 = [_r3(AR.alloc(KC * 512, BF16), 512) for _ in range(2)]
    w3_b = [[Buf("w3%d_%d" % (s_, q_)) for q_ in range(4)] for s_ in range(2)]
    sga = [AR.alloc(512, BF16) for _ in range(2)]
    sga_b = [Buf("sga0"), Buf("sga1")]
    sgs = [AR.alloc(512, BF16) for _ in range(2)]
    sgs_b = [Buf("sgs0"), Buf("sgs1")]
    m1 = [AR.alloc(512) for _ in range(2)]
    m1_b = [Buf("m10"), Buf("m11")]
    m2 = [AR.alloc(512) for _ in range(2)]
    m2_b = [Buf("m20"), Buf("m21")]
    xsl = [AR.alloc(512) for _ in range(2)]
    xsl_b = [Buf("xsl0"), Buf("xsl1")]
    hst = [AR.alloc(512) for _ in range(2)]
    hst_b = [Buf("hst0"), Buf("hst1")]
    hsq = AR.alloc(512, BF16)
    hsq_b = Buf("hsq")
    w_ao_r = w_ao.rearrange("(kc p) c -> p kc c", p=128)
    w_so_r = w_so.rearrange("(kc p) c -> p kc c", p=128)
    w_o_r = w_o.rearrange("(kc p) c -> p kc c", p=128)
    wq = []
    for g in range(8):
        wq.append(("A", w_ao_r, g, 16))
        wq.append(("S", w_so_r, g, 32))
    for g in range(8):
        wq.append(("O", w_o_r, g, 32))
    wcnt = [0]

    def load_w3(slot, src_r, g, nkc):
        for q in range(0, nkc, 8):
            P.dma("pool", w3[slot][:, q:q + 8, :], src_r[:, q:q + 8, g * 512:(g + 1) * 512], w3_b[slot][q // 8], writes=[w3_b[slot][q // 8]])

    all_w = [(tc3, e) for tc3 in range(NTC3) for e in wq]
    load_w3(0, all_w[0][1][1], all_w[0][1][2], all_w[0][1][3])
    wi = 0
    ei = 0
    for tc3 in range(NTC3):
        c0 = tc3 * 512
        P.dma("sp", ogc, s_ogT[:, c0:c0 + 512].rearrange("(kc p) t -> p kc t", p=128), ogc_b, writes=[ogc_b])
        P.dma("sp", ync, s_ynT[:, c0:c0 + 512].rearrange("(kc p) t -> p kc t", p=128), ync_b, writes=[ync_b])
        for (kind, src_r, g, nkc) in wq:
            slot = wi % 2
            wi += 1
            if wi < len(all_w):
                nx = all_w[wi][1]
                load_w3(wi % 2, nx[1], nx[2], nx[3])
            W = w3[slot]
            Wb = w3_b[slot]
            if kind == "A":
                for dt_ in range(4):
                    for kc in range(16):
                        P.op("pe", lambda E, dt_=dt_, kc=kc, W=W: E.matmul(PS[dt_][:, :], lhsT=W[:, kc, dt_ * 128:(dt_ + 1) * 128], rhs=ogc[:, kc, :],
                                                                            start=(kc == 0), stop=(kc == 15)),
                             reads=[Wb[kc // 8], ogc_b], writes=[PSB[dt_]])
            elif kind == "S":
                for dt_ in range(4):
                    til = g * 4 + dt_
                    e2 = ei % 2
                    ei += 1
                    P.dma("sp", sga[e2], s_sgm[til * 128:(til + 1) * 128, c0:c0 + 512], sga_b[e2], writes=[sga_b[e2]])
                    P.dma("sp", sgs[e2], s_sgm[D + til * 128:D + (til + 1) * 128, c0:c0 + 512], sgs_b[e2], writes=[sgs_b[e2]])
                    for kc in range(32):
                        P.op("pe", lambda E, dt_=dt_, kc=kc, W=W: E.matmul(PS[4 + dt_][:, :], lhsT=W[:, kc, dt_ * 128:(dt_ + 1) * 128], rhs=ync[:, kc, :],
                                                                            start=(kc == 0), stop=(kc == 31)),
                             reads=[Wb[kc // 8], ync_b], writes=[PSB[4 + dt_]])
                    P.op("dve", lambda E, dt_=dt_, e2=e2: E.tensor_tensor(out=m1[e2], in0=PS[dt_][:, :], in1=sga[e2], op=ALU.mult),
                         reads=[PSB[dt_], sga_b[e2]], writes=[m1_b[e2]])
                    P.op("dve", lambda E, dt_=dt_, e2=e2: E.tensor_tensor(out=m2[e2], in0=PS[4 + dt_][:, :], in1=sgs[e2], op=ALU.mult),
                         reads=[PSB[4 + dt_], sgs_b[e2]], writes=[m2_b[e2]])
                    P.op("dve", lambda E, til=til, e2=e2: E.tensor_tensor(out=mrg[:, til, :], in0=m1[e2], in1=m2[e2], op=ALU.add),
                         reads=[m1_b[e2], m2_b[e2]], writes=[(mrg_b, til)])
            else:
                for tt in range(4):
                    ttg = tc3 * 4 + tt
                    e2 = ei % 2
                    ei += 1
                    pi = (g * 4 + tt) % 8
                    P.dma("sp", xsl[e2], x_all[TP + ttg * 128:TP + (ttg + 1) * 128, g * 512:(g + 1) * 512], xsl_b[e2], writes=[xsl_b[e2]])
                    for kc in range(32):
                        P.op("pe", lambda E, pi=pi, tt=tt, kc=kc, W=W: E.matmul(PS[pi][:, :], lhsT=mrg[:, kc, tt * 128:(tt + 1) * 128], rhs=W[:, kc, :],
                                                                                start=(kc == 0), stop=(kc == 31)),
                             reads=[Wb[kc // 8], mrg_b], writes=[PSB[pi]])
                    P.op("dve", lambda E, pi=pi, e2=e2: E.tensor_tensor(out=hst[e2], in0=PS[pi][:, :], in1=xsl[e2], op=ALU.add),
                         reads=[PSB[pi], xsl_b[e2]], writes=[hst_b[e2]])
                    P.op("act", lambda E, e2=e2, ttg=ttg, g=g: E.activation(out=hsq, in_=hst[e2], func=AF.Square, accum_out=ssf[:, ttg * 8 + g:ttg * 8 + g + 1]),
                         reads=[hst_b[e2]], writes=[hsq_b, (ssf_b, ttg * 8 + g)])
                    P.dma("sp", out[ttg * 128:(ttg + 1) * 128, g * 512:(g + 1) * 512], hst[e2], hst_b[e2], reads=[hst_b[e2]])

    P.barrier()
    AR.reset()
    fnw = AR.alloc(D)
    fnw_b = Buf("fnw")
    P.dma("sp", fnw, c_fnw, fnw_b, writes=[fnw_b])
    hrow = [AR.alloc(D) for _ in range(2)]
    hrow_b = [Buf("hrow0"), Buf("hrow1")]
    rs = AR.alloc(NTT * 2)
    rs_b = Buf("rs")
    ssf_b.st = {}
    for ttg in range(NTT):
        e2 = ttg % 2
        P.dma("sp", hrow[e2], out[ttg * 128:(ttg + 1) * 128, :], hrow_b[e2], writes=[hrow_b[e2]])
        P.op("dve", lambda E, ttg=ttg: E.tensor_reduce(out=rs[:, 2 * ttg:2 * ttg + 1], in_=ssf[:, ttg * 8:(ttg + 1) * 8], axis=AX.X, op=ALU.add),
             reads=[ssf_b], writes=[(rs_b, ttg)])
        P.op("dve", lambda E, ttg=ttg: E.tensor_scalar(out=rs[:, 2 * ttg + 1:2 * ttg + 2], in0=rs[:, 2 * ttg:2 * ttg + 1], scalar1=1.0 / D, scalar2=EPS,
                                                       op0=ALU.mult, op1=ALU.add),
             reads=[(rs_b, ttg)], writes=[(rs_b, ttg)])
        P.op("act", lambda E, ttg=ttg: E.sqrt(out=rs[:, 2 * ttg:2 * ttg + 1], in_=rs[:, 2 * ttg + 1:2 * ttg + 2]),
             reads=[(rs_b, ttg)], writes=[(rs_b, ttg)])
        P.op("dve", lambda E, ttg=ttg: E.reciprocal(out=rs[:, 2 * ttg:2 * ttg + 1], in_=rs[:, 2 * ttg:2 * ttg + 1]),
             reads=[(rs_b, ttg)], writes=[(rs_b, ttg)])
        P.op("dve", lambda E, ttg=ttg, e2=e2: E.scalar_tensor_tensor(out=hrow[e2], in0=hrow[e2], scalar=rs[:, 2 * ttg:2 * ttg + 1], in1=fnw,
                                                                     op0=ALU.mult, op1=ALU.mult),
             reads=[hrow_b[e2], (rs_b, ttg), fnw_b], writes=[hrow_b[e2]])
        P.dma("sp", out[ttg * 128:(ttg + 1) * 128, :], hrow[e2], hrow_b[e2], reads=[hrow_b[e2]])

    return finalize()


def make_consts(TP, TM, pos_main0, prefix_valid):
    T = TP + TM
    NBLK = T // BLK
    NQT = TM // 128
    c = {}
    c["c_ident"] = np.eye(128, dtype=np.float32)
    r = np.zeros((32, 128), np.float32)
    for m in range(32):
        r[(m + 16) % 32, m] = 1.0
    c["c_rot"] = r
    k = np.arange(128)
    c["c_tri"] = (k[:, None] <= k[None, :]).astype(np.float32)
    c["c_ustr"] = (k[:, None] > k[None, :]).astype(np.float32)
    q = np.arange(512)
    caus = np.zeros((128, 4, 512), np.float32)
    for j in range(4):
        caus[:, j, :] = ((j * 128 + k)[:, None] <= q[None, :])
    c["c_caus"] = caus.reshape(128, 2048)
    oh = np.zeros((NBLK, NBLK, 128), np.float32)
    for b in range(NBLK):
        oh[b, b, :] = 1.0
    c["c_onehot"] = oh.reshape(NBLK, NBLK * 128)
    gb = np.full((NQT, NBLK), -BIG, np.float32)
    ob = np.full((NQT, NBLK), -3 * BIG, np.float32)
    pblk = TP // BLK
    for qt in range(NQT):
        own = pblk + (qt * 128) // BLK
        for b in range(own):
            if b >= pblk or prefix_valid:
                gb[qt, b] = 0.0
        ob[qt, own] = 0.0
    c["c_gb"] = np.ascontiguousarray(np.broadcast_to(gb.reshape(1, -1), (128, NQT * NBLK)))
    c["c_ob"] = np.ascontiguousarray(np.broadcast_to(ob.reshape(1, -1), (128, NQT * NBLK)))
    c["c_flag"] = np.full((128, 1), 1.0 if prefix_valid else 0.0, np.float32)
    half = 16
    inv_freq = (ROPE_THETA ** (-np.arange(half, dtype=np.float32) * 2.0 / 32)).astype(np.float32)
    pos = np.arange(T, dtype=np.float32) - TP + pos_main0
    ang = (pos[None, :] * inv_freq[:, None]).astype(np.float32)
    cos = np.cos(ang).astype(np.float32)
    sin = np.sin(ang).astype(np.float32)
    c["c_cos"] = np.ascontiguousarray(np.concatenate([cos, cos, np.ones((96, T), np.float32)], 0))
    c["c_sin"] = np.ascontiguousarray(np.concatenate([-sin, sin, np.zeros((96, T), np.float32)], 0))
    return c


def make_param_consts(norm_w, conv_w, conv_b, dt_bias, a_log, d_skip, ssd_norm_w, gate_bias, final_norm_w):
    c = {}
    c["c_normw"] = np.ascontiguousarray(norm_w.reshape(KC, 128).T)
    cw = conv_w.reshape(4, 48, 128)
    c["c_convw"] = np.ascontiguousarray(cw.transpose(2, 1, 0).reshape(128, 48 * 4))
    c["c_convb"] = np.ascontiguousarray(conv_b.reshape(48, 128).T)
    c["c_gbias"] = np.ascontiguousarray(gate_bias.reshape(64, 128).T)
    bc = lambda v: np.ascontiguousarray(np.broadcast_to(v.reshape(1, -1), (128, v.size)))
    c["c_dtb"] = bc(dt_bias)
    c["c_alog"] = bc(a_log)
    c["c_dsk"] = bc(np.repeat(d_skip, SP_))
    c["c_snw"] = bc(ssd_norm_w)
    c["c_fnw"] = bc(final_norm_w)
    return c


_CACHE = {}


def kernel(x, norm_w, w_in, conv_w, conv_b, dt_bias, a_log, d_skip, ssd_norm_w,
           w_attn_out, w_ssd_out, gate_bias, w_out, final_norm_w):
    x = np.asarray(x, np.float32)
    B, S, _ = x.shape
    TP = TM = S // 2
    f = lambda a: np.ascontiguousarray(np.asarray(a, np.float32))
    pc = make_param_consts(f(norm_w)[0], f(conv_w)[0], f(conv_b)[0], f(dt_bias)[0], f(a_log)[0], f(d_skip)[0],
                           f(ssd_norm_w)[0], f(gate_bias)[0], f(final_norm_w))
    shared = {"w_in": f(w_in)[0], "w_ao": f(w_attn_out)[0], "w_so": f(w_ssd_out)[0], "w_o": f(w_out)[0]}
    shared.update(pc)
    in_maps = []
    for core in range(8):
        b, j = core // 2, core % 2
        if j == 0:
            xa = np.concatenate([np.zeros((TP, D), np.float32), x[b, 0:TM]], 0)
        else:
            xa = x[b]
        m = {"x_all": np.ascontiguousarray(xa)}
        m.update(shared)
        m.update(make_consts(TP, TM, j * TM, j == 1))
        in_maps.append(m)
    key = (TP, TM)
    if key not in _CACHE:
        _CACHE[key] = build_program(TP, TM)
    nc = _CACHE[key]
    res = run_bass_kernel_spmd(nc, in_maps, core_ids=list(range(8)))
    outp = np.empty((B, S, D), np.float32)
    for core in range(8):
        b, j = core // 2, core % 2
        outp[b, j * TM:(j + 1) * TM] = res.results[core]["out"]
    return outp
```

```python
import numpy as np
import concourse.bass as bass
import concourse.mybir as mybir
from concourse.bass_utils import run_bass_kernel_spmd

F32 = mybir.dt.float32
BF16 = mybir.dt.bfloat16
AF = mybir.ActivationFunctionType
ALU = mybir.AluOpType
AX = mybir.AxisListType

D = 4096
KC = D // 128
AW = 2048
NH = 16
HD = 128
SSD_IN = 4096
SH = 64
SP_ = 64
SG = 8
SN = 128
CONVD = SSD_IN + 2 * SG * SN
IN_COLS = 4 * AW + SSD_IN + CONVD + SH + 2 * D
OFF_Q, OFF_K, OFF_V, OFF_G = 0, AW, 2 * AW, 3 * AW
OFF_Z = 4 * AW
OFF_X = OFF_Z + SSD_IN
OFF_B = OFF_X + SSD_IN
OFF_C = OFF_B + SG * SN
OFF_DT = OFF_C + SG * SN
OFF_GA = OFF_DT + SH
OFF_GS = OFF_GA + D
EPS = 1e-6
BIG = 30000.0
BLK = 256
ROPE_THETA = 500000.0


class Buf:
    __slots__ = ("name", "st", "sem", "cnt", "psum")

    def __init__(self, name, psum=False):
        self.name = name
        self.psum = psum
        self.st = {}
        self.sem = None
        self.cnt = 0


class Prog:
    COMPUTE = ("pe", "act", "dve")
    QUEUES = ("sp", "pool")

    def __init__(self, nc):
        self.nc = nc
        self.ops = {e: [] for e in self.COMPUTE + self.QUEUES}
        self.sems = {}
        self.pending = {e: [] for e in self.COMPUTE + self.QUEUES}
        self.dma_bufs = []
        self.nsem = 0

    def new_sem(self, name):
        s = self.nc.semaphore(name)
        h = s.__enter__()
        self.nsem += 1
        return h

    def setup(self):
        for e in self.COMPUTE:
            self.sems[e] = self.new_sem("sem_" + e)

    def _conf(self, buf, key):
        if key is None:
            return list(buf.st.values())
        r = []
        if key in buf.st:
            r.append(buf.st[key])
        if None in buf.st:
            r.append(buf.st[None])
        return r

    def _deps(self, reads, writes, eng=None):
        deps = []
        for (b, k) in reads:
            for ent in self._conf(b, k):
                if ent[0] is not None:
                    deps.append(ent[0])
                if b.psum:
                    for t in ent[1].values():
                        if not (t[0] == "eng" and t[1] == eng):
                            deps.append(t)
        for (b, k) in writes:
            for ent in self._conf(b, k):
                if ent[0] is not None:
                    deps.append(ent[0])
                deps.extend(ent[1].values())
        return deps

    @staticmethod
    def _addreader(d, tok):
        key = (tok[0], tok[1] if tok[0] == "eng" else id(tok[1]))
        if key not in d or d[key][2] < tok[2]:
            d[key] = tok

    def _update(self, tok, reads, writes):
        for (b, k) in reads:
            if k not in b.st:
                b.st[k] = [None, {}]
            self._addreader(b.st[k][1], tok)
        for (b, k) in writes:
            if k is None:
                b.st = {None: [tok, {}]}
            else:
                b.st[k] = [tok, {}]
                if None in b.st:
                    pass

    def _mkwaits(self, eng, deps):
        w = {}
        for t in deps:
            if t[0] == "eng":
                if t[1] == eng and eng == "pe":
                    continue
                key = ("eng", t[1])
                if key not in w or w[key] < t[2]:
                    w[key] = t[2]
                self.ops[t[1]][t[2]]["flag"] = True
            else:
                key = ("dma", id(t[1]), t[1])
                if key not in w or w[key] < t[2]:
                    w[key] = t[2]
        return w

    @staticmethod
    def _norm(r):
        b, k = r if isinstance(r, tuple) else (r, None)
        if b.psum:
            k = None
        return (b, k)

    def op(self, eng, fn, reads=(), writes=()):
        reads = [self._norm(r) for r in reads]
        writes = [self._norm(r) for r in writes]
        deps = self._deps(reads, writes, eng) + self.pending[eng]
        self.pending[eng] = []
        idx = len(self.ops[eng])
        self.ops[eng].append({"fn": fn, "waits": self._mkwaits(eng, deps), "flag": False, "dma": None})
        self._update(("eng", eng, idx), reads, writes)

    def dma(self, q, out_ap, in_ap, slot, reads=(), writes=()):
        reads = [self._norm(r) for r in reads]
        writes = [self._norm(r) for r in writes]
        deps = self._deps(reads, writes, q) + self.pending[q]
        self.pending[q] = []
        if slot.sem is None:
            slot.sem = self.new_sem("dsem_" + slot.name)
            self.dma_bufs.append(slot)
        slot.cnt += 16
        tok = ("dma", slot.sem, slot.cnt)

        def fn(e, out_ap=out_ap, in_ap=in_ap):
            return e.dma_start(out=out_ap, in_=in_ap)
        self.ops[q].append({"fn": fn, "waits": self._mkwaits(q, deps), "flag": False, "dma": (slot.sem, 16)})
        self._update(tok, reads, writes)

    def barrier(self):
        toks = []
        for e in self.COMPUTE:
            if self.ops[e]:
                toks.append(("eng", e, len(self.ops[e]) - 1))
        for b in self.dma_bufs:
            toks.append(("dma", b.sem, b.cnt))
        for e in self.COMPUTE + self.QUEUES:
            self.pending[e] = self.pending[e] + toks

    def final_wait_tokens(self):
        return [("dma", b.sem, b.cnt) for b in self.dma_bufs]

    def emit(self):
        nc = self.nc
        cnts = {}
        for e in self.COMPUTE:
            c = 0
            arr = []
            for o in self.ops[e]:
                if o["flag"]:
                    c += 1
                arr.append(c)
            cnts[e] = arr
            assert c < 60000, (e, c)
        for b in self.dma_bufs:
            assert b.cnt < 60000, (b.name, b.cnt)
        engobj = {"pe": "tensor", "act": "scalar", "dve": "vector", "sp": "sync", "pool": "gpsimd"}
        with nc.Block() as block:
            for e in self.COMPUTE + self.QUEUES:
                ops = self.ops[e]
                sems = self.sems

                def body(E, ops=ops, e=e):
                    waited = {}
                    for o in ops:
                        for key, val in o["waits"].items():
                            if key[0] == "eng":
                                sem = sems[key[1]]
                                v = cnts[key[1]][val]
                                wk = key
                            else:
                                sem = key[2]
                                v = val
                                wk = key[:2]
                            if waited.get(wk, -1) >= v:
                                continue
                            waited[wk] = v
                            E.wait_ge(sem, v)
                        ins = o["fn"](E)
                        if o["dma"] is not None:
                            ins.then_inc(o["dma"][0], o["dma"][1])
                        elif o["flag"]:
                            ins.then_inc(sems[e], 1)
                getattr(block, engobj[e])(body)


class Arena:
    def __init__(self, ap, nwords):
        self.ap = ap
        self.n = nwords
        self.off = 0
        self.mark_ = 0

    def mark(self):
        self.mark_ = self.off

    def reset(self):
        self.off = self.mark_

    def alloc(self, nelem, dtype=F32, parts=128):
        words = nelem if dtype == F32 else (nelem + 1) // 2
        words = (words + 7) // 8 * 8
        assert self.off + words <= self.n, ("arena overflow", self.off, words, self.n)
        a = self.ap[0:parts, self.off:self.off + words]
        self.off += words
        if dtype != F32:
            a = a.bitcast(dtype)
        return a[:, 0:nelem]


def _r3(ap, b):
    return ap.rearrange("p (a b) -> p a b", b=b)


def build_program(TP, TM, debug=False, stop_after=None):
    T = TP + TM
    NBLK = T // BLK
    nc = bass.Bass("TRN2", target_bir_lowering=False)
    P = Prog(nc)

    def din(name, shape, dt=F32):
        return nc.dram_tensor(name, list(shape), dt, kind="ExternalInput").ap()

    skind = "ExternalOutput" if debug else "Internal"

    def dscr(name, shape, dt):
        return nc.dram_tensor(name, list(shape), dt, kind=skind).ap()

    x_all = din("x_all", [T, D])
    import os as _os
    _wc = int(_os.environ.get('KDBG_WCOLS', IN_COLS))
    _small = 'KDBG_WCOLS' in _os.environ
    w_in = din("w_in", [D, _wc])
    w_ao = din("w_ao", [128 if _small else AW, D])
    w_so = din("w_so", [128 if _small else SSD_IN, D])
    w_o = din("w_o", [128 if _small else D, D])
    c_normw = din("c_normw", [128, KC])
    c_convw = din("c_convw", [128, 48 * 4])
    c_convb = din("c_convb", [128, 48])
    c_gbias = din("c_gbias", [128, 64])
    c_dtb = din("c_dtb", [128, SH])
    c_alog = din("c_alog", [128, SH])
    c_dsk = din("c_dsk", [128, SSD_IN])
    c_snw = din("c_snw", [128, SSD_IN])
    c_fnw = din("c_fnw", [128, D])
    c_cos = din("c_cos", [128, T])
    c_sin = din("c_sin", [128, T])
    c_ident = din("c_ident", [128, 128])
    c_rot = din("c_rot", [32, 128])
    c_tri = din("c_tri", [128, 128])
    c_ustr = din("c_ustr", [128, 128])
    c_caus = din("c_caus", [128, 4 * 512])
    c_onehot = din("c_onehot", [NBLK, NBLK * 128])
    c_gb = din("c_gb", [128, (TM // 128) * NBLK])
    c_ob = din("c_ob", [128, (TM // 128) * NBLK])
    c_flag = din("c_flag", [128, 1])

    out = nc.dram_tensor("out", [TM, D], F32, kind="ExternalOutput").ap()

    s_qT = dscr("s_qT", [AW, TM], BF16)
    s_kT = dscr("s_kT", [AW, T], BF16)
    s_v = dscr("s_v", [T, AW], BF16)
    s_sg = dscr("s_sg", [TM, AW], BF16)
    s_sz = dscr("s_sz", [TM, SSD_IN], BF16)
    s_xs = dscr("s_xs", [T, SSD_IN], BF16)
    s_bT = dscr("s_bT", [SG * SN, T], BF16)
    s_btm = dscr("s_btm", [T, SG * SN], BF16)
    s_cT = dscr("s_cT", [SG * SN, T], BF16)
    s_dt = dscr("s_dt", [T, SH], F32)
    s_sgm = dscr("s_sgm", [2 * D, TM], BF16)
    s_ogT = dscr("s_ogT", [AW, TM], BF16)
    s_ynT = dscr("s_ynT", [SSD_IN, TM], BF16)

    AW_WORDS = 51 * 1024
    arena_g = nc.sbuf_tensor("arena", [128, AW_WORDS], F32)
    arena_t = arena_g.__enter__()
    AR = Arena(arena_t, AW_WORDS)
    psg = [nc.psum_tensor("ps%d" % i, [128, 512], F32) for i in range(8)]
    PS = [g.__enter__() for g in psg]
    PSB = [Buf("ps%d" % i, psum=True) for i in range(8)]
    P.setup()

    def cload(name, src, nelem, dt=F32, parts=128, cast=False):
        ap = AR.alloc(nelem, dt, parts)
        b = Buf(name)
        P.dma("pool" if cast else "sp", ap, src, b, writes=[b])
        return ap, b

    ident_f, identf_b = cload("identf", c_ident, 128)
    ident, ident_b = cload("ident", c_ident, 128, BF16, cast=True)
    rot, rot_b = cload("rot", c_rot, 128, BF16, parts=32, cast=True)
    tri, tri_b = cload("tri", c_tri, 128)
    ustr, ustr_b = cload("ustr", c_ustr, 128)
    flag, flag_b = cload("flag", c_flag, 1)
    ssf = AR.alloc((TM // 128) * 8)
    ssf_b = Buf("ssf")
    AR.mark()

    SCALE = HD ** -0.5

    def finalize():
        P.pending["sp"] = P.pending["sp"] + P.final_wait_tokens()
        P.ops["sp"].append({"fn": lambda E: E.nop(), "waits": P._mkwaits("sp", P.pending["sp"]), "flag": False, "dma": None})
        print("nsem", P.nsem, {e: len(P.ops[e]) for e in P.ops})
        P.emit()
        return nc


    TCH = 1024
    normw, normw_b = cload("normw", c_normw, KC)
    convw, convw_b = cload("convw", c_convw, 48 * 4)
    convb, convb_b = cload("convb", c_convb, 48)
    gbias, gbias_b = cload("gbias", c_gbias, 64)
    dtb, dtb_b = cload("dtb", c_dtb, SH)
    uT = _r3(AR.alloc(KC * TCH, BF16), TCH)
    uT_b = Buf("uT")
    wsl = [_r3(AR.alloc(KC * 512, BF16), 512) for _ in range(2)]
    wsl_b = [[Buf("w%d_%d" % (s_, q_)) for q_ in range(4)] for s_ in range(2)]
    xt = [AR.alloc(D) for _ in range(1)]
    xt_b = [Buf("xt0")]
    xn = AR.alloc(D, BF16)
    xn_b = Buf("xn")
    ssq = AR.alloc(8)
    ssq_b = Buf("ssq")
    halo = AR.alloc(48 * 3)
    halo_b = Buf("halo")
    cst = [AR.alloc(520) for _ in range(2)]
    cst_b = [Buf("cst0"), Buf("cst1")]
    acc = [AR.alloc(512) for _ in range(2)]
    acc_b = [Buf("acc0"), Buf("acc1")]
    NST = 8
    stg = [AR.alloc(512, BF16) for _ in range(NST)]
    stg_b = [Buf("stg%d" % i) for i in range(NST)]
    stt = [AR.alloc(512, BF16) for _ in range(2)]
    stt_b = [Buf("stt0"), Buf("stt1")]
    NRT = 4
    rt1s = [AR.alloc(512) for _ in range(NRT)]
    rt1s_b = [Buf("rt1_%d" % i) for i in range(NRT)]
    rt2 = AR.alloc(512)
    rt2_b = Buf("rt2")
    cosb = AR.alloc(TCH)
    sinb = AR.alloc(TCH)
    cos_b, sin_b = Buf("cos"), Buf("sin")
    dts = [AR.alloc(64 * 4) for _ in range(2)]
    dts_b = [Buf("dts0"), Buf("dts1")]

    P.op("dve", lambda E: E.memset(halo, 0.0), writes=[halo_b])

    state = {"ps": 0, "st": 0, "stt": 0, "cst": 0, "w": 0, "dts": 0, "rt": 0}
    deferred = []
    LAG = 2

    def defer(fn):
        deferred.append(fn)
        while len(deferred) > LAG:
            deferred.pop(0)()

    def drain(n=1):
        for _ in range(n):
            if deferred:
                deferred.pop(0)()


    def next_ps(lo=0, hi=5):
        i = lo + state["ps"] % (hi - lo)
        state["ps"] += 1
        return i

    def next_stg():
        i = state["st"] % NST
        state["st"] += 1
        return i

    def groups_for(is_prefix):
        g = []
        for i in range(4):
            g.append(("K", OFF_K + 512 * i, 512, i))
        for i in range(4):
            g.append(("V", OFF_V + 512 * i, 512, i))
        for i in range(8):
            g.append(("X", OFF_X + 512 * i, 512, i))
        for i in range(2):
            g.append(("B", OFF_B + 512 * i, 512, i))
        g.append(("DT", OFF_DT, 64, 0))
        for i in range(2):
            g.append(("C", OFF_C + 512 * i, 512, i))
        if not is_prefix:
            for i in range(4):
                g.append(("Q", OFF_Q + 512 * i, 512, i))
            for i in range(4):
                g.append(("G", OFF_G + 512 * i, 512, i))
            for i in range(8):
                g.append(("Z", OFF_Z + 512 * i, 512, i))
            for i in range(16):
                g.append(("GM", OFF_GA + 512 * i, 512, i))
        return g

    w_in_r = w_in.rearrange("(kc p) c -> p kc c", p=128)

    def load_w(slot, src_r, c0, ncols, nkc=KC):
        for q in range(0, nkc, 8):
            P.dma("pool", wsl[slot][:, q:q + 8, 0:ncols], src_r[:, q:q + 8, c0:c0 + ncols],
                  wsl_b[slot][q // 8], writes=[wsl_b[slot][q // 8]])

    def fm_store(src_ap, sbuf_b, dst, row0, col0, n=512):
        P.dma("sp", dst[row0:row0 + 128, col0:col0 + n], src_ap, sbuf_b, reads=[sbuf_b])

    def transposed_store(src_bf, src_b, dst, tok0, col0):
        pi = 6 + state["stt"] % 2
        si = state["stt"] % 2
        state["stt"] += 1
        psb = PS[pi].bitcast(BF16)
        for j in range(4):
            P.op("pe", lambda E, j=j, psb=psb: E.transpose(psb[:, j * 128:(j + 1) * 128],
                                                           src_bf[:, j * 128:(j + 1) * 128], ident),
                 reads=[src_b, ident_b], writes=[(PSB[pi], j)])
        P.op("act", lambda E, psb=psb, si=si: E.copy(out=stt[si], in_=psb[:, 0:512]),
             reads=[PSB[pi]], writes=[stt_b[si]])
        P.dma("sp", dst[tok0:tok0 + 512, col0:col0 + 128].rearrange("(j p) c -> p j c", p=128),
              _r3(stt[si], 128), stt_b[si], reads=[stt_b[si]])

    import os as _os
    n_chunks = int(_os.environ.get('KDBG_CHUNKS', T // TCH))
    for tc in range(n_chunks):
        t0 = tc * TCH
        is_prefix = t0 < TP
        tm0 = t0 - TP
        drain(len(deferred))
        P.dma("sp", cosb, c_cos[:, t0:t0 + TCH], cos_b, writes=[cos_b])
        P.dma("sp", sinb, c_sin[:, t0:t0 + TCH], sin_b, writes=[sin_b])
        for tt in range(TCH // 128):
            r0 = t0 + tt * 128
            P.dma("sp", xt[0], x_all[r0:r0 + 128, :], xt_b[0], writes=[xt_b[0]])
            P.op("act", lambda E: E.activation(out=xn, in_=xt[0], func=AF.Square, accum_out=ssq[:, 0:1]),
                 reads=[xt_b[0]], writes=[xn_b, (ssq_b, 0)])
            P.op("dve", lambda E: E.tensor_scalar(out=ssq[:, 1:2], in0=ssq[:, 0:1], scalar1=1.0 / D, scalar2=EPS,
                                                  op0=ALU.mult, op1=ALU.add),
                 reads=[(ssq_b, 0)], writes=[(ssq_b, 1)])
            P.op("act", lambda E: E.sqrt(out=ssq[:, 3:4], in_=ssq[:, 1:2]), reads=[(ssq_b, 1)], writes=[(ssq_b, 3)])
            P.op("dve", lambda E: E.reciprocal(out=ssq[:, 2:3], in_=ssq[:, 3:4]), reads=[(ssq_b, 3)], writes=[(ssq_b, 2)])
            P.op("act", lambda E: E.activation(out=xn, in_=xt[0], func=AF.Copy, scale=ssq[:, 2:3]),
                 reads=[xt_b[0], (ssq_b, 2)], writes=[xn_b])
            for j in range(8):
                pi = 6 + j % 2
                psb = PS[pi].bitcast(BF16)
                for q in range(4):
                    kc = 4 * j + q
                    P.op("pe", lambda E, psb=psb, q=q, kc=kc: E.transpose(psb[:, q * 128:(q + 1) * 128],
                                                                          xn[:, kc * 128:(kc + 1) * 128], ident),
                         reads=[xn_b, ident_b], writes=[(PSB[pi], q)])
                P.op("dve", lambda E, psb=psb, j=j, tt=tt: E.tensor_tensor(
                    out=uT[:, 4 * j:4 * j + 4, tt * 128:(tt + 1) * 128],
                    in0=_r3(psb[:, 0:512], 128),
                    in1=normw[:, 4 * j:4 * j + 4].unsqueeze(2).to_broadcast([128, 4, 128]), op=ALU.mult),
                     reads=[PSB[pi], normw_b], writes=[(uT_b, tt)])

        glist = groups_for(is_prefix)[:int(_os.environ.get('KDBG_GROUPS', 1000))]
        if not glist:
            continue
        load_w(state["w"] % 2, w_in_r, glist[0][1], glist[0][2])
        for gi, (kind, c0, ncols, gidx) in enumerate(glist):
            slot = state["w"] % 2
            state["w"] += 1
            if gi + 1 < len(glist):
                load_w(state["w"] % 2, w_in_r, glist[gi + 1][1], glist[gi + 1][2])
            W = wsl[slot]
            Wb = wsl_b[slot]
            if kind in ("K", "Q", "X", "B", "C", "GM"):
                for ct in range(4):
                    for th in range(TCH // 512):
                        pi = next_ps()
                        for kc in range(KC):
                            P.op("pe", lambda E, pi=pi, kc=kc, ct=ct, th=th, W=W: E.matmul(
                                PS[pi][:, :], lhsT=W[:, kc, ct * 128:(ct + 1) * 128],
                                rhs=uT[:, kc, th * 512:(th + 1) * 512], start=(kc == 0), stop=(kc == KC - 1)),
                                 reads=[Wb[kc // 8], uT_b], writes=[PSB[pi]])
                        tok0 = t0 + th * 512
                        if kind in ("K", "Q"):
                            si = next_stg()
                            P.op("act", lambda E, pi=pi, si=si: E.copy(out=stg[si], in_=PS[pi][:, :]),
                                 reads=[PSB[pi]], writes=[stg_b[si]])
                            ri_ = state["rt"] % NRT
                            state["rt"] += 1
                            rt1, rt1_b = rt1s[ri_], rt1s_b[ri_]
                            P.op("dve", lambda E, pi=pi, th=th, rt1=rt1: E.tensor_tensor(
                                out=rt1, in0=PS[pi][:, :], in1=cosb[:, th * 512:(th + 1) * 512], op=ALU.mult),
                                 reads=[PSB[pi], cos_b], writes=[rt1_b])

                            def part_b(si=si, th=th, rt1=rt1, rt1_b=rt1_b, kind=kind, gidx=gidx, ct=ct, tok0=tok0):
                                pr = 5
                                P.op("pe", lambda E: E.matmul(PS[pr][:, :], lhsT=rot, rhs=stg[si][0:32, :], start=True, stop=True),
                                     reads=[stg_b[si], rot_b], writes=[PSB[pr]])
                                P.op("dve", lambda E: E.tensor_tensor(
                                    out=rt2, in0=PS[pr][:, :], in1=sinb[:, th * 512:(th + 1) * 512], op=ALU.mult),
                                     reads=[PSB[pr], sin_b], writes=[rt2_b])
                                P.op("dve", lambda E: E.tensor_tensor(out=stg[si], in0=rt1, in1=rt2, op=ALU.add),
                                     reads=[rt1_b, rt2_b, stg_b[si]], writes=[stg_b[si]])
                                row0 = gidx * 512 + ct * 128
                                if kind == "K":
                                    fm_store(stg[si], stg_b[si], s_kT, row0, tok0)
                                else:
                                    fm_store(stg[si], stg_b[si], s_qT, row0, tok0 - TP)
                            defer(part_b)
                        elif kind == "GM":
                            si = next_stg()
                            til = gidx * 4 + ct
                            P.op("act", lambda E, pi=pi, si=si, til=til: E.activation(
                                out=stg[si], in_=PS[pi][:, :], func=AF.Sigmoid, bias=gbias[:, til:til + 1]),
                                 reads=[PSB[pi], gbias_b], writes=[stg_b[si]])
                            fm_store(stg[si], stg_b[si], s_sgm, til * 128, tok0 - TP)
                        else:
                            cti = {"X": 0, "B": 32, "C": 40}[kind] + gidx * 4 + ct
                            ci = state["cst"] % 2
                            state["cst"] += 1
                            P.op("dve", lambda E, ci=ci, cti=cti: E.tensor_copy(out=cst[ci][:, 0:3], in_=halo[:, cti * 3:cti * 3 + 3]),
                                 reads=[(halo_b, cti)], writes=[(cst_b[ci], "h")])
                            P.op("act", lambda E, ci=ci, pi=pi: E.copy(out=cst[ci][:, 3:515], in_=PS[pi][:, :]),
                                 reads=[PSB[pi]], writes=[(cst_b[ci], "m")])
                            P.op("dve", lambda E, ci=ci, cti=cti: E.tensor_copy(out=halo[:, cti * 3:cti * 3 + 3], in_=cst[ci][:, 512:515]),
                                 reads=[(cst_b[ci], "m")], writes=[(halo_b, cti)])
                            P.op("dve", lambda E, ci=ci, cti=cti: E.tensor_scalar(
                                out=acc[ci], in0=cst[ci][:, 0:512], scalar1=convw[:, cti * 4:cti * 4 + 1], scalar2=None, op0=ALU.mult),
                                 reads=[cst_b[ci], convw_b], writes=[acc_b[ci]])
                            for k in range(1, 4):
                                P.op("dve", lambda E, ci=ci, cti=cti, k=k: E.scalar_tensor_tensor(
                                    out=acc[ci], in0=cst[ci][:, k:k + 512], scalar=convw[:, cti * 4 + k:cti * 4 + k + 1],
                                    in1=acc[ci], op0=ALU.mult, op1=ALU.add),
                                     reads=[cst_b[ci], convw_b, acc_b[ci]], writes=[acc_b[ci]])
                            si = next_stg()
                            P.op("act", lambda E, ci=ci, si=si, cti=cti: E.activation(
                                out=stg[si], in_=acc[ci], func=AF.Silu, bias=convb[:, cti:cti + 1]),
                                 reads=[acc_b[ci], convb_b], writes=[stg_b[si]])
                            col0 = gidx * 512 + ct * 128

                            def part_b(si=si, kind=kind, col0=col0, tok0=tok0):
                                if kind == "X":
                                    transposed_store(stg[si], stg_b[si], s_xs, tok0, col0)
                                elif kind == "B":
                                    fm_store(stg[si], stg_b[si], s_bT, col0, tok0)
                                    transposed_store(stg[si], stg_b[si], s_btm, tok0, col0)
                                else:
                                    fm_store(stg[si], stg_b[si], s_cT, col0, tok0)
                            defer(part_b)
            else:
                for tt in range(TCH // 128):
                    pi = next_ps()
                    for kc in range(KC):
                        P.op("pe", lambda E, pi=pi, kc=kc, tt=tt, W=W, ncols=ncols: E.matmul(
                            PS[pi][:, 0:ncols], lhsT=uT[:, kc, tt * 128:(tt + 1) * 128],
                            rhs=W[:, kc, 0:ncols], start=(kc == 0), stop=(kc == KC - 1)),
                             reads=[Wb[kc // 8], uT_b], writes=[PSB[pi]])
                    tok0 = t0 + tt * 128
                    if kind == "DT":
                        di = state["dts"] % 2
                        state["dts"] += 1
                        d3 = _r3(dts[di], 64)
                        db = dts_b[di]
                        P.op("dve", lambda E, pi=pi, d3=d3: E.tensor_tensor(out=d3[:, 0, :], in0=PS[pi][:, 0:64], in1=dtb, op=ALU.add),
                             reads=[PSB[pi], dtb_b], writes=[(db, 0)])
                        P.op("act", lambda E, d3=d3: E.activation(out=d3[:, 1, :], in_=d3[:, 0, :], func=AF.Abs),
                             reads=[(db, 0)], writes=[(db, 1)])
                        P.op("act", lambda E, d3=d3: E.activation(out=d3[:, 2, :], in_=d3[:, 1, :], func=AF.Exp, scale=-1.0),
                             reads=[(db, 1)], writes=[(db, 2)])
                        P.op("act", lambda E, d3=d3: E.activation(out=d3[:, 1, :], in_=d3[:, 2, :], func=AF.Ln, bias=1.0),
                             reads=[(db, 2)], writes=[(db, 1)])
                        P.op("dve", lambda E, d3=d3: E.scalar_tensor_tensor(out=d3[:, 3, :], in0=d3[:, 0, :], scalar=0.0, in1=d3[:, 1, :],
                                                                            op0=ALU.max, op1=ALU.add),
                             reads=[(db, 0), (db, 1)], writes=[(db, 3)])
                        P.dma("sp", s_dt[tok0:tok0 + 128, :], d3[:, 3, :], db, reads=[(db, 3)])
                    else:
                        si = next_stg()
                        if kind == "V":
                            P.op("act", lambda E, pi=pi, si=si: E.copy(out=stg[si], in_=PS[pi][:, :]),
                                 reads=[PSB[pi]], writes=[stg_b[si]])
                            dst = s_v[tok0:tok0 + 128, gidx * 512:(gidx + 1) * 512]
                        else:
                            P.op("act", lambda E, pi=pi, si=si: E.activation(out=stg[si], in_=PS[pi][:, :], func=AF.Silu),
                                 reads=[PSB[pi]], writes=[stg_b[si]])
                            dd = s_sg if kind == "G" else s_sz
                            dst = dd[tok0 - TP:tok0 - TP + 128, gidx * 512:(gidx + 1) * 512]
                        P.dma("sp", dst, stg[si], stg_b[si], reads=[stg_b[si]])
                    drain(1)

    drain(len(deferred))
    P.barrier()
    if stop_after == 1:
        return finalize()
    AR.reset()
    for b in PSB:
        b.st = {}

    NQT = TM // 128
    NQC = TM // 512
    NKT = T // 128
    PKT = TP // 128
    gb_c, gb_b = cload("gb", c_gb, NQT * NBLK)
    ob_c, ob_b = cload("ob", c_ob, NQT * NBLK)
    caus, caus_b = cload("caus", c_caus, 4 * 512, BF16, cast=True)
    kT = [AR.alloc(T, BF16) for _ in range(2)]
    kT_b = [Buf("kT0"), Buf("kT1")]
    qT = [AR.alloc(TM, BF16) for _ in range(2)]
    qT_b = [Buf("qT0"), Buf("qT1")]
    va = [_r3(AR.alloc(NKT * 132, BF16), 132) for _ in range(2)]
    va_b = [Buf("va0"), Buf("va1")]
    sgt = [_r3(AR.alloc(NQT * 128, BF16), 128) for _ in range(2)]
    sgt_b = [Buf("sg0"), Buf("sg1")]
    kmf = AR.alloc(NBLK)
    kmf_b = Buf("kmf")
    kmb = AR.alloc(NBLK, BF16)
    kmb_b = Buf("kmb")
    NG = NQT * NBLK
    gA = AR.alloc(NG)
    gB = AR.alloc(NG)
    gC = AR.alloc(NG)
    gM = AR.alloc(NQT)
    gA_b, gB_b, gC_b, gM_b = Buf("gA"), Buf("gB"), Buf("gC"), Buf("gM")
    gbf = AR.alloc(NG, BF16)
    gbf_b = Buf("gbf")
    biasT = AR.alloc(TM, BF16, NBLK)
    biasT_b = Buf("biasT")
    NPT = 6
    pt = [AR.alloc(512, BF16) for _ in range(NPT)]
    pt_b = [Buf("pt%d" % i) for i in range(NPT)]
    rinv = AR.alloc(8)
    rinv_b = Buf("rinv")
    ogs = [AR.alloc(128, BF16) for _ in range(4)]
    ogs_b = [Buf("ogs%d" % i) for i in range(4)]
    ogst = [AR.alloc(512, BF16) for _ in range(2)]
    ogst_b = [Buf("ogst0"), Buf("ogst1")]

    for s in range(2):
        P.op("dve", lambda E, s=s: E.memset(va[s][:, :, 128:129], 1.0), writes=[(va_b[s], "one")])

    def attn_load(h):
        s = h % 2
        P.dma("sp", kT[s], s_kT[h * 128:(h + 1) * 128, :], kT_b[s], writes=[kT_b[s]])
        P.dma("sp", qT[s], s_qT[h * 128:(h + 1) * 128, :], qT_b[s], writes=[qT_b[s]])
        P.dma("sp", va[s][:, :, 0:128], s_v[:, h * 128:(h + 1) * 128].rearrange("(t p) c -> p t c", p=128),
              va_b[s], writes=[(va_b[s], "v")])
        P.dma("sp", sgt[s], s_sg[:, h * 128:(h + 1) * 128].rearrange("(t p) c -> p t c", p=128),
              sgt_b[s], writes=[sgt_b[s]])

    s_mask = [dscr("s_mask%d" % i, [NBLK, TM], BF16) for i in range(2)]
    smask_b = [Buf("smask0"), Buf("smask1")]
    mfull = [AR.alloc(NBLK * TM, BF16) for _ in range(2)]
    mfull_b = [[Buf("mf%d_%d" % (i, q)) for q in range(4)] for i in range(2)]
    pt_i = 0
    og_i = 0

    def prologue(h):
        s = h % 2
        K_, Q_, V_, SGt = kT[s], qT[s], va[s], sgt[s]
        P.op("dve", lambda E, K_=K_: E.tensor_reduce(out=kmf, in_=_r3(K_, BLK), axis=AX.X, op=ALU.add),
             reads=[kT_b[s]], writes=[kmf_b])
        P.op("act", lambda E: E.activation(out=kmb, in_=kmf, func=AF.Copy, scale=1.0 / BLK),
             reads=[kmf_b], writes=[kmb_b])
        for qt in range(NQT):
            P.op("pe", lambda E, qt=qt, Q_=Q_: E.matmul(PS[7][:, qt * NBLK:(qt + 1) * NBLK], lhsT=Q_[:, qt * 128:(qt + 1) * 128],
                                                        rhs=kmb, start=True, stop=True),
                 reads=[qT_b[s], kmb_b], writes=[(PSB[7], qt)])
        g3 = lambda a: _r3(a, NBLK)
        mb = lambda: gM.unsqueeze(2).to_broadcast([128, NQT, NBLK])
        P.op("dve", lambda E: E.tensor_tensor(out=gA, in0=PS[7][:, 0:NG], in1=gb_c, op=ALU.add),
             reads=[PSB[7], gb_b], writes=[gA_b])
        P.op("dve", lambda E: E.tensor_reduce(out=gM, in_=g3(gA), axis=AX.X, op=ALU.max), reads=[gA_b], writes=[gM_b])
        P.op("dve", lambda E: E.tensor_tensor(out=g3(gC), in0=g3(gA), in1=mb(), op=ALU.is_ge), reads=[gA_b, gM_b], writes=[gC_b])
        P.op("dve", lambda E: E.scalar_tensor_tensor(out=gB, in0=gC, scalar=-BIG, in1=gA, op0=ALU.mult, op1=ALU.add),
             reads=[gC_b, gA_b], writes=[gB_b])
        P.op("dve", lambda E: E.tensor_reduce(out=gM, in_=g3(gB), axis=AX.X, op=ALU.max), reads=[gB_b], writes=[gM_b])
        P.op("dve", lambda E: E.tensor_tensor(out=g3(gC), in0=g3(gB), in1=mb(), op=ALU.is_ge), reads=[gB_b, gM_b], writes=[gC_b])
        P.op("dve", lambda E: E.scalar_tensor_tensor(out=gB, in0=gC, scalar=-BIG, in1=gB, op0=ALU.mult, op1=ALU.add),
             reads=[gC_b, gB_b], writes=[gB_b])
        P.op("dve", lambda E: E.tensor_reduce(out=gM, in_=g3(gB), axis=AX.X, op=ALU.max), reads=[gB_b], writes=[gM_b])
        P.op("dve", lambda E: E.tensor_tensor(out=g3(gC), in0=g3(gA), in1=mb(), op=ALU.is_ge), reads=[gA_b, gM_b], writes=[gC_b])
        P.op("dve", lambda E: E.tensor_scalar(out=gB, in0=gC, scalar1=BIG, scalar2=-BIG, op0=ALU.mult, op1=ALU.add),
             reads=[gC_b], writes=[gB_b])
        P.op("dve", lambda E: E.tensor_tensor(out=gA, in0=gB, in1=gb_c, op=ALU.add), reads=[gB_b, gb_b, gA_b], writes=[gA_b])
        P.op("dve", lambda E: E.tensor_tensor(out=gB, in0=gA, in1=ob_c, op=ALU.max), reads=[gA_b, ob_b, gB_b], writes=[gB_b])
        P.op("dve", lambda E: E.tensor_single_scalar(out=gbf, in_=gB, scalar=-1.0, op=ALU.is_ge), reads=[gB_b], writes=[gbf_b])
        for half in range(NQT // 8):
            psb = PS[7].bitcast(BF16)
            for q8 in range(8):
                qt = half * 8 + q8
                P.op("pe", lambda E, psb=psb, q8=q8, qt=qt: E.transpose(psb[0:NBLK, q8 * 128:(q8 + 1) * 128],
                                                                        gbf[:, qt * NBLK:(qt + 1) * NBLK], ident),
                     reads=[gbf_b, ident_b], writes=[(PSB[7], q8)])
            P.op("act", lambda E, psb=psb, half=half: E.copy(out=biasT[:, half * 1024:(half + 1) * 1024], in_=psb[0:NBLK, 0:1024]),
                 reads=[PSB[7]], writes=[(biasT_b, half)])
        P.dma("sp", s_mask[s], biasT, biasT_b, reads=[biasT_b], writes=[smask_b[s]])
        for q4 in range(4):
            nb4 = NBLK // 4
            P.dma("sp", mfull[s][:, q4 * nb4 * TM:(q4 + 1) * nb4 * TM],
                  s_mask[s][q4 * nb4:(q4 + 1) * nb4, :].rearrange("b t -> (b t)").partition_broadcast(128),
                  mfull_b[s][q4], reads=[smask_b[s]], writes=[mfull_b[s][q4]])

    def main_qc(h, qc):
        nonlocal pt_i, og_i
        s = h % 2
        K_, Q_, V_, SGt, MF = kT[s], qT[s], va[s], sgt[s], mfull[s]
        if True:
            nkt = PKT + 4 * (qc + 1)

            def emit_s(kt, qc=qc):
                nonlocal pt_i
                pi = (0, 1, 6)[kt % 3]
                blk = kt // 2
                P.op("pe", lambda E, pi=pi, kt=kt, qc=qc, K_=K_, Q_=Q_: E.matmul(
                    PS[pi][:, :], lhsT=K_[:, kt * 128:(kt + 1) * 128], rhs=Q_[:, qc * 512:(qc + 1) * 512], start=True, stop=True),
                     reads=[kT_b[s], qT_b[s]], writes=[PSB[pi]])
                pj = pt_i % NPT
                pt_i += 1
                P.op("act", lambda E, pi=pi, pj=pj: E.activation(out=pt[pj], in_=PS[pi][:, :], func=AF.Exp, scale=SCALE),
                     reads=[PSB[pi]], writes=[pt_b[pj]])
                moff = blk * TM + qc * 512
                P.op("dve", lambda E, pj=pj, moff=moff, MF=MF: E.tensor_tensor(out=pt[pj], in0=pt[pj], in1=MF[:, moff:moff + 512], op=ALU.mult),
                     reads=[pt_b[pj], mfull_b[s][blk // (NBLK // 4)]], writes=[pt_b[pj]])
                dj = kt - (PKT + 4 * qc)
                if dj >= 0:
                    P.op("dve", lambda E, pj=pj, dj=dj: E.tensor_tensor(out=pt[pj], in0=pt[pj], in1=caus[:, dj * 512:(dj + 1) * 512], op=ALU.mult),
                         reads=[pt_b[pj], caus_b], writes=[pt_b[pj]])
                return pj

            def emit_pv(kt, pj, nkt=nkt):
                for qs in range(4):
                    P.op("pe", lambda E, qs=qs, pj=pj, kt=kt, nkt=nkt, V_=V_: E.matmul(
                        PS[2 + qs][:, 0:129], lhsT=pt[pj][:, qs * 128:(qs + 1) * 128], rhs=V_[:, kt, 0:129],
                        start=(kt == 0), stop=(kt == nkt - 1)),
                         reads=[pt_b[pj], va_b[s]], writes=[PSB[2 + qs]])

            pend = []
            for kt in range(nkt):
                pend.append((kt, emit_s(kt)))
                if len(pend) > 2:
                    emit_pv(*pend.pop(0))
            while pend:
                emit_pv(*pend.pop(0))
            oi = og_i % 2
            og_i += 1
            psb = PS[7].bitcast(BF16)
            for qs in range(4):
                qt = qc * 4 + qs
                P.op("dve", lambda E, qs=qs: E.reciprocal(out=rinv[:, qs:qs + 1], in_=PS[2 + qs][:, 128:129]),
                     reads=[PSB[2 + qs]], writes=[(rinv_b, qs)])
                P.op("dve", lambda E, qs=qs, qt=qt, SGt=SGt: E.scalar_tensor_tensor(
                    out=ogs[qs], in0=PS[2 + qs][:, 0:128], scalar=rinv[:, qs:qs + 1], in1=SGt[:, qt, :], op0=ALU.mult, op1=ALU.mult),
                     reads=[PSB[2 + qs], (rinv_b, qs), sgt_b[s]], writes=[ogs_b[qs]])
                P.op("pe", lambda E, qs=qs, psb=psb: E.transpose(psb[:, qs * 128:(qs + 1) * 128], ogs[qs], ident),
                     reads=[ogs_b[qs], ident_b], writes=[(PSB[7], qs)])
            P.op("act", lambda E, psb=psb, oi=oi: E.copy(out=ogst[oi], in_=psb[:, 0:512]), reads=[PSB[7]], writes=[ogst_b[oi]])
            P.dma("sp", s_ogT[h * 128:(h + 1) * 128, qc * 512:(qc + 1) * 512], ogst[oi], ogst_b[oi], reads=[ogst_b[oi]])


    attn_load(0)
    prologue(0)
    for h in range(NH):
        if h + 1 < NH:
            attn_load(h + 1)
        main_qc(h, 0)
        if h + 1 < NH:
            prologue(h + 1)
        for qc in range(1, NQC):
            main_qc(h, qc)

    P.barrier()
    if stop_after == 2:
        return finalize()
    AR.reset()
    for b in PSB:
        b.st = {}

    alog, alog_b = cload("alog", c_alog, SH)
    dsk, dsk_b = cload("dsk", c_dsk, SSD_IN)
    snw, snw_b = cload("snw", c_snw, SSD_IN)
    Aneg = AR.alloc(SH)
    Aneg_b = Buf("Aneg")
    P.op("act", lambda E: E.activation(out=Aneg, in_=alog, func=AF.Exp), reads=[alog_b], writes=[Aneg_b])
    P.op("dve", lambda E: E.tensor_single_scalar(out=Aneg, in_=Aneg, scalar=-1.0, op=ALU.mult), reads=[Aneg_b], writes=[Aneg_b])
    xs_t = [AR.alloc(SSD_IN, BF16) for _ in range(2)]
    xs_b = [Buf("xs0"), Buf("xs1")]
    btm_t = [AR.alloc(SG * SN, BF16) for _ in range(2)]
    btm_b = [Buf("btm0"), Buf("btm1")]
    bT_t = [AR.alloc(SG * 128, BF16) for _ in range(2)]
    bT_b = [Buf("bT0"), Buf("bT1")]
    cT_t = [AR.alloc(SG * 128, BF16) for _ in range(2)]
    cT_b = [Buf("cT0"), Buf("cT1")]
    dt_t = [AR.alloc(SH) for _ in range(2)]
    dt_b = [Buf("dt0"), Buf("dt1")]
    sz_t = [AR.alloc(SSD_IN, BF16) for _ in range(2)]
    sz_b = [Buf("sz0"), Buf("sz1")]
    stS = AR.alloc(SSD_IN)
    stS_b = Buf("state")
    stB = AR.alloc(SSD_IN, BF16)
    stB_b = Buf("stateb")
    a_tm = AR.alloc(SH)
    a_b = Buf("a_tm")
    acs = AR.alloc(SH)
    acs_b = Buf("acs")
    E_tm = AR.alloc(SH)
    E_b = Buf("E_tm")
    wl = AR.alloc(SH)
    wl_b = Buf("wl")
    w_tm = AR.alloc(SH)
    w_b = Buf("w_tm")
    Etot = AR.alloc(SH)
    Etot_b = Buf("Etot")
    dtw = AR.alloc(SH)
    dtw_b = Buf("dtw")
    xdt = AR.alloc(SSD_IN, BF16)
    xdt_b = Buf("xdt")
    xdtw = AR.alloc(SSD_IN, BF16)
    xdtw_b = Buf("xdtw")
    rall = [AR.alloc(512) for _ in range(2)]
    rall_b = [Buf("rall0"), Buf("rall1")]
    decT = [AR.alloc(512, BF16) for _ in range(2)]
    decT_b = [Buf("decT0"), Buf("decT1")]
    cbm = [AR.alloc(128, BF16) for _ in range(2)]
    cbm_b = [Buf("cbm0"), Buf("cbm1")]
    MT = [AR.alloc(512, BF16) for _ in range(2)]
    MT_b = [Buf("MT0"), Buf("MT1")]
    yv = [AR.alloc(512) for _ in range(2)]
    yv_b = [Buf("yv0"), Buf("yv1")]
    y2 = [AR.alloc(512) for _ in range(2)]
    y2_b = [Buf("y20"), Buf("y21")]
    ysq = AR.alloc(512)
    ysq_b = Buf("ysq")
    ssn = AR.alloc(8)
    ssn_b = Buf("ssn")
    ynb = [AR.alloc(512, BF16) for _ in range(2)]
    ynb_b = [Buf("ynb0"), Buf("ynb1")]
    ynst = [AR.alloc(512, BF16) for _ in range(2)]
    ynst_b = [Buf("ynst0"), Buf("ynst1")]

    P.op("dve", lambda E: E.memset(stS, 0.0), writes=[stS_b])
    P.op("dve", lambda E: E.memset(stB, 0.0), writes=[stB_b])

    NCH = T // 128
    PCH = TP // 128

    def ssd_load(c):
        s = c % 2
        r0 = c * 128
        P.dma("sp", xs_t[s], s_xs[r0:r0 + 128, :], xs_b[s], writes=[xs_b[s]])
        P.dma("sp", btm_t[s], s_btm[r0:r0 + 128, :], btm_b[s], writes=[btm_b[s]])
        P.dma("sp", _r3(bT_t[s], 128), s_bT[:, r0:r0 + 128].rearrange("(g n) s -> n g s", n=128), bT_b[s], writes=[bT_b[s]])
        P.dma("sp", dt_t[s], s_dt[r0:r0 + 128, :], dt_b[s], writes=[dt_b[s]])
        if c >= PCH:
            m0 = r0 - TP
            P.dma("sp", _r3(cT_t[s], 128), s_cT[:, r0:r0 + 128].rearrange("(g n) s -> n g s", n=128), cT_b[s], writes=[cT_b[s]])
            P.dma("sp", sz_t[s], s_sz[m0:m0 + 128, :], sz_b[s], writes=[sz_b[s]])

    onesf = AR.alloc(128)
    onesf_b = Buf("onesf")
    P.op("dve", lambda E: E.memset(onesf, 1.0), writes=[onesf_b])
    caus01 = AR.alloc(128, BF16)
    caus01_b = Buf("caus01")
    P.op("dve", lambda E: E.tensor_copy(out=caus01, in_=tri), reads=[tri_b], writes=[caus01_b])

    ssd_load(0)
    ri = 0
    for c in range(NCH):
        s = c % 2
        main = c >= PCH
        if c + 1 < NCH:
            ssd_load(c + 1)
        XS, BTM, BT, CT, DT, SZ = xs_t[s], btm_t[s], _r3(bT_t[s], 128), _r3(cT_t[s], 128), dt_t[s], sz_t[s]
        xsb, btmb, bTb, cTb, dtb_, szb = xs_b[s], btm_b[s], bT_b[s], cT_b[s], dt_b[s], sz_b[s]
        P.op("dve", lambda E, DT=DT: E.tensor_tensor(out=a_tm, in0=DT, in1=Aneg, op=ALU.mult), reads=[dtb_, Aneg_b], writes=[a_b])
        P.op("pe", lambda E: E.matmul(PS[6][:, 0:64], lhsT=tri, rhs=a_tm, start=True, stop=True), reads=[tri_b, a_b], writes=[(PSB[6], 0)])
        P.op("pe", lambda E: E.matmul(PS[6][:, 64:128], lhsT=onesf, rhs=a_tm, start=True, stop=True), reads=[onesf_b, a_b], writes=[(PSB[6], 1)])
        P.op("act", lambda E: E.copy(out=acs, in_=PS[6][:, 0:64]), reads=[(PSB[6], 0)], writes=[acs_b])
        P.op("act", lambda E: E.activation(out=E_tm, in_=PS[6][:, 0:64], func=AF.Exp), reads=[(PSB[6], 0)], writes=[E_b])
        P.op("act", lambda E: E.activation(out=Etot, in_=PS[6][:, 64:128], func=AF.Exp), reads=[(PSB[6], 1)], writes=[Etot_b])
        P.op("dve", lambda E: E.tensor_tensor(out=wl, in0=PS[6][:, 64:128], in1=acs, op=ALU.subtract), reads=[(PSB[6], 1), acs_b], writes=[wl_b])
        P.op("act", lambda E: E.activation(out=w_tm, in_=wl, func=AF.Exp), reads=[wl_b], writes=[w_b])
        P.op("dve", lambda E, DT=DT: E.tensor_tensor(out=dtw, in0=DT, in1=w_tm, op=ALU.mult), reads=[dtb_, w_b], writes=[dtw_b])
        if c < PCH:
            P.op("dve", lambda E: E.tensor_scalar(out=dtw, in0=dtw, scalar1=flag[:, 0:1], scalar2=None, op0=ALU.mult),
                 reads=[dtw_b, flag_b], writes=[dtw_b])
        x3 = lambda a: _r3(a, SP_)
        P.op("dve", lambda E, XS=XS: E.tensor_tensor(out=x3(xdtw), in0=x3(XS), in1=dtw.unsqueeze(2).to_broadcast([128, SH, SP_]), op=ALU.mult),
             reads=[xsb, dtw_b], writes=[xdtw_b])
        if main:
            P.op("dve", lambda E, XS=XS, DT=DT: E.tensor_tensor(out=x3(xdt), in0=x3(XS), in1=DT.unsqueeze(2).to_broadcast([128, SH, SP_]), op=ALU.mult),
                 reads=[xsb, dtb_], writes=[xdt_b])
        def front(g):
            nonlocal ri
            gs = slice(g * 512, (g + 1) * 512)
            if True:
                ci = g % 2
                P.op("pe", lambda E, g=g, BT=BT, CT=CT: E.matmul(PS[7][:, 0:128], lhsT=BT[:, g, :], rhs=CT[:, g, :], start=True, stop=True),
                     reads=[bTb, cTb], writes=[PSB[7]])
                P.op("dve", lambda E, ci=ci: E.tensor_tensor(out=cbm[ci], in0=PS[7][:, 0:128], in1=caus01, op=ALU.mult),
                     reads=[PSB[7], caus01_b], writes=[cbm_b[ci]])
                ypi = 4 + g % 2
                for qd in range(2):
                    h0 = g * 8 + qd * 4
                    rj = ri % 2
                    ri += 1
                    P.op("dve", lambda E, rj=rj, h0=h0: E.tensor_tensor(
                        out=_r3(rall[rj], 128), in0=tri.unsqueeze(1).to_broadcast([128, 4, 128]),
                        in1=a_tm[:, h0:h0 + 4].unsqueeze(2).to_broadcast([128, 4, 128]), op=ALU.mult),
                         reads=[tri_b, a_b], writes=[rall_b[rj]])
                    dpi = rj
                    P.op("pe", lambda E, dpi=dpi, rj=rj: E.matmul(PS[dpi][:, :], lhsT=ustr, rhs=rall[rj], start=True, stop=True),
                         reads=[ustr_b, rall_b[rj]], writes=[PSB[dpi]])
                    P.op("act", lambda E, dpi=dpi, rj=rj: E.activation(out=decT[rj], in_=PS[dpi][:, :], func=AF.Exp),
                         reads=[PSB[dpi]], writes=[decT_b[rj]])
                    P.op("dve", lambda E, rj=rj, ci=ci: E.tensor_tensor(
                        out=_r3(MT[rj], 128), in0=_r3(decT[rj], 128), in1=cbm[ci].unsqueeze(1).to_broadcast([128, 4, 128]), op=ALU.mult),
                         reads=[decT_b[rj], cbm_b[ci]], writes=[MT_b[rj]])
                    for hh in range(4):
                        h = h0 + hh
                        col = (qd * 4 + hh) * 64
                        P.op("pe", lambda E, ypi=ypi, rj=rj, hh=hh, h=h, col=col: E.matmul(
                            PS[ypi][:, col:col + 64], lhsT=MT[rj][:, hh * 128:(hh + 1) * 128], rhs=xdt[:, h * 64:(h + 1) * 64],
                            start=True, stop=True),
                             reads=[MT_b[rj], xdt_b], writes=[(PSB[ypi], col)])
                opi = 2 + g % 2
                P.op("pe", lambda E, opi=opi, g=g, gs=gs, CT=CT: E.matmul(PS[opi][:, :], lhsT=CT[:, g, :], rhs=stB[:, gs], start=True, stop=True),
                     reads=[cTb, (stB_b, g)], writes=[PSB[opi]])

        def back(g):
            gs = slice(g * 512, (g + 1) * 512)
            ypi = 4 + g % 2
            opi = 2 + g % 2
            if main:
                yj = g % 2
                hs = slice(g * 8, g * 8 + 8)
                P.op("dve", lambda E, opi=opi, yj=yj, hs=hs: E.tensor_tensor(
                    out=x3(yv[yj]), in0=x3(PS[opi][:, :]), in1=E_tm[:, hs].unsqueeze(2).to_broadcast([128, 8, SP_]), op=ALU.mult),
                     reads=[PSB[opi], E_b], writes=[yv_b[yj]])
                P.op("dve", lambda E, ypi=ypi, yj=yj: E.tensor_tensor(out=yv[yj], in0=yv[yj], in1=PS[ypi][:, :], op=ALU.add),
                     reads=[yv_b[yj], PSB[ypi]], writes=[yv_b[yj]])
                P.op("dve", lambda E, yj=yj, gs=gs, XS=XS: E.tensor_tensor(out=y2[yj], in0=XS[:, gs], in1=dsk[:, gs], op=ALU.mult),
                     reads=[xsb, dsk_b], writes=[y2_b[yj]])
                P.op("dve", lambda E, yj=yj: E.tensor_tensor(out=yv[yj], in0=yv[yj], in1=y2[yj], op=ALU.add),
                     reads=[yv_b[yj], y2_b[yj]], writes=[yv_b[yj]])
                P.op("dve", lambda E, yj=yj, gs=gs, SZ=SZ: E.tensor_tensor(out=yv[yj], in0=yv[yj], in1=SZ[:, gs], op=ALU.mult),
                     reads=[yv_b[yj], szb], writes=[yv_b[yj]])
                P.op("act", lambda E, yj=yj: E.activation(out=ysq, in_=yv[yj], func=AF.Square, accum_out=ssn[:, 0:1]),
                     reads=[yv_b[yj]], writes=[ysq_b, (ssn_b, 0)])
                P.op("dve", lambda E: E.tensor_scalar(out=ssn[:, 1:2], in0=ssn[:, 0:1], scalar1=1.0 / 512, scalar2=EPS, op0=ALU.mult, op1=ALU.add),
                     reads=[(ssn_b, 0)], writes=[(ssn_b, 1)])
                P.op("act", lambda E: E.activation(out=ssn[:, 3:4], in_=ssn[:, 1:2], func=AF.Ln), reads=[(ssn_b, 1)], writes=[(ssn_b, 3)])
                P.op("act", lambda E: E.activation(out=ssn[:, 2:3], in_=ssn[:, 3:4], func=AF.Exp, scale=-0.5), reads=[(ssn_b, 3)], writes=[(ssn_b, 2)])
                P.op("dve", lambda E, yj=yj, gs=gs: E.scalar_tensor_tensor(
                    out=ynb[yj], in0=yv[yj], scalar=ssn[:, 2:3], in1=snw[:, gs], op0=ALU.mult, op1=ALU.mult),
                     reads=[yv_b[yj], (ssn_b, 2), snw_b], writes=[ynb_b[yj]])
                psb = PS[6].bitcast(BF16)
                for j in range(4):
                    P.op("pe", lambda E, j=j, yj=yj, psb=psb: E.transpose(psb[:, 512 + j * 128:512 + (j + 1) * 128], ynb[yj][:, j * 128:(j + 1) * 128], ident),
                         reads=[ynb_b[yj], ident_b], writes=[(PSB[6], 10 + j)])
                P.op("act", lambda E, yj=yj, psb=psb: E.copy(out=ynst[yj], in_=psb[:, 512:1024]),
                     reads=[(PSB[6], 10), (PSB[6], 11), (PSB[6], 12), (PSB[6], 13)], writes=[ynst_b[yj]])
                m0 = c * 128 - TP
                P.dma("sp", s_ynT[g * 512:(g + 1) * 512, m0:m0 + 128].rearrange("(j p) t -> p j t", p=128),
                      _r3(ynst[yj], 128), ynst_b[yj], reads=[ynst_b[yj]])
            if c + 1 < NCH:
                spi = 2 + g % 2
                P.op("pe", lambda E, spi=spi, g=g, gs=gs, BTM=BTM: E.matmul(PS[spi][:, :], lhsT=BTM[:, g * 128:(g + 1) * 128], rhs=xdtw[:, gs], start=True, stop=True),
                     reads=[btmb, xdtw_b], writes=[PSB[spi]])
                hs = slice(g * 8, g * 8 + 8)
                P.op("dve", lambda E, gs=gs, hs=hs: E.tensor_tensor(
                    out=x3(stS[:, gs]), in0=x3(stS[:, gs]), in1=Etot[:, hs].unsqueeze(2).to_broadcast([128, 8, SP_]), op=ALU.mult),
                     reads=[(stS_b, g), Etot_b], writes=[(stS_b, g)])
                P.op("dve", lambda E, gs=gs, spi=spi: E.tensor_tensor(out=stS[:, gs], in0=stS[:, gs], in1=PS[spi][:, :], op=ALU.add),
                     reads=[(stS_b, g), PSB[spi]], writes=[(stS_b, g)])
                P.op("act", lambda E, gs=gs: E.copy(out=stB[:, gs], in_=stS[:, gs]), reads=[(stS_b, g)], writes=[(stB_b, g)])


        if main:
            front(0)
            for g in range(1, SG):
                front(g)
                back(g - 1)
            back(SG - 1)
        else:
            for g in range(SG):
                back(g)

    P.barrier()
    if stop_after == 3:
        return finalize()
    AR.reset()
    for b in PSB:
        b.st = {}

    NTC3 = TM // 512
    NTT = TM // 128
    ogc = _r3(AR.alloc(16 * 512, BF16), 512)
    ogc_b = Buf("ogc")
    ync = _r3(AR.alloc(32 * 512, BF16), 512)
    ync_b = Buf("ync")
    mrg = _r3(AR.alloc(32 * 512, BF16), 512)
    mrg_b = Buf("mrg")
    w3 = [_r3(AR.alloc(KC * 512, BF16), 512) for _ in range(2)]
    w3_b = [[Buf("w3%d_%d" % (s_, q_)) for q_ in range(4)] for s_ in range(2)]
    sga = [AR.alloc(512, BF16) for _ in range(2)]
    sga_b = [Buf("sga0"), Buf("sga1")]
    sgs = [AR.alloc(512, BF16) for _ in range(2)]
    sgs_b = [Buf("sgs0"), Buf("sgs1")]
    m1 = [AR.alloc(512) for _ in range(2)]
    m1_b = [Buf("m10"), Buf("m11")]
    m2 = [AR.alloc(512) for _ in range(2)]
    m2_b = [Buf("m20"), Buf("m21")]
    xsl = [AR.alloc(512) for _ in range(2)]
    xsl_b = [Buf("xsl0"), Buf("xsl1")]
    hst = [AR.alloc(512) for _ in range(2)]
    hst_b = [Buf("hst0"), Buf("hst1")]
    hsq = AR.alloc(512, BF16)
    hsq_b = Buf("hsq")
    w_ao_r = w_ao.rearrange("(kc p) c -> p kc c", p=128)
    w_so_r = w_so.rearrange("(kc p) c -> p kc c", p=128)
    w_o_r = w_o.rearrange("(kc p) c -> p kc c", p=128)
    wq = []
    for g in range(8):
        wq.append(("A", w_ao_r, g, 16))
        wq.append(("S", w_so_r, g, 32))
    for g in range(8):
        wq.append(("O", w_o_r, g, 32))
    wcnt = [0]

    def load_w3(slot, src_r, g, nkc):
        for q in range(0, nkc, 8):
            P.dma("pool", w3[slot][:, q:q + 8, :], src_r[:, q:q + 8, g * 512:(g + 1) * 512], w3_b[slot][q // 8], writes=[w3_b[slot][q // 8]])

    all_w = [(tc3, e) for tc3 in range(NTC3) for e in wq]
    load_w3(0, all_w[0][1][1], all_w[0][1][2], all_w[0][1][3])
    wi = 0
    ei = 0
    for tc3 in range(NTC3):
        c0 = tc3 * 512
        P.dma("sp", ogc, s_ogT[:, c0:c0 + 512].rearrange("(kc p) t -> p kc t", p=128), ogc_b, writes=[ogc_b])
        P.dma("sp", ync, s_ynT[:, c0:c0 + 512].rearrange("(kc p) t -> p kc t", p=128), ync_b, writes=[ync_b])
        for (kind, src_r, g, nkc) in wq:
            slot = wi % 2
            wi += 1
            if wi < len(all_w):
                nx = all_w[wi][1]
                load_w3(wi % 2, nx[1], nx[2], nx[3])
            W = w3[slot]
            Wb = w3_b[slot]
            if kind == "A":
                for dt_ in range(4):
                    for kc in range(16):
                        P.op("pe", lambda E, dt_=dt_, kc=kc, W=W: E.matmul(PS[dt_][:, :], lhsT=W[:, kc, dt_ * 128:(dt_ + 1) * 128], rhs=ogc[:, kc, :],
                                                                            start=(kc == 0), stop=(kc == 15)),
                             reads=[Wb[kc // 8], ogc_b], writes=[PSB[dt_]])
            elif kind == "S":
                for dt_ in range(4):
                    til = g * 4 + dt_
                    e2 = ei % 2
                    ei += 1
                    P.dma("sp", sga[e2], s_sgm[til * 128:(til + 1) * 128, c0:c0 + 512], sga_b[e2], writes=[sga_b[e2]])
                    P.dma("sp", sgs[e2], s_sgm[D + til * 128:D + (til + 1) * 128, c0:c0 + 512], sgs_b[e2], writes=[sgs_b[e2]])
                    for kc in range(32):
                        P.op("pe", lambda E, dt_=dt_, kc=kc, W=W: E.matmul(PS[4 + dt_][:, :], lhsT=W[:, kc, dt_ * 128:(dt_ + 1) * 128], rhs=ync[:, kc, :],
                                                                            start=(kc == 0), stop=(kc == 31)),
                             reads=[Wb[kc // 8], ync_b], writes=[PSB[4 + dt_]])
                    P.op("dve", lambda E, dt_=dt_, e2=e2: E.tensor_tensor(out=m1[e2], in0=PS[dt_][:, :], in1=sga[e2], op=ALU.mult),
                         reads=[PSB[dt_], sga_b[e2]], writes=[m1_b[e2]])
                    P.op("dve", lambda E, dt_=dt_, e2=e2: E.tensor_tensor(out=m2[e2], in0=PS[4 + dt_][:, :], in1=sgs[e2], op=ALU.mult),
                         reads=[PSB[4 + dt_], sgs_b[e2]], writes=[m2_b[e2]])
                    P.op("dve", lambda E, til=til, e2=e2: E.tensor_tensor(out=mrg[:, til, :], in0=m1[e2], in1=m2[e2], op=ALU.add),
                         reads=[m1_b[e2], m2_b[e2]], writes=[(mrg_b, til)])
            else:
                for tt in range(4):
                    ttg = tc3 * 4 + tt
                    e2 = ei % 2
                    ei += 1
                    pi = (g * 4 + tt) % 8
                    P.dma("sp", xsl[e2], x_all[TP + ttg * 128:TP + (ttg + 1) * 128, g * 512:(g + 1) * 512], xsl_b[e2], writes=[xsl_b[e2]])
                    for kc in range(32):
                        P.op("pe", lambda E, pi=pi, tt=tt, kc=kc, W=W: E.matmul(PS[pi][:, :], lhsT=mrg[:, kc, tt * 128:(tt + 1) * 128], rhs=W[:, kc, :],
                                                                                start=(kc == 0), stop=(kc == 31)),
                             reads=[Wb[kc // 8], mrg_b], writes=[PSB[pi]])
                    P.op("dve", lambda E, pi=pi, e2=e2: E.tensor_tensor(out=hst[e2], in0=PS[pi][:, :], in1=xsl[e2], op=ALU.add),
                         reads=[PSB[pi], xsl_b[e2]], writes=[hst_b[e2]])
                    P.op("act", lambda E, e2=e2, ttg=ttg, g=g: E.activation(out=hsq, in_=hst[e2], func=AF.Square, accum_out=ssf[:, ttg * 8 + g:ttg * 8 + g + 1]),
                         reads=[hst_b[e2]], writes=[hsq_b, (ssf_b, ttg * 8 + g)])
                    P.dma("sp", out[ttg * 128:(ttg + 1) * 128, g * 512:(g + 1) * 512], hst[e2], hst_b[e2], reads=[hst_b[e2]])

    P.barrier()
    AR.reset()
    fnw = AR.alloc(D)
    fnw_b = Buf("fnw")
    P.dma("sp", fnw, c_fnw, fnw_b, writes=[fnw_b])
    hrow = [AR.alloc(D) for _ in range(2)]
    hrow_b = [Buf("hrow0"), Buf("hrow1")]
    rs = AR.alloc(NTT * 2)
    rs_b = Buf("rs")
    ssf_b.st = {}
    for ttg in range(NTT):
        e2 = ttg % 2
        P.dma("sp", hrow[e2], out[ttg * 128:(ttg + 1) * 128, :], hrow_b[e2], writes=[hrow_b[e2]])
        P.op("dve", lambda E, ttg=ttg: E.tensor_reduce(out=rs[:, 2 * ttg:2 * ttg + 1], in_=ssf[:, ttg * 8:(ttg + 1) * 8], axis=AX.X, op=ALU.add),
             reads=[ssf_b], writes=[(rs_b, ttg)])
        P.op("dve", lambda E, ttg=ttg: E.tensor_scalar(out=rs[:, 2 * ttg + 1:2 * ttg + 2], in0=rs[:, 2 * ttg:2 * ttg + 1], scalar1=1.0 / D, scalar2=EPS,
                                                       op0=ALU.mult, op1=ALU.add),
             reads=[(rs_b, ttg)], writes=[(rs_b, ttg)])
        P.op("act", lambda E, ttg=ttg: E.sqrt(out=rs[:, 2 * ttg:2 * ttg + 1], in_=rs[:, 2 * ttg + 1:2 * ttg + 2]),
             reads=[(rs_b, ttg)], writes=[(rs_b, ttg)])
        P.op("dve", lambda E, ttg=ttg: E.reciprocal(out=rs[:, 2 * ttg:2 * ttg + 1], in_=rs[:, 2 * ttg:2 * ttg + 1]),
             reads=[(rs_b, ttg)], writes=[(rs_b, ttg)])
        P.op("dve", lambda E, ttg=ttg, e2=e2: E.scalar_tensor_tensor(out=hrow[e2], in0=hrow[e2], scalar=rs[:, 2 * ttg:2 * ttg + 1], in1=fnw,
                                                                     op0=ALU.mult, op1=ALU.mult),
             reads=[hrow_b[e2], (rs_b, ttg), fnw_b], writes=[hrow_b[e2]])
        P.dma("sp", out[ttg * 128:(ttg + 1) * 128, :], hrow[e2], hrow_b[e2], reads=[hrow_b[e2]])

    return finalize()


def make_consts(TP, TM, pos_main0, prefix_valid):
    T = TP + TM
    NBLK = T // BLK
    NQT = TM // 128
    c = {}
    c["c_ident"] = np.eye(128, dtype=np.float32)
    r = np.zeros((32, 128), np.float32)
    for m in range(32):
        r[(m + 16) % 32, m] = 1.0
    c["c_rot"] = r
    k = np.arange(128)
    c["c_tri"] = (k[:, None] <= k[None, :]).astype(np.float32)
    c["c_ustr"] = (k[:, None] > k[None, :]).astype(np.float32)
    q = np.arange(512)
    caus = np.zeros((128, 4, 512), np.float32)
    for j in range(4):
        caus[:, j, :] = ((j * 128 + k)[:, None] <= q[None, :])
    c["c_caus"] = caus.reshape(128, 2048)
    oh = np.zeros((NBLK, NBLK, 128), np.float32)
    for b in range(NBLK):
        oh[b, b, :] = 1.0
    c["c_onehot"] = oh.reshape(NBLK, NBLK * 128)
    gb = np.full((NQT, NBLK), -BIG, np.float32)
    ob = np.full((NQT, NBLK), -3 * BIG, np.float32)
    pblk = TP // BLK
    for qt in range(NQT):
        own = pblk + (qt * 128) // BLK
        for b in range(own):
            if b >= pblk or prefix_valid:
                gb[qt, b] = 0.0
        ob[qt, own] = 0.0
    c["c_gb"] = np.ascontiguousarray(np.broadcast_to(gb.reshape(1, -1), (128, NQT * NBLK)))
    c["c_ob"] = np.ascontiguousarray(np.broadcast_to(ob.reshape(1, -1), (128, NQT * NBLK)))
    c["c_flag"] = np.full((128, 1), 1.0 if prefix_valid else 0.0, np.float32)
    half = 16
    inv_freq = (ROPE_THETA ** (-np.arange(half, dtype=np.float32) * 2.0 / 32)).astype(np.float32)
    pos = np.arange(T, dtype=np.float32) - TP + pos_main0
    ang = (pos[None, :] * inv_freq[:, None]).astype(np.float32)
    cos = np.cos(ang).astype(np.float32)
    sin = np.sin(ang).astype(np.float32)
    c["c_cos"] = np.ascontiguousarray(np.concatenate([cos, cos, np.ones((96, T), np.float32)], 0))
    c["c_sin"] = np.ascontiguousarray(np.concatenate([-sin, sin, np.zeros((96, T), np.float32)], 0))
    return c


def make_param_consts(norm_w, conv_w, conv_b, dt_bias, a_log, d_skip, ssd_norm_w, gate_bias, final_norm_w):
    c = {}
    c["c_normw"] = np.ascontiguousarray(norm_w.reshape(KC, 128).T)
    cw = conv_w.reshape(4, 48, 128)
    c["c_convw"] = np.ascontiguousarray(cw.transpose(2, 1, 0).reshape(128, 48 * 4))
    c["c_convb"] = np.ascontiguousarray(conv_b.reshape(48, 128).T)
    c["c_gbias"] = np.ascontiguousarray(gate_bias.reshape(64, 128).T)
    bc = lambda v: np.ascontiguousarray(np.broadcast_to(v.reshape(1, -1), (128, v.size)))
    c["c_dtb"] = bc(dt_bias)
    c["c_alog"] = bc(a_log)
    c["c_dsk"] = bc(np.repeat(d_skip, SP_))
    c["c_snw"] = bc(ssd_norm_w)
    c["c_fnw"] = bc(final_norm_w)
    return c


_CACHE = {}


def kernel(x, norm_w, w_in, conv_w, conv_b, dt_bias, a_log, d_skip, ssd_norm_w,
           w_attn_out, w_ssd_out, gate_bias, w_out, final_norm_w):
    x = np.asarray(x, np.float32)
    B, S, _ = x.shape
    TP = TM = S // 2
    f = lambda a: np.ascontiguousarray(np.asarray(a, np.float32))
    pc = make_param_consts(f(norm_w)[0], f(conv_w)[0], f(conv_b)[0], f(dt_bias)[0], f(a_log)[0], f(d_skip)[0],
                           f(ssd_norm_w)[0], f(gate_bias)[0], f(final_norm_w))
    shared = {"w_in": f(w_in)[0], "w_ao": f(w_attn_out)[0], "w_so": f(w_ssd_out)[0], "w_o": f(w_out)[0]}
    shared.update(pc)
    in_maps = []
    for core in range(8):
        b, j = core // 2, core % 2
        if j == 0:
            xa = np.concatenate([np.zeros((TP, D), np.float32), x[b, 0:TM]], 0)
        else:
            xa = x[b]
        m = {"x_all": np.ascontiguousarray(xa)}
        m.update(shared)
        m.update(make_consts(TP, TM, j * TM, j == 1))
        in_maps.append(m)
    key = (TP, TM)
    if key not in _CACHE:
        _CACHE[key] = build_program(TP, TM)
    nc = _CACHE[key]
    res = run_bass_kernel_spmd(nc, in_maps, core_ids=list(range(8)))
    outp = np.empty((B, S, D), np.float32)
    for core in range(8):
        b, j = core // 2, core % 2
        outp[b, j * TM:(j + 1) * TM] = res.results[core]["out"]
    return outp
```

```python
import numpy as np
import concourse.bass as bass
import concourse.mybir as mybir
from concourse.bass_utils import run_bass_kernel_spmd

F32 = mybir.dt.float32
BF16 = mybir.dt.bfloat16
AF = mybir.ActivationFunctionType
ALU = mybir.AluOpType
AX = mybir.AxisListType

D = 4096
KC = D // 128
AW = 2048
NH = 16
HD = 128
SSD_IN = 4096
SH = 64
SP_ = 64
SG = 8
SN = 128
CONVD = SSD_IN + 2 * SG * SN
IN_COLS = 4 * AW + SSD_IN + CONVD + SH + 2 * D
OFF_Q, OFF_K, OFF_V, OFF_G = 0, AW, 2 * AW, 3 * AW
OFF_Z = 4 * AW
OFF_X = OFF_Z + SSD_IN
OFF_B = OFF_X + SSD_IN
OFF_C = OFF_B + SG * SN
OFF_DT = OFF_C + SG * SN
OFF_GA = OFF_DT + SH
OFF_GS = OFF_GA + D
EPS = 1e-6
BIG = 30000.0
BLK = 256
ROPE_THETA = 500000.0


class Buf:
    __slots__ = ("name", "st", "sem", "cnt", "psum")

    def __init__(self, name, psum=False):
        self.name = name
        self.psum = psum
        self.st = {}
        self.sem = None
        self.cnt = 0


class Prog:
    COMPUTE = ("pe", "act", "dve")
    QUEUES = ("sp", "pool")

    def __init__(self, nc):
        self.nc = nc
        self.ops = {e: [] for e in self.COMPUTE + self.QUEUES}
        self.sems = {}
        self.pending = {e: [] for e in self.COMPUTE + self.QUEUES}
        self.dma_bufs = []
        self.nsem = 0

    def new_sem(self, name):
        s = self.nc.semaphore(name)
        h = s.__enter__()
        self.nsem += 1
        return h

    def setup(self):
        for e in self.COMPUTE:
            self.sems[e] = self.new_sem("sem_" + e)

    def _conf(self, buf, key):
        if key is None:
            return list(buf.st.values())
        r = []
        if key in buf.st:
            r.append(buf.st[key])
        if None in buf.st:
            r.append(buf.st[None])
        return r

    def _deps(self, reads, writes, eng=None):
        deps = []
        for (b, k) in reads:
            for ent in self._conf(b, k):
                if ent[0] is not None:
                    deps.append(ent[0])
                if b.psum:
                    for t in ent[1].values():
                        if not (t[0] == "eng" and t[1] == eng):
                            deps.append(t)
        for (b, k) in writes:
            for ent in self._conf(b, k):
                if ent[0] is not None:
                    deps.append(ent[0])
                deps.extend(ent[1].values())
        return deps

    @staticmethod
    def _addreader(d, tok):
        key = (tok[0], tok[1] if tok[0] == "eng" else id(tok[1]))
        if key not in d or d[key][2] < tok[2]:
            d[key] = tok

    def _update(self, tok, reads, writes):
        for (b, k) in reads:
            if k not in b.st:
                b.st[k] = [None, {}]
            self._addreader(b.st[k][1], tok)
        for (b, k) in writes:
            if k is None:
                b.st = {None: [tok, {}]}
            else:
                b.st[k] = [tok, {}]
                if None in b.st:
                    pass

    def _mkwaits(self, eng, deps):
        w = {}
        for t in deps:
            if t[0] == "eng":
                if t[1] == eng and eng == "pe":
                    continue
                key = ("eng", t[1])
                if key not in w or w[key] < t[2]:
                    w[key] = t[2]
                self.ops[t[1]][t[2]]["flag"] = True
            else:
                key = ("dma", id(t[1]), t[1])
                if key not in w or w[key] < t[2]:
                    w[key] = t[2]
        return w

    @staticmethod
    def _norm(r):
        b, k = r if isinstance(r, tuple) else (r, None)
        if b.psum:
            k = None
        return (b, k)

    def op(self, eng, fn, reads=(), writes=()):
        reads = [self._norm(r) for r in reads]
        writes = [self._norm(r) for r in writes]
        deps = self._deps(reads, writes, eng) + self.pending[eng]
        self.pending[eng] = []
        idx = len(self.ops[eng])
        self.ops[eng].append({"fn": fn, "waits": self._mkwaits(eng, deps), "flag": False, "dma": None})
        self._update(("eng", eng, idx), reads, writes)

    def dma(self, q, out_ap, in_ap, slot, reads=(), writes=()):
        reads = [self._norm(r) for r in reads]
        writes = [self._norm(r) for r in writes]
        deps = self._deps(reads, writes, q) + self.pending[q]
        self.pending[q] = []
        if slot.sem is None:
            slot.sem = self.new_sem("dsem_" + slot.name)
            self.dma_bufs.append(slot)
        slot.cnt += 16
        tok = ("dma", slot.sem, slot.cnt)

        def fn(e, out_ap=out_ap, in_ap=in_ap):
            return e.dma_start(out=out_ap, in_=in_ap)
        self.ops[q].append({"fn": fn, "waits": self._mkwaits(q, deps), "flag": False, "dma": (slot.sem, 16)})
        self._update(tok, reads, writes)

    def barrier(self):
        toks = []
        for e in self.COMPUTE:
            if self.ops[e]:
                toks.append(("eng", e, len(self.ops[e]) - 1))
        for b in self.dma_bufs:
            toks.append(("dma", b.sem, b.cnt))
        for e in self.COMPUTE + self.QUEUES:
            self.pending[e] = self.pending[e] + toks

    def final_wait_tokens(self):
        return [("dma", b.sem, b.cnt) for b in self.dma_bufs]

    def emit(self):
        nc = self.nc
        cnts = {}
        for e in self.COMPUTE:
            c = 0
            arr = []
            for o in self.ops[e]:
                if o["flag"]:
                    c += 1
                arr.append(c)
            cnts[e] = arr
            assert c < 60000, (e, c)
        for b in self.dma_bufs:
            assert b.cnt < 60000, (b.name, b.cnt)
        engobj = {"pe": "tensor", "act": "scalar", "dve": "vector", "sp": "sync", "pool": "gpsimd"}
        with nc.Block() as block:
            for e in self.COMPUTE + self.QUEUES:
                ops = self.ops[e]
                sems = self.sems

                def body(E, ops=ops, e=e):
                    waited = {}
                    for o in ops:
                        for key, val in o["waits"].items():
                            if key[0] == "eng":
                                sem = sems[key[1]]
                                v = cnts[key[1]][val]
                                wk = key
                            else:
                                sem = key[2]
                                v = val
                                wk = key[:2]
                            if waited.get(wk, -1) >= v:
                                continue
                            waited[wk] = v
                            E.wait_ge(sem, v)
                        ins = o["fn"](E)
                        if o["dma"] is not None:
                            ins.then_inc(o["dma"][0], o["dma"][1])
                        elif o["flag"]:
                            ins.then_inc(sems[e], 1)
                getattr(block, engobj[e])(body)


class Arena:
    def __init__(self, ap, nwords):
        self.ap = ap
        self.n = nwords
        self.off = 0
        self.mark_ = 0

    def mark(self):
        self.mark_ = self.off

    def reset(self):
        self.off = self.mark_

    def alloc(self, nelem, dtype=F32, parts=128):
        words = nelem if dtype == F32 else (nelem + 1) // 2
        words = (words + 7) // 8 * 8
        assert self.off + words <= self.n, ("arena overflow", self.off, words, self.n)
        a = self.ap[0:parts, self.off:self.off + words]
        self.off += words
        if dtype != F32:
            a = a.bitcast(dtype)
        return a[:, 0:nelem]


def _r3(ap, b):
    return ap.rearrange("p (a b) -> p a b", b=b)


def build_program(TP, TM, debug=False, stop_after=None):
    T = TP + TM
    NBLK = T // BLK
    nc = bass.Bass("TRN2", target_bir_lowering=False)
    P = Prog(nc)

    def din(name, shape, dt=F32):
        return nc.dram_tensor(name, list(shape), dt, kind="ExternalInput").ap()

    skind = "ExternalOutput" if debug else "Internal"

    def dscr(name, shape, dt):
        return nc.dram_tensor(name, list(shape), dt, kind=skind).ap()

    x_all = din("x_all", [T, D])
    w_in = din("w_in", [D, IN_COLS])
    w_ao = din("w_ao", [AW, D])
    w_so = din("w_so", [SSD_IN, D])
    w_o = din("w_o", [D, D])
    c_normw = din("c_normw", [128, KC])
    c_convw = din("c_convw", [128, 48 * 4])
    c_convb = din("c_convb", [128, 48])
    c_gbias = din("c_gbias", [128, 64])
    c_dtb = din("c_dtb", [128, SH])
    c_alog = din("c_alog", [128, SH])
    c_dsk = din("c_dsk", [128, SSD_IN])
    c_snw = din("c_snw", [128, SSD_IN])
    c_fnw = din("c_fnw", [128, D])
    c_cos = din("c_cos", [128, T])
    c_sin = din("c_sin", [128, T])
    c_ident = din("c_ident", [128, 128])
    c_rot = din("c_rot", [32, 128])
    c_tri = din("c_tri", [128, 128])
    c_ustr = din("c_ustr", [128, 128])
    c_caus = din("c_caus", [128, 4 * 512])
    c_gb = din("c_gb", [128, (TM // 128) * NBLK])
    c_ob = din("c_ob", [128, (TM // 128) * NBLK])
    c_flag = din("c_flag", [128, 1])

    out = nc.dram_tensor("out", [TM, D], F32, kind="ExternalOutput").ap()

    s_qT = dscr("s_qT", [AW, TM], BF16)
    s_kT = dscr("s_kT", [AW, T], BF16)
    s_v = dscr("s_v", [T, AW], BF16)
    s_sg = dscr("s_sg", [TM, AW], BF16)
    s_sz = dscr("s_sz", [TM, SSD_IN], BF16)
    s_xs = dscr("s_xs", [T, SSD_IN], BF16)
    s_bT = dscr("s_bT", [SG * SN, T], BF16)
    s_btm = dscr("s_btm", [T, SG * SN], BF16)
    s_cT = dscr("s_cT", [SG * SN, T], BF16)
    s_dt = dscr("s_dt", [T, SH], F32)
    s_sgm = dscr("s_sgm", [2 * D, TM], BF16)
    s_ogT = dscr("s_ogT", [AW, TM], BF16)
    s_ynT = dscr("s_ynT", [SSD_IN, TM], BF16)

    AW_WORDS = 51 * 1024
    arena_g = nc.sbuf_tensor("arena", [128, AW_WORDS], F32)
    arena_t = arena_g.__enter__()
    AR = Arena(arena_t, AW_WORDS)
    psg = [nc.psum_tensor("ps%d" % i, [128, 512], F32) for i in range(8)]
    PS = [g.__enter__() for g in psg]
    PSB = [Buf("ps%d" % i, psum=True) for i in range(8)]
    P.setup()

    def cload(name, src, nelem, dt=F32, parts=128, cast=False):
        ap = AR.alloc(nelem, dt, parts)
        b = Buf(name)
        P.dma("pool" if cast else "sp", ap, src, b, writes=[b])
        return ap, b

    ident_f, identf_b = cload("identf", c_ident, 128)
    ident, ident_b = cload("ident", c_ident, 128, BF16, cast=True)
    rot, rot_b = cload("rot", c_rot, 128, BF16, parts=32, cast=True)
    tri, tri_b = cload("tri", c_tri, 128)
    ustr, ustr_b = cload("ustr", c_ustr, 128)
    flag, flag_b = cload("flag", c_flag, 1)
    ssf = AR.alloc((TM // 128) * 8)
    ssf_b = Buf("ssf")
    AR.mark()

    SCALE = HD ** -0.5

    def finalize():
        P.pending["sp"] = P.pending["sp"] + P.final_wait_tokens()
        P.ops["sp"].append({"fn": lambda E: E.nop(), "waits": P._mkwaits("sp", P.pending["sp"]), "flag": False, "dma": None})
        P.emit()
        return nc


    TCH = 1024
    normw, normw_b = cload("normw", c_normw, KC)
    convw, convw_b = cload("convw", c_convw, 48 * 4)
    convb, convb_b = cload("convb", c_convb, 48)
    gbias, gbias_b = cload("gbias", c_gbias, 64)
    dtb, dtb_b = cload("dtb", c_dtb, SH)
    uT = _r3(AR.alloc(KC * TCH, BF16), TCH)
    uT_b = Buf("uT")
    wsl = [_r3(AR.alloc(KC * 512, BF16), 512) for _ in range(2)]
    wsl_b = [[Buf("w%d_%d" % (s_, q_)) for q_ in range(4)] for s_ in range(2)]
    xt = [AR.alloc(D) for _ in range(1)]
    xt_b = [Buf("xt0")]
    xn = AR.alloc(D, BF16)
    xn_b = Buf("xn")
    ssq = AR.alloc(8)
    ssq_b = Buf("ssq")
    halo = AR.alloc(48 * 3)
    halo_b = Buf("halo")
    cst = [AR.alloc(520) for _ in range(2)]
    cst_b = [Buf("cst0"), Buf("cst1")]
    acc = [AR.alloc(512) for _ in range(2)]
    acc_b = [Buf("acc0"), Buf("acc1")]
    NST = 8
    stg = [AR.alloc(512, BF16) for _ in range(NST)]
    stg_b = [Buf("stg%d" % i) for i in range(NST)]
    stt = [AR.alloc(512, BF16) for _ in range(2)]
    stt_b = [Buf("stt0"), Buf("stt1")]
    NRT = 4
    rt1s = [AR.alloc(512) for _ in range(NRT)]
    rt1s_b = [Buf("rt1_%d" % i) for i in range(NRT)]
    rt2 = AR.alloc(512)
    rt2_b = Buf("rt2")
    cosb = AR.alloc(TCH)
    sinb = AR.alloc(TCH)
    cos_b, sin_b = Buf("cos"), Buf("sin")
    dts = [AR.alloc(64 * 4) for _ in range(2)]
    dts_b = [Buf("dts0"), Buf("dts1")]

    P.op("dve", lambda E: E.memset(halo, 0.0), writes=[halo_b])

    state = {"ps": 0, "st": 0, "stt": 0, "cst": 0, "w": 0, "dts": 0, "rt": 0}
    deferred = []
    LAG = 2

    def defer(fn):
        deferred.append(fn)
        while len(deferred) > LAG:
            deferred.pop(0)()

    def drain(n=1):
        for _ in range(n):
            if deferred:
                deferred.pop(0)()


    def next_ps(lo=0, hi=5):
        i = lo + state["ps"] % (hi - lo)
        state["ps"] += 1
        return i

    def next_stg():
        i = state["st"] % NST
        state["st"] += 1
        return i

    def groups_for(is_prefix):
        g = []
        for i in range(4):
            g.append(("K", OFF_K + 512 * i, 512, i))
        for i in range(4):
            g.append(("V", OFF_V + 512 * i, 512, i))
        for i in range(8):
            g.append(("X", OFF_X + 512 * i, 512, i))
        for i in range(2):
            g.append(("B", OFF_B + 512 * i, 512, i))
        g.append(("DT", OFF_DT, 64, 0))
        for i in range(2):
            g.append(("C", OFF_C + 512 * i, 512, i))
        if not is_prefix:
            for i in range(4):
                g.append(("Q", OFF_Q + 512 * i, 512, i))
            for i in range(4):
                g.append(("G", OFF_G + 512 * i, 512, i))
            for i in range(8):
                g.append(("Z", OFF_Z + 512 * i, 512, i))
            for i in range(16):
                g.append(("GM", OFF_GA + 512 * i, 512, i))
        return g

    w_in_r = w_in.rearrange("(kc p) c -> p kc c", p=128)

    def load_w(slot, src_r, c0, ncols, nkc=KC):
        for q in range(0, nkc, 8):
            P.dma("pool", wsl[slot][:, q:q + 8, 0:ncols], src_r[:, q:q + 8, c0:c0 + ncols],
                  wsl_b[slot][q // 8], writes=[wsl_b[slot][q // 8]])

    def fm_store(src_ap, sbuf_b, dst, row0, col0, n=512):
        P.dma("sp", dst[row0:row0 + 128, col0:col0 + n], src_ap, sbuf_b, reads=[sbuf_b])

    def transposed_store(src_bf, src_b, dst, tok0, col0):
        pi = 6 + state["stt"] % 2
        si = state["stt"] % 2
        state["stt"] += 1
        psb = PS[pi].bitcast(BF16)
        for j in range(4):
            P.op("pe", lambda E, j=j, psb=psb: E.transpose(psb[:, j * 128:(j + 1) * 128],
                                                           src_bf[:, j * 128:(j + 1) * 128], ident),
                 reads=[src_b, ident_b], writes=[(PSB[pi], j)])
        P.op("act", lambda E, psb=psb, si=si: E.copy(out=stt[si], in_=psb[:, 0:512]),
             reads=[PSB[pi]], writes=[stt_b[si]])
        P.dma("sp", dst[tok0:tok0 + 512, col0:col0 + 128].rearrange("(j p) c -> p j c", p=128),
              _r3(stt[si], 128), stt_b[si], reads=[stt_b[si]])

    n_chunks = T // TCH
    for tc in range(n_chunks):
        t0 = tc * TCH
        is_prefix = t0 < TP
        tm0 = t0 - TP
        drain(len(deferred))
        P.dma("sp", cosb, c_cos[:, t0:t0 + TCH], cos_b, writes=[cos_b])
        P.dma("sp", sinb, c_sin[:, t0:t0 + TCH], sin_b, writes=[sin_b])
        for tt in range(TCH // 128):
            r0 = t0 + tt * 128
            P.dma("sp", xt[0], x_all[r0:r0 + 128, :], xt_b[0], writes=[xt_b[0]])
            P.op("act", lambda E: E.activation(out=xn, in_=xt[0], func=AF.Square, accum_out=ssq[:, 0:1]),
                 reads=[xt_b[0]], writes=[xn_b, (ssq_b, 0)])
            P.op("dve", lambda E: E.tensor_scalar(out=ssq[:, 1:2], in0=ssq[:, 0:1], scalar1=1.0 / D, scalar2=EPS,
                                                  op0=ALU.mult, op1=ALU.add),
                 reads=[(ssq_b, 0)], writes=[(ssq_b, 1)])
            P.op("act", lambda E: E.sqrt(out=ssq[:, 3:4], in_=ssq[:, 1:2]), reads=[(ssq_b, 1)], writes=[(ssq_b, 3)])
            P.op("dve", lambda E: E.reciprocal(out=ssq[:, 2:3], in_=ssq[:, 3:4]), reads=[(ssq_b, 3)], writes=[(ssq_b, 2)])
            P.op("act", lambda E: E.activation(out=xn, in_=xt[0], func=AF.Copy, scale=ssq[:, 2:3]),
                 reads=[xt_b[0], (ssq_b, 2)], writes=[xn_b])
            for j in range(8):
                pi = 6 + j % 2
                psb = PS[pi].bitcast(BF16)
                for q in range(4):
                    kc = 4 * j + q
                    P.op("pe", lambda E, psb=psb, q=q, kc=kc: E.transpose(psb[:, q * 128:(q + 1) * 128],
                                                                          xn[:, kc * 128:(kc + 1) * 128], ident),
                         reads=[xn_b, ident_b], writes=[(PSB[pi], q)])
                P.op("dve", lambda E, psb=psb, j=j, tt=tt: E.tensor_tensor(
                    out=uT[:, 4 * j:4 * j + 4, tt * 128:(tt + 1) * 128],
                    in0=_r3(psb[:, 0:512], 128),
                    in1=normw[:, 4 * j:4 * j + 4].unsqueeze(2).to_broadcast([128, 4, 128]), op=ALU.mult),
                     reads=[PSB[pi], normw_b], writes=[(uT_b, tt)])

        glist = groups_for(is_prefix)
        load_w(state["w"] % 2, w_in_r, glist[0][1], glist[0][2])
        for gi, (kind, c0, ncols, gidx) in enumerate(glist):
            slot = state["w"] % 2
            state["w"] += 1
            if gi + 1 < len(glist):
                load_w(state["w"] % 2, w_in_r, glist[gi + 1][1], glist[gi + 1][2])
            W = wsl[slot]
            Wb = wsl_b[slot]
            if kind in ("K", "Q", "X", "B", "C", "GM"):
                for ct in range(4):
                    for th in range(TCH // 512):
                        pi = next_ps()
                        for kc in range(KC):
                            P.op("pe", lambda E, pi=pi, kc=kc, ct=ct, th=th, W=W: E.matmul(
                                PS[pi][:, :], lhsT=W[:, kc, ct * 128:(ct + 1) * 128],
                                rhs=uT[:, kc, th * 512:(th + 1) * 512], start=(kc == 0), stop=(kc == KC - 1)),
                                 reads=[Wb[kc // 8], uT_b], writes=[PSB[pi]])
                        tok0 = t0 + th * 512
                        if kind in ("K", "Q"):
                            si = next_stg()
                            P.op("act", lambda E, pi=pi, si=si: E.copy(out=stg[si], in_=PS[pi][:, :]),
                                 reads=[PSB[pi]], writes=[stg_b[si]])
                            ri_ = state["rt"] % NRT
                            state["rt"] += 1
                            rt1, rt1_b = rt1s[ri_], rt1s_b[ri_]
                            P.op("dve", lambda E, pi=pi, th=th, rt1=rt1: E.tensor_tensor(
                                out=rt1, in0=PS[pi][:, :], in1=cosb[:, th * 512:(th + 1) * 512], op=ALU.mult),
                                 reads=[PSB[pi], cos_b], writes=[rt1_b])

                            def part_b(si=si, th=th, rt1=rt1, rt1_b=rt1_b, kind=kind, gidx=gidx, ct=ct, tok0=tok0):
                                pr = 5
                                P.op("pe", lambda E: E.matmul(PS[pr][:, :], lhsT=rot, rhs=stg[si][0:32, :], start=True, stop=True),
                                     reads=[stg_b[si], rot_b], writes=[PSB[pr]])
                                P.op("dve", lambda E: E.tensor_tensor(
                                    out=rt2, in0=PS[pr][:, :], in1=sinb[:, th * 512:(th + 1) * 512], op=ALU.mult),
                                     reads=[PSB[pr], sin_b], writes=[rt2_b])
                                P.op("dve", lambda E: E.tensor_tensor(out=stg[si], in0=rt1, in1=rt2, op=ALU.add),
                                     reads=[rt1_b, rt2_b, stg_b[si]], writes=[stg_b[si]])
                                row0 = gidx * 512 + ct * 128
                                if kind == "K":
                                    fm_store(stg[si], stg_b[si], s_kT, row0, tok0)
                                else:
                                    fm_store(stg[si], stg_b[si], s_qT, row0, tok0 - TP)
                            defer(part_b)
                        elif kind == "GM":
                            si = next_stg()
                            til = gidx * 4 + ct
                            P.op("act", lambda E, pi=pi, si=si, til=til: E.activation(
                                out=stg[si], in_=PS[pi][:, :], func=AF.Sigmoid, bias=gbias[:, til:til + 1]),
                                 reads=[PSB[pi], gbias_b], writes=[stg_b[si]])
                            fm_store(stg[si], stg_b[si], s_sgm, til * 128, tok0 - TP)
                        else:
                            cti = {"X": 0, "B": 32, "C": 40}[kind] + gidx * 4 + ct
                            ci = state["cst"] % 2
                            state["cst"] += 1
                            P.op("dve", lambda E, ci=ci, cti=cti: E.tensor_copy(out=cst[ci][:, 0:3], in_=halo[:, cti * 3:cti * 3 + 3]),
                                 reads=[(halo_b, cti)], writes=[(cst_b[ci], "h")])
                            P.op("act", lambda E, ci=ci, pi=pi: E.copy(out=cst[ci][:, 3:515], in_=PS[pi][:, :]),
                                 reads=[PSB[pi]], writes=[(cst_b[ci], "m")])
                            P.op("dve", lambda E, ci=ci, cti=cti: E.tensor_copy(out=halo[:, cti * 3:cti * 3 + 3], in_=cst[ci][:, 512:515]),
                                 reads=[(cst_b[ci], "m")], writes=[(halo_b, cti)])
                            P.op("dve", lambda E, ci=ci, cti=cti: E.tensor_scalar(
                                out=acc[ci], in0=cst[ci][:, 0:512], scalar1=convw[:, cti * 4:cti * 4 + 1], scalar2=None, op0=ALU.mult),
                                 reads=[cst_b[ci], convw_b], writes=[acc_b[ci]])
                            for k in range(1, 4):
                                P.op("dve", lambda E, ci=ci, cti=cti, k=k: E.scalar_tensor_tensor(
                                    out=acc[ci], in0=cst[ci][:, k:k + 512], scalar=convw[:, cti * 4 + k:cti * 4 + k + 1],
                                    in1=acc[ci], op0=ALU.mult, op1=ALU.add),
                                     reads=[cst_b[ci], convw_b, acc_b[ci]], writes=[acc_b[ci]])
                            si = next_stg()
                            P.op("act", lambda E, ci=ci, si=si, cti=cti: E.activation(
                                out=stg[si], in_=acc[ci], func=AF.Silu, bias=convb[:, cti:cti + 1]),
                                 reads=[acc_b[ci], convb_b], writes=[stg_b[si]])
                            col0 = gidx * 512 + ct * 128

                            def part_b(si=si, kind=kind, col0=col0, tok0=tok0):
                                if kind == "X":
                                    transposed_store(stg[si], stg_b[si], s_xs, tok0, col0)
                                elif kind == "B":
                                    fm_store(stg[si], stg_b[si], s_bT, col0, tok0)
                                    transposed_store(stg[si], stg_b[si], s_btm, tok0, col0)
                                else:
                                    fm_store(stg[si], stg_b[si], s_cT, col0, tok0)
                            defer(part_b)
            else:
                for tt in range(TCH // 128):
                    pi = next_ps()
                    for kc in range(KC):
                        P.op("pe", lambda E, pi=pi, kc=kc, tt=tt, W=W, ncols=ncols: E.matmul(
                            PS[pi][:, 0:ncols], lhsT=uT[:, kc, tt * 128:(tt + 1) * 128],
                            rhs=W[:, kc, 0:ncols], start=(kc == 0), stop=(kc == KC - 1)),
                             reads=[Wb[kc // 8], uT_b], writes=[PSB[pi]])
                    tok0 = t0 + tt * 128
                    if kind == "DT":
                        di = state["dts"] % 2
                        state["dts"] += 1
                        d3 = _r3(dts[di], 64)
                        db = dts_b[di]
                        P.op("dve", lambda E, pi=pi, d3=d3: E.tensor_tensor(out=d3[:, 0, :], in0=PS[pi][:, 0:64], in1=dtb, op=ALU.add),
                             reads=[PSB[pi], dtb_b], writes=[(db, 0)])
                        P.op("act", lambda E, d3=d3: E.activation(out=d3[:, 1, :], in_=d3[:, 0, :], func=AF.Abs),
                             reads=[(db, 0)], writes=[(db, 1)])
                        P.op("act", lambda E, d3=d3: E.activation(out=d3[:, 2, :], in_=d3[:, 1, :], func=AF.Exp, scale=-1.0),
                             reads=[(db, 1)], writes=[(db, 2)])
                        P.op("act", lambda E, d3=d3: E.activation(out=d3[:, 1, :], in_=d3[:, 2, :], func=AF.Ln, bias=1.0),
                             reads=[(db, 2)], writes=[(db, 1)])
                        P.op("dve", lambda E, d3=d3: E.scalar_tensor_tensor(out=d3[:, 3, :], in0=d3[:, 0, :], scalar=0.0, in1=d3[:, 1, :],
                                                                            op0=ALU.max, op1=ALU.add),
                             reads=[(db, 0), (db, 1)], writes=[(db, 3)])
                        P.dma("sp", s_dt[tok0:tok0 + 128, :], d3[:, 3, :], db, reads=[(db, 3)])
                    else:
                        si = next_stg()
                        if kind == "V":
                            P.op("act", lambda E, pi=pi, si=si: E.copy(out=stg[si], in_=PS[pi][:, :]),
                                 reads=[PSB[pi]], writes=[stg_b[si]])
                            dst = s_v[tok0:tok0 + 128, gidx * 512:(gidx + 1) * 512]
                        else:
                            P.op("act", lambda E, pi=pi, si=si: E.activation(out=stg[si], in_=PS[pi][:, :], func=AF.Silu),
                                 reads=[PSB[pi]], writes=[stg_b[si]])
                            dd = s_sg if kind == "G" else s_sz
                            dst = dd[tok0 - TP:tok0 - TP + 128, gidx * 512:(gidx + 1) * 512]
                        P.dma("sp", dst, stg[si], stg_b[si], reads=[stg_b[si]])
                    drain(1)

    drain(len(deferred))
    P.barrier()
    if stop_after == 1:
        return finalize()
    AR.reset()
    for b in PSB:
        b.st = {}

    NQT = TM // 128
    NQC = TM // 512
    NKT = T // 128
    PKT = TP // 128
    gb_c, gb_b = cload("gb", c_gb, NQT * NBLK)
    ob_c, ob_b = cload("ob", c_ob, NQT * NBLK)
    caus, caus_b = cload("caus", c_caus, 4 * 512, BF16, cast=True)
    kT = [AR.alloc(T, BF16) for _ in range(2)]
    kT_b = [Buf("kT0"), Buf("kT1")]
    qT = [AR.alloc(TM, BF16) for _ in range(2)]
    qT_b = [Buf("qT0"), Buf("qT1")]
    va = [_r3(AR.alloc(NKT * 132, BF16), 132) for _ in range(2)]
    va_b = [Buf("va0"), Buf("va1")]
    sgt = [_r3(AR.alloc(NQT * 128, BF16), 128) for _ in range(2)]
    sgt_b = [Buf("sg0"), Buf("sg1")]
    kmf = AR.alloc(NBLK)
    kmf_b = Buf("kmf")
    kmb = AR.alloc(NBLK, BF16)
    kmb_b = Buf("kmb")
    NG = NQT * NBLK
    gA = AR.alloc(NG)
    gB = AR.alloc(NG)
    gC = AR.alloc(NG)
    gM = AR.alloc(NQT)
    gA_b, gB_b, gC_b, gM_b = Buf("gA"), Buf("gB"), Buf("gC"), Buf("gM")
    gbf = AR.alloc(NG, BF16)
    gbf_b = Buf("gbf")
    biasT = AR.alloc(TM, BF16, NBLK)
    biasT_b = Buf("biasT")
    NPT = 6
    pt = [AR.alloc(512, BF16) for _ in range(NPT)]
    pt_b = [Buf("pt%d" % i) for i in range(NPT)]
    rinv = AR.alloc(8)
    rinv_b = Buf("rinv")
    ogs = [AR.alloc(128, BF16) for _ in range(4)]
    ogs_b = [Buf("ogs%d" % i) for i in range(4)]
    ogst = [AR.alloc(512, BF16) for _ in range(2)]
    ogst_b = [Buf("ogst0"), Buf("ogst1")]

    for s in range(2):
        P.op("dve", lambda E, s=s: E.memset(va[s][:, :, 128:129], 1.0), writes=[(va_b[s], "one")])

    def attn_load(h):
        s = h % 2
        P.dma("sp", kT[s], s_kT[h * 128:(h + 1) * 128, :], kT_b[s], writes=[kT_b[s]])
        P.dma("sp", qT[s], s_qT[h * 128:(h + 1) * 128, :], qT_b[s], writes=[qT_b[s]])
        P.dma("sp", va[s][:, :, 0:128], s_v[:, h * 128:(h + 1) * 128].rearrange("(t p) c -> p t c", p=128),
              va_b[s], writes=[(va_b[s], "v")])
        P.dma("sp", sgt[s], s_sg[:, h * 128:(h + 1) * 128].rearrange("(t p) c -> p t c", p=128),
              sgt_b[s], writes=[sgt_b[s]])

    s_mask = [dscr("s_mask%d" % i, [NBLK, TM], BF16) for i in range(2)]
    smask_b = [Buf("smask0"), Buf("smask1")]
    mfull = [AR.alloc(NBLK * TM, BF16) for _ in range(2)]
    mfull_b = [[Buf("mf%d_%d" % (i, q)) for q in range(4)] for i in range(2)]
    pt_i = 0
    og_i = 0

    def prologue(h):
        s = h % 2
        K_, Q_, V_, SGt = kT[s], qT[s], va[s], sgt[s]
        P.op("dve", lambda E, K_=K_: E.tensor_reduce(out=kmf, in_=_r3(K_, BLK), axis=AX.X, op=ALU.add),
             reads=[kT_b[s]], writes=[kmf_b])
        P.op("act", lambda E: E.activation(out=kmb, in_=kmf, func=AF.Copy, scale=1.0 / BLK),
             reads=[kmf_b], writes=[kmb_b])
        for qt in range(NQT):
            P.op("pe", lambda E, qt=qt, Q_=Q_: E.matmul(PS[7][:, qt * NBLK:(qt + 1) * NBLK], lhsT=Q_[:, qt * 128:(qt + 1) * 128],
                                                        rhs=kmb, start=True, stop=True),
                 reads=[qT_b[s], kmb_b], writes=[(PSB[7], qt)])
        g3 = lambda a: _r3(a, NBLK)
        mb = lambda: gM.unsqueeze(2).to_broadcast([128, NQT, NBLK])
        P.op("dve", lambda E: E.tensor_tensor(out=gA, in0=PS[7][:, 0:NG], in1=gb_c, op=ALU.add),
             reads=[PSB[7], gb_b], writes=[gA_b])
        P.op("dve", lambda E: E.tensor_reduce(out=gM, in_=g3(gA), axis=AX.X, op=ALU.max), reads=[gA_b], writes=[gM_b])
        P.op("dve", lambda E: E.tensor_tensor(out=g3(gC), in0=g3(gA), in1=mb(), op=ALU.is_ge), reads=[gA_b, gM_b], writes=[gC_b])
        P.op("dve", lambda E: E.scalar_tensor_tensor(out=gB, in0=gC, scalar=-BIG, in1=gA, op0=ALU.mult, op1=ALU.add),
             reads=[gC_b, gA_b], writes=[gB_b])
        P.op("dve", lambda E: E.tensor_reduce(out=gM, in_=g3(gB), axis=AX.X, op=ALU.max), reads=[gB_b], writes=[gM_b])
        P.op("dve", lambda E: E.tensor_tensor(out=g3(gC), in0=g3(gB), in1=mb(), op=ALU.is_ge), reads=[gB_b, gM_b], writes=[gC_b])
        P.op("dve", lambda E: E.scalar_tensor_tensor(out=gB, in0=gC, scalar=-BIG, in1=gB, op0=ALU.mult, op1=ALU.add),
             reads=[gC_b, gB_b], writes=[gB_b])
        P.op("dve", lambda E: E.tensor_reduce(out=gM, in_=g3(gB), axis=AX.X, op=ALU.max), reads=[gB_b], writes=[gM_b])
        P.op("dve", lambda E: E.tensor_tensor(out=g3(gC), in0=g3(gA), in1=mb(), op=ALU.is_ge), reads=[gA_b, gM_b], writes=[gC_b])
        P.op("dve", lambda E: E.tensor_scalar(out=gB, in0=gC, scalar1=BIG, scalar2=-BIG, op0=ALU.mult, op1=ALU.add),
             reads=[gC_b], writes=[gB_b])
        P.op("dve", lambda E: E.tensor_tensor(out=gA, in0=gB, in1=gb_c, op=ALU.add), reads=[gB_b, gb_b, gA_b], writes=[gA_b])
        P.op("dve", lambda E: E.tensor_tensor(out=gB, in0=gA, in1=ob_c, op=ALU.max), reads=[gA_b, ob_b, gB_b], writes=[gB_b])
        P.op("dve", lambda E: E.tensor_single_scalar(out=gbf, in_=gB, scalar=-1.0, op=ALU.is_ge), reads=[gB_b], writes=[gbf_b])
        for half in range(NQT // 8):
            psb = PS[7].bitcast(BF16)
            for q8 in range(8):
                qt = half * 8 + q8
                P.op("pe", lambda E, psb=psb, q8=q8, qt=qt: E.transpose(psb[0:NBLK, q8 * 128:(q8 + 1) * 128],
                                                                        gbf[:, qt * NBLK:(qt + 1) * NBLK], ident),
                     reads=[gbf_b, ident_b], writes=[(PSB[7], q8)])
            P.op("act", lambda E, psb=psb, half=half: E.copy(out=biasT[:, half * 1024:(half + 1) * 1024], in_=psb[0:NBLK, 0:1024]),
                 reads=[PSB[7]], writes=[(biasT_b, half)])
        P.dma("sp", s_mask[s], biasT, biasT_b, reads=[biasT_b], writes=[smask_b[s]])
        for q4 in range(4):
            nb4 = NBLK // 4
            P.dma("sp", mfull[s][:, q4 * nb4 * TM:(q4 + 1) * nb4 * TM],
                  s_mask[s][q4 * nb4:(q4 + 1) * nb4, :].rearrange("b t -> (b t)").partition_broadcast(128),
                  mfull_b[s][q4], reads=[smask_b[s]], writes=[mfull_b[s][q4]])

    def main_qc(h, qc):
        nonlocal pt_i, og_i
        s = h % 2
        K_, Q_, V_, SGt, MF = kT[s], qT[s], va[s], sgt[s], mfull[s]
        if True:
            nkt = PKT + 4 * (qc + 1)

            def emit_s(kt, qc=qc):
                nonlocal pt_i
                pi = (0, 1, 6)[kt % 3]
                blk = kt // 2
                P.op("pe", lambda E, pi=pi, kt=kt, qc=qc, K_=K_, Q_=Q_: E.matmul(
                    PS[pi][:, :], lhsT=K_[:, kt * 128:(kt + 1) * 128], rhs=Q_[:, qc * 512:(qc + 1) * 512], start=True, stop=True),
                     reads=[kT_b[s], qT_b[s]], writes=[PSB[pi]])
                pj = pt_i % NPT
                pt_i += 1
                P.op("act", lambda E, pi=pi, pj=pj: E.activation(out=pt[pj], in_=PS[pi][:, :], func=AF.Exp, scale=SCALE),
                     reads=[PSB[pi]], writes=[pt_b[pj]])
                moff = blk * TM + qc * 512
                P.op("dve", lambda E, pj=pj, moff=moff, MF=MF: E.tensor_tensor(out=pt[pj], in0=pt[pj], in1=MF[:, moff:moff + 512], op=ALU.mult),
                     reads=[pt_b[pj], mfull_b[s][blk // (NBLK // 4)]], writes=[pt_b[pj]])
                dj = kt - (PKT + 4 * qc)
                if dj >= 0:
                    P.op("dve", lambda E, pj=pj, dj=dj: E.tensor_tensor(out=pt[pj], in0=pt[pj], in1=caus[:, dj * 512:(dj + 1) * 512], op=ALU.mult),
                         reads=[pt_b[pj], caus_b], writes=[pt_b[pj]])
                return pj

            def emit_pv(kt, pj, nkt=nkt):
                for qs in range(4):
                    P.op("pe", lambda E, qs=qs, pj=pj, kt=kt, nkt=nkt, V_=V_: E.matmul(
                        PS[2 + qs][:, 0:129], lhsT=pt[pj][:, qs * 128:(qs + 1) * 128], rhs=V_[:, kt, 0:129],
                        start=(kt == 0), stop=(kt == nkt - 1)),
                         reads=[pt_b[pj], va_b[s]], writes=[PSB[2 + qs]])

            pend = []
            for kt in range(nkt):
                pend.append((kt, emit_s(kt)))
                if len(pend) > 2:
                    emit_pv(*pend.pop(0))
            while pend:
                emit_pv(*pend.pop(0))
            oi = og_i % 2
            og_i += 1
            psb = PS[7].bitcast(BF16)
            for qs in range(4):
                qt = qc * 4 + qs
                P.op("dve", lambda E, qs=qs: E.reciprocal(out=rinv[:, qs:qs + 1], in_=PS[2 + qs][:, 128:129]),
                     reads=[PSB[2 + qs]], writes=[(rinv_b, qs)])
                P.op("dve", lambda E, qs=qs, qt=qt, SGt=SGt: E.scalar_tensor_tensor(
                    out=ogs[qs], in0=PS[2 + qs][:, 0:128], scalar=rinv[:, qs:qs + 1], in1=SGt[:, qt, :], op0=ALU.mult, op1=ALU.mult),
                     reads=[PSB[2 + qs], (rinv_b, qs), sgt_b[s]], writes=[ogs_b[qs]])
                P.op("pe", lambda E, qs=qs, psb=psb: E.transpose(psb[:, qs * 128:(qs + 1) * 128], ogs[qs], ident),
                     reads=[ogs_b[qs], ident_b], writes=[(PSB[7], qs)])
            P.op("act", lambda E, psb=psb, oi=oi: E.copy(out=ogst[oi], in_=psb[:, 0:512]), reads=[PSB[7]], writes=[ogst_b[oi]])
            P.dma("sp", s_ogT[h * 128:(h + 1) * 128, qc * 512:(qc + 1) * 512], ogst[oi], ogst_b[oi], reads=[ogst_b[oi]])


    attn_load(0)
    prologue(0)
    for h in range(NH):
        if h + 1 < NH:
            attn_load(h + 1)
        main_qc(h, 0)
        if h + 1 < NH:
            prologue(h + 1)
        for qc in range(1, NQC):
            main_qc(h, qc)

    P.barrier()
    if stop_after == 2:
        return finalize()
    AR.reset()
    for b in PSB:
        b.st = {}

    alog, alog_b = cload("alog", c_alog, SH)
    dsk, dsk_b = cload("dsk", c_dsk, SSD_IN)
    snw, snw_b = cload("snw", c_snw, SSD_IN)
    Aneg = AR.alloc(SH)
    Aneg_b = Buf("Aneg")
    P.op("act", lambda E: E.activation(out=Aneg, in_=alog, func=AF.Exp), reads=[alog_b], writes=[Aneg_b])
    P.op("dve", lambda E: E.tensor_single_scalar(out=Aneg, in_=Aneg, scalar=-1.0, op=ALU.mult), reads=[Aneg_b], writes=[Aneg_b])
    xs_t = [AR.alloc(SSD_IN, BF16) for _ in range(2)]
    xs_b = [Buf("xs0"), Buf("xs1")]
    btm_t = [AR.alloc(SG * SN, BF16) for _ in range(2)]
    btm_b = [Buf("btm0"), Buf("btm1")]
    bT_t = [AR.alloc(SG * 128, BF16) for _ in range(2)]
    bT_b = [Buf("bT0"), Buf("bT1")]
    cT_t = [AR.alloc(SG * 128, BF16) for _ in range(2)]
    cT_b = [Buf("cT0"), Buf("cT1")]
    dt_t = [AR.alloc(SH) for _ in range(2)]
    dt_b = [Buf("dt0"), Buf("dt1")]
    sz_t = [AR.alloc(SSD_IN, BF16) for _ in range(2)]
    sz_b = [Buf("sz0"), Buf("sz1")]
    stS = AR.alloc(SSD_IN)
    stS_b = Buf("state")
    stB = AR.alloc(SSD_IN, BF16)
    stB_b = Buf("stateb")
    a_tm = AR.alloc(SH)
    a_b = Buf("a_tm")
    acs = AR.alloc(SH)
    acs_b = Buf("acs")
    E_tm = AR.alloc(SH)
    E_b = Buf("E_tm")
    wl = AR.alloc(SH)
    wl_b = Buf("wl")
    w_tm = AR.alloc(SH)
    w_b = Buf("w_tm")
    Etot = AR.alloc(SH)
    Etot_b = Buf("Etot")
    dtw = AR.alloc(SH)
    dtw_b = Buf("dtw")
    xdt = AR.alloc(SSD_IN, BF16)
    xdt_b = Buf("xdt")
    xdtw = AR.alloc(SSD_IN, BF16)
    xdtw_b = Buf("xdtw")
    rall = [AR.alloc(512) for _ in range(2)]
    rall_b = [Buf("rall0"), Buf("rall1")]
    decT = [AR.alloc(512, BF16) for _ in range(2)]
    decT_b = [Buf("decT0"), Buf("decT1")]
    cbm = [AR.alloc(128, BF16) for _ in range(2)]
    cbm_b = [Buf("cbm0"), Buf("cbm1")]
    MT = [AR.alloc(512, BF16) for _ in range(2)]
    MT_b = [Buf("MT0"), Buf("MT1")]
    yv = [AR.alloc(512) for _ in range(2)]
    yv_b = [Buf("yv0"), Buf("yv1")]
    y2 = [AR.alloc(512) for _ in range(2)]
    y2_b = [Buf("y20"), Buf("y21")]
    ysq = AR.alloc(512)
    ysq_b = Buf("ysq")
    ssn = AR.alloc(8)
    ssn_b = Buf("ssn")
    ynb = [AR.alloc(512, BF16) for _ in range(2)]
    ynb_b = [Buf("ynb0"), Buf("ynb1")]
    ynst = [AR.alloc(512, BF16) for _ in range(2)]
    ynst_b = [Buf("ynst0"), Buf("ynst1")]

    P.op("dve", lambda E: E.memset(stS, 0.0), writes=[stS_b])
    P.op("dve", lambda E: E.memset(stB, 0.0), writes=[stB_b])

    NCH = T // 128
    PCH = TP // 128

    def ssd_load(c):
        s = c % 2
        r0 = c * 128
        P.dma("sp", xs_t[s], s_xs[r0:r0 + 128, :], xs_b[s], writes=[xs_b[s]])
        P.dma("sp", btm_t[s], s_btm[r0:r0 + 128, :], btm_b[s], writes=[btm_b[s]])
        P.dma("sp", _r3(bT_t[s], 128), s_bT[:, r0:r0 + 128].rearrange("(g n) s -> n g s", n=128), bT_b[s], writes=[bT_b[s]])
        P.dma("sp", dt_t[s], s_dt[r0:r0 + 128, :], dt_b[s], writes=[dt_b[s]])
        if c >= PCH:
            m0 = r0 - TP
            P.dma("sp", _r3(cT_t[s], 128), s_cT[:, r0:r0 + 128].rearrange("(g n) s -> n g s", n=128), cT_b[s], writes=[cT_b[s]])
            P.dma("sp", sz_t[s], s_sz[m0:m0 + 128, :], sz_b[s], writes=[sz_b[s]])

    onesf = AR.alloc(128)
    onesf_b = Buf("onesf")
    P.op("dve", lambda E: E.memset(onesf, 1.0), writes=[onesf_b])
    caus01 = AR.alloc(128, BF16)
    caus01_b = Buf("caus01")
    P.op("dve", lambda E: E.tensor_copy(out=caus01, in_=tri), reads=[tri_b], writes=[caus01_b])

    ssd_load(0)
    ri = 0
    for c in range(NCH):
        s = c % 2
        main = c >= PCH
        if c + 1 < NCH:
            ssd_load(c + 1)
        XS, BTM, BT, CT, DT, SZ = xs_t[s], btm_t[s], _r3(bT_t[s], 128), _r3(cT_t[s], 128), dt_t[s], sz_t[s]
        xsb, btmb, bTb, cTb, dtb_, szb = xs_b[s], btm_b[s], bT_b[s], cT_b[s], dt_b[s], sz_b[s]
        P.op("dve", lambda E, DT=DT: E.tensor_tensor(out=a_tm, in0=DT, in1=Aneg, op=ALU.mult), reads=[dtb_, Aneg_b], writes=[a_b])
        P.op("pe", lambda E: E.matmul(PS[6][:, 0:64], lhsT=tri, rhs=a_tm, start=True, stop=True), reads=[tri_b, a_b], writes=[(PSB[6], 0)])
        P.op("pe", lambda E: E.matmul(PS[6][:, 64:128], lhsT=onesf, rhs=a_tm, start=True, stop=True), reads=[onesf_b, a_b], writes=[(PSB[6], 1)])
        P.op("act", lambda E: E.copy(out=acs, in_=PS[6][:, 0:64]), reads=[(PSB[6], 0)], writes=[acs_b])
        P.op("act", lambda E: E.activation(out=E_tm, in_=PS[6][:, 0:64], func=AF.Exp), reads=[(PSB[6], 0)], writes=[E_b])
        P.op("act", lambda E: E.activation(out=Etot, in_=PS[6][:, 64:128], func=AF.Exp), reads=[(PSB[6], 1)], writes=[Etot_b])
        P.op("dve", lambda E: E.tensor_tensor(out=wl, in0=PS[6][:, 64:128], in1=acs, op=ALU.subtract), reads=[(PSB[6], 1), acs_b], writes=[wl_b])
        P.op("act", lambda E: E.activation(out=w_tm, in_=wl, func=AF.Exp), reads=[wl_b], writes=[w_b])
        P.op("dve", lambda E, DT=DT: E.tensor_tensor(out=dtw, in0=DT, in1=w_tm, op=ALU.mult), reads=[dtb_, w_b], writes=[dtw_b])
        if c < PCH:
            P.op("dve", lambda E: E.tensor_scalar(out=dtw, in0=dtw, scalar1=flag[:, 0:1], scalar2=None, op0=ALU.mult),
                 reads=[dtw_b, flag_b], writes=[dtw_b])
        x3 = lambda a: _r3(a, SP_)
        P.op("dve", lambda E, XS=XS: E.tensor_tensor(out=x3(xdtw), in0=x3(XS), in1=dtw.unsqueeze(2).to_broadcast([128, SH, SP_]), op=ALU.mult),
             reads=[xsb, dtw_b], writes=[xdtw_b])
        if main:
            P.op("dve", lambda E, XS=XS, DT=DT: E.tensor_tensor(out=x3(xdt), in0=x3(XS), in1=DT.unsqueeze(2).to_broadcast([128, SH, SP_]), op=ALU.mult),
                 reads=[xsb, dtb_], writes=[xdt_b])
        def front(g):
            nonlocal ri
            gs = slice(g * 512, (g + 1) * 512)
            if True:
                ci = g % 2
                P.op("pe", lambda E, g=g, BT=BT, CT=CT: E.matmul(PS[7][:, 0:128], lhsT=BT[:, g, :], rhs=CT[:, g, :], start=True, stop=True),
                     reads=[bTb, cTb], writes=[PSB[7]])
                P.op("dve", lambda E, ci=ci: E.tensor_tensor(out=cbm[ci], in0=PS[7][:, 0:128], in1=caus01, op=ALU.mult),
                     reads=[PSB[7], caus01_b], writes=[cbm_b[ci]])
                ypi = 4 + g % 2
                for qd in range(2):
                    h0 = g * 8 + qd * 4
                    rj = ri % 2
                    ri += 1
                    P.op("dve", lambda E, rj=rj, h0=h0: E.tensor_tensor(
                        out=_r3(rall[rj], 128), in0=tri.unsqueeze(1).to_broadcast([128, 4, 128]),
                        in1=a_tm[:, h0:h0 + 4].unsqueeze(2).to_broadcast([128, 4, 128]), op=ALU.mult),
                         reads=[tri_b, a_b], writes=[rall_b[rj]])
                    dpi = rj
                    P.op("pe", lambda E, dpi=dpi, rj=rj: E.matmul(PS[dpi][:, :], lhsT=ustr, rhs=rall[rj], start=True, stop=True),
                         reads=[ustr_b, rall_b[rj]], writes=[PSB[dpi]])
                    P.op("act", lambda E, dpi=dpi, rj=rj: E.activation(out=decT[rj], in_=PS[dpi][:, :], func=AF.Exp),
                         reads=[PSB[dpi]], writes=[decT_b[rj]])
                    P.op("dve", lambda E, rj=rj, ci=ci: E.tensor_tensor(
                        out=_r3(MT[rj], 128), in0=_r3(decT[rj], 128), in1=cbm[ci].unsqueeze(1).to_broadcast([128, 4, 128]), op=ALU.mult),
                         reads=[decT_b[rj], cbm_b[ci]], writes=[MT_b[rj]])
                    for hh in range(4):
                        h = h0 + hh
                        col = (qd * 4 + hh) * 64
                        P.op("pe", lambda E, ypi=ypi, rj=rj, hh=hh, h=h, col=col: E.matmul(
                            PS[ypi][:, col:col + 64], lhsT=MT[rj][:, hh * 128:(hh + 1) * 128], rhs=xdt[:, h * 64:(h + 1) * 64],
                            start=True, stop=True),
                             reads=[MT_b[rj], xdt_b], writes=[(PSB[ypi], col)])
                opi = 2 + g % 2
                P.op("pe", lambda E, opi=opi, g=g, gs=gs, CT=CT: E.matmul(PS[opi][:, :], lhsT=CT[:, g, :], rhs=stB[:, gs], start=True, stop=True),
                     reads=[cTb, (stB_b, g)], writes=[PSB[opi]])

        def back(g):
            gs = slice(g * 512, (g + 1) * 512)
            ypi = 4 + g % 2
            opi = 2 + g % 2
            if main:
                yj = g % 2
                hs = slice(g * 8, g * 8 + 8)
                P.op("dve", lambda E, opi=opi, yj=yj, hs=hs: E.tensor_tensor(
                    out=x3(yv[yj]), in0=x3(PS[opi][:, :]), in1=E_tm[:, hs].unsqueeze(2).to_broadcast([128, 8, SP_]), op=ALU.mult),
                     reads=[PSB[opi], E_b], writes=[yv_b[yj]])
                P.op("dve", lambda E, ypi=ypi, yj=yj: E.tensor_tensor(out=yv[yj], in0=yv[yj], in1=PS[ypi][:, :], op=ALU.add),
                     reads=[yv_b[yj], PSB[ypi]], writes=[yv_b[yj]])
                P.op("dve", lambda E, yj=yj, gs=gs, XS=XS: E.tensor_tensor(out=y2[yj], in0=XS[:, gs], in1=dsk[:, gs], op=ALU.mult),
                     reads=[xsb, dsk_b], writes=[y2_b[yj]])
                P.op("dve", lambda E, yj=yj: E.tensor_tensor(out=yv[yj], in0=yv[yj], in1=y2[yj], op=ALU.add),
                     reads=[yv_b[yj], y2_b[yj]], writes=[yv_b[yj]])
                P.op("dve", lambda E, yj=yj, gs=gs, SZ=SZ: E.tensor_tensor(out=yv[yj], in0=yv[yj], in1=SZ[:, gs], op=ALU.mult),
                     reads=[yv_b[yj], szb], writes=[yv_b[yj]])
                P.op("act", lambda E, yj=yj: E.activation(out=ysq, in_=yv[yj], func=AF.Square, accum_out=ssn[:, 0:1]),
                     reads=[yv_b[yj]], writes=[ysq_b, (ssn_b, 0)])
                P.op("dve", lambda E: E.tensor_scalar(out=ssn[:, 1:2], in0=ssn[:, 0:1], scalar1=1.0 / 512, scalar2=EPS, op0=ALU.mult, op1=ALU.add),
                     reads=[(ssn_b, 0)], writes=[(ssn_b, 1)])
                P.op("act", lambda E: E.activation(out=ssn[:, 3:4], in_=ssn[:, 1:2], func=AF.Ln), reads=[(ssn_b, 1)], writes=[(ssn_b, 3)])
                P.op("act", lambda E: E.activation(out=ssn[:, 2:3], in_=ssn[:, 3:4], func=AF.Exp, scale=-0.5), reads=[(ssn_b, 3)], writes=[(ssn_b, 2)])
                P.op("dve", lambda E, yj=yj, gs=gs: E.scalar_tensor_tensor(
                    out=ynb[yj], in0=yv[yj], scalar=ssn[:, 2:3], in1=snw[:, gs], op0=ALU.mult, op1=ALU.mult),
                     reads=[yv_b[yj], (ssn_b, 2), snw_b], writes=[ynb_b[yj]])
                psb = PS[6].bitcast(BF16)
                for j in range(4):
                    P.op("pe", lambda E, j=j, yj=yj, psb=psb: E.transpose(psb[:, 512 + j * 128:512 + (j + 1) * 128], ynb[yj][:, j * 128:(j + 1) * 128], ident),
                         reads=[ynb_b[yj], ident_b], writes=[(PSB[6], 10 + j)])
                P.op("act", lambda E, yj=yj, psb=psb: E.copy(out=ynst[yj], in_=psb[:, 512:1024]),
                     reads=[(PSB[6], 10), (PSB[6], 11), (PSB[6], 12), (PSB[6], 13)], writes=[ynst_b[yj]])
                m0 = c * 128 - TP
                P.dma("sp", s_ynT[g * 512:(g + 1) * 512, m0:m0 + 128].rearrange("(j p) t -> p j t", p=128),
                      _r3(ynst[yj], 128), ynst_b[yj], reads=[ynst_b[yj]])
            if c + 1 < NCH:
                spi = 2 + g % 2
                P.op("pe", lambda E, spi=spi, g=g, gs=gs, BTM=BTM: E.matmul(PS[spi][:, :], lhsT=BTM[:, g * 128:(g + 1) * 128], rhs=xdtw[:, gs], start=True, stop=True),
                     reads=[btmb, xdtw_b], writes=[PSB[spi]])
                hs = slice(g * 8, g * 8 + 8)
                P.op("dve", lambda E, gs=gs, hs=hs: E.tensor_tensor(
                    out=x3(stS[:, gs]), in0=x3(stS[:, gs]), in1=Etot[:, hs].unsqueeze(2).to_broadcast([128, 8, SP_]), op=ALU.mult),
                     reads=[(stS_b, g), Etot_b], writes=[(stS_b, g)])
                P.op("dve", lambda E, gs=gs, spi=spi: E.tensor_tensor(out=stS[:, gs], in0=stS[:, gs], in1=PS[spi][:, :], op=ALU.add),
                     reads=[(stS_b, g), PSB[spi]], writes=[(stS_b, g)])
                P.op("act", lambda E, gs=gs: E.copy(out=stB[:, gs], in_=stS[:, gs]), reads=[(stS_b, g)], writes=[(stB_b, g)])


        if main:
            front(0)
            for g in range(1, SG):
                front(g)
                back(g - 1)
            back(SG - 1)
        else:
            for g in range(SG):
                back(g)

    P.barrier()
    if stop_after == 3:
        return finalize()
    AR.reset()
    for b in PSB:
        b.st = {}

    NTC3 = TM // 512
    NTT = TM // 128
    ogc = _r3(AR.alloc(16 * 512, BF16), 512)
    ogc_b = Buf("ogc")
    ync = _r3(AR.alloc(32 * 512, BF16), 512)
    ync_b = Buf("ync")
    mrg = _r3(AR.alloc(32 * 512, BF16), 512)
    mrg_b = Buf("mrg")
    w3 = [_r3(AR.alloc(KC * 512, BF16), 512) for _ in range(2)]
    w3_b = [[Buf("w3%d_%d" % (s_, q_)) for q_ in range(4)] for s_ in range(2)]
    sga = [AR.alloc(512, BF16) for _ in range(2)]
    sga_b = [Buf("sga0"), Buf("sga1")]
    sgs = [AR.alloc(512, BF16) for _ in range(2)]
    sgs_b = [Buf("sgs0"), Buf("sgs1")]
    m1 = [AR.alloc(512) for _ in range(2)]
    m1_b = [Buf("m10"), Buf("m11")]
    m2 = [AR.alloc(512) for _ in range(2)]
    m2_b = [Buf("m20"), Buf("m21")]
    xsl = [AR.alloc(512) for _ in range(2)]
    xsl_b = [Buf("xsl0"), Buf("xsl1")]
    hst = [AR.alloc(512) for _ in range(2)]
    hst_b = [Buf("hst0"), Buf("hst1")]
    hsq = AR.alloc(512, BF16)
    hsq_b = Buf("hsq")
    w_ao_r = w_ao.rearrange("(kc p) c -> p kc c", p=128)
    w_so_r = w_so.rearrange("(kc p) c -> p kc c", p=128)
    w_o_r = w_o.rearrange("(kc p) c -> p kc c", p=128)
    wq = []
    for g in range(8):
        wq.append(("A", w_ao_r, g, 16))
        wq.append(("S", w_so_r, g, 32))
    for g in range(8):
        wq.append(("O", w_o_r, g, 32))
    wcnt = [0]

    def load_w3(slot, src_r, g, nkc):
        for q in range(0, nkc, 8):
            P.dma("pool", w3[slot][:, q:q + 8, :], src_r[:, q:q + 8, g * 512:(g + 1) * 512], w3_b[slot][q // 8], writes=[w3_b[slot][q // 8]])

    all_w = [(tc3, e) for tc3 in range(NTC3) for e in wq]
    load_w3(0, all_w[0][1][1], all_w[0][1][2], all_w[0][1][3])
    wi = 0
    ei = 0
    for tc3 in range(NTC3):
        c0 = tc3 * 512
        P.dma("sp", ogc, s_ogT[:, c0:c0 + 512].rearrange("(kc p) t -> p kc t", p=128), ogc_b, writes=[ogc_b])
        P.dma("sp", ync, s_ynT[:, c0:c0 + 512].rearrange("(kc p) t -> p kc t", p=128), ync_b, writes=[ync_b])
        for (kind, src_r, g, nkc) in wq:
            slot = wi % 2
            wi += 1
            if wi < len(all_w):
                nx = all_w[wi][1]
                load_w3(wi % 2, nx[1], nx[2], nx[3])
            W = w3[slot]
            Wb = w3_b[slot]
            if kind == "A":
                for dt_ in range(4):
                    for kc in range(16):
                        P.op("pe", lambda E, dt_=dt_, kc=kc, W=W: E.matmul(PS[dt_][:, :], lhsT=W[:, kc, dt_ * 128:(dt_ + 1) * 128], rhs=ogc[:, kc, :],
                                                                            start=(kc == 0), stop=(kc == 15)),
                             reads=[Wb[kc // 8], ogc_b], writes=[PSB[dt_]])
            elif kind == "S":
                for dt_ in range(4):
                    til = g * 4 + dt_
                    e2 = ei % 2
                    ei += 1
                    P.dma("sp", sga[e2], s_sgm[til * 128:(til + 1) * 128, c0:c0 + 512], sga_b[e2], writes=[sga_b[e2]])
                    P.dma("sp", sgs[e2], s_sgm[D + til * 128:D + (til + 1) * 128, c0:c0 + 512], sgs_b[e2], writes=[sgs_b[e2]])
                    for kc in range(32):
                        P.op("pe", lambda E, dt_=dt_, kc=kc, W=W: E.matmul(PS[4 + dt_][:, :], lhsT=W[:, kc, dt_ * 128:(dt_ + 1) * 128], rhs=ync[:, kc, :],
                                                                            start=(kc == 0), stop=(kc == 31)),
                             reads=[Wb[kc // 8], ync_b], writes=[PSB[4 + dt_]])
                    P.op("dve", lambda E, dt_=dt_, e2=e2: E.tensor_tensor(out=m1[e2], in0=PS[dt_][:, :], in1=sga[e2], op=ALU.mult),
                         reads=[PSB[dt_], sga_b[e2]], writes=[m1_b[e2]])
                    P.op("dve", lambda E, dt_=dt_, e2=e2: E.tensor_tensor(out=m2[e2], in0=PS[4 + dt_][:, :], in1=sgs[e2], op=ALU.mult),
                         reads=[PSB[4 + dt_], sgs_b[e2]], writes=[m2_b[e2]])
                    P.op("dve", lambda E, til=til, e2=e2: E.tensor_tensor(out=mrg[:, til, :], in0=m1[e2], in1=m2[e2], op=ALU.add),
                         reads=[m1_b[e2], m2_b[e2]], writes=[(mrg_b, til)])
            else:
                for tt in range(4):
                    ttg = tc3 * 4 + tt
                    e2 = ei % 2
                    ei += 1
                    pi = (g * 4 + tt) % 8
                    P.dma("sp", xsl[e2], x_all[TP + ttg * 128:TP + (ttg + 1) * 128, g * 512:(g + 1) * 512], xsl_b[e2], writes=[xsl_b[e2]])
                    for kc in range(32):
                        P.op("pe", lambda E, pi=pi, tt=tt, kc=kc, W=W: E.matmul(PS[pi][:, :], lhsT=mrg[:, kc, tt * 128:(tt + 1) * 128], rhs=W[:, kc, :],
                                                                                start=(kc == 0), stop=(kc == 31)),
                             reads=[Wb[kc // 8], mrg_b], writes=[PSB[pi]])
                    P.op("dve", lambda E, pi=pi, e2=e2: E.tensor_tensor(out=hst[e2], in0=PS[pi][:, :], in1=xsl[e2], op=ALU.add),
                         reads=[PSB[pi], xsl_b[e2]], writes=[hst_b[e2]])
                    P.op("act", lambda E, e2=e2, ttg=ttg, g=g: E.activation(out=hsq, in_=hst[e2], func=AF.Square, accum_out=ssf[:, ttg * 8 + g:ttg * 8 + g + 1]),
                         reads=[hst_b[e2]], writes=[hsq_b, (ssf_b, ttg * 8 + g)])
                    P.dma("sp", out[ttg * 128:(ttg + 1) * 128, g * 512:(g + 1) * 512], hst[e2], hst_b[e2], reads=[hst_b[e2]])

    P.barrier()
    AR.reset()
    fnw = AR.alloc(D)
    fnw_b = Buf("fnw")
    P.dma("sp", fnw, c_fnw, fnw_b, writes=[fnw_b])
    hrow = [AR.alloc(D) for _ in range(2)]
    hrow_b = [Buf("hrow0"), Buf("hrow1")]
    rs = AR.alloc(NTT * 2)
    rs_b = Buf("rs")
    ssf_b.st = {}
    for ttg in range(NTT):
        e2 = ttg % 2
        P.dma("sp", hrow[e2], out[ttg * 128:(ttg + 1) * 128, :], hrow_b[e2], writes=[hrow_b[e2]])
        P.op("dve", lambda E, ttg=ttg: E.tensor_reduce(out=rs[:, 2 * ttg:2 * ttg + 1], in_=ssf[:, ttg * 8:(ttg + 1) * 8], axis=AX.X, op=ALU.add),
             reads=[ssf_b], writes=[(rs_b, ttg)])
        P.op("dve", lambda E, ttg=ttg: E.tensor_scalar(out=rs[:, 2 * ttg + 1:2 * ttg + 2], in0=rs[:, 2 * ttg:2 * ttg + 1], scalar1=1.0 / D, scalar2=EPS,
                                                       op0=ALU.mult, op1=ALU.add),
             reads=[(rs_b, ttg)], writes=[(rs_b, ttg)])
        P.op("act", lambda E, ttg=ttg: E.sqrt(out=rs[:, 2 * ttg:2 * ttg + 1], in_=rs[:, 2 * ttg + 1:2 * ttg + 2]),
             reads=[(rs_b, ttg)], writes=[(rs_b, ttg)])
        P.op("dve", lambda E, ttg=ttg: E.reciprocal(out=rs[:, 2 * ttg:2 * ttg + 1], in_=rs[:, 2 * ttg:2 * ttg + 1]),
             reads=[(rs_b, ttg)], writes=[(rs_b, ttg)])
        P.op("dve", lambda E, ttg=ttg, e2=e2: E.scalar_tensor_tensor(out=hrow[e2], in0=hrow[e2], scalar=rs[:, 2 * ttg:2 * ttg + 1], in1=fnw,
                                                                     op0=ALU.mult, op1=ALU.mult),
             reads=[hrow_b[e2], (rs_b, ttg), fnw_b], writes=[hrow_b[e2]])
        P.dma("sp", out[ttg * 128:(ttg + 1) * 128, :], hrow[e2], hrow_b[e2], reads=[hrow_b[e2]])

    return finalize()


def make_consts(TP, TM, pos_main0, prefix_valid):
    T = TP + TM
    NBLK = T // BLK
    NQT = TM // 128
    c = {}
    c["c_ident"] = np.eye(128, dtype=np.float32)
    r = np.zeros((32, 128), np.float32)
    for m in range(32):
        r[(m + 16) % 32, m] = 1.0
    c["c_rot"] = r
    k = np.arange(128)
    c["c_tri"] = (k[:, None] <= k[None, :]).astype(np.float32)
    c["c_ustr"] = (k[:, None] > k[None, :]).astype(np.float32)
    q = np.arange(512)
    caus = np.zeros((128, 4, 512), np.float32)
    for j in range(4):
        caus[:, j, :] = ((j * 128 + k)[:, None] <= q[None, :])
    c["c_caus"] = caus.reshape(128, 2048)
    gb = np.full((NQT, NBLK), -BIG, np.float32)
    ob = np.full((NQT, NBLK), -3 * BIG, np.float32)
    pblk = TP // BLK
    for qt in range(NQT):
        own = pblk + (qt * 128) // BLK
        for b in range(own):
            if b >= pblk or prefix_valid:
                gb[qt, b] = 0.0
        ob[qt, own] = 0.0
    c["c_gb"] = np.ascontiguousarray(np.broadcast_to(gb.reshape(1, -1), (128, NQT * NBLK)))
    c["c_ob"] = np.ascontiguousarray(np.broadcast_to(ob.reshape(1, -1), (128, NQT * NBLK)))
    c["c_flag"] = np.full((128, 1), 1.0 if prefix_valid else 0.0, np.float32)
    half = 16
    inv_freq = (ROPE_THETA ** (-np.arange(half, dtype=np.float32) * 2.0 / 32)).astype(np.float32)
    pos = np.arange(T, dtype=np.float32) - TP + pos_main0
    ang = (pos[None, :] * inv_freq[:, None]).astype(np.float32)
    cos = np.cos(ang).astype(np.float32)
    sin = np.sin(ang).astype(np.float32)
    c["c_cos"] = np.ascontiguousarray(np.concatenate([cos, cos, np.ones((96, T), np.float32)], 0))
    c["c_sin"] = np.ascontiguousarray(np.concatenate([-sin, sin, np.zeros((96, T), np.float32)], 0))
    return c


def make_param_consts(norm_w, conv_w, conv_b, dt_bias, a_log, d_skip, ssd_norm_w, gate_bias, final_norm_w):
    c = {}
    c["c_normw"] = np.ascontiguousarray(norm_w.reshape(KC, 128).T)
    cw = conv_w.reshape(4, 48, 128)
    c["c_convw"] = np.ascontiguousarray(cw.transpose(2, 1, 0).reshape(128, 48 * 4))
    c["c_convb"] = np.ascontiguousarray(conv_b.reshape(48, 128).T)
    c["c_gbias"] = np.ascontiguousarray(gate_bias.reshape(64, 128).T)
    bc = lambda v: np.ascontiguousarray(np.broadcast_to(v.reshape(1, -1), (128, v.size)))
    c["c_dtb"] = bc(dt_bias)
    c["c_alog"] = bc(a_log)
    c["c_dsk"] = bc(np.repeat(d_skip, SP_))
    c["c_snw"] = bc(ssd_norm_w)
    c["c_fnw"] = bc(final_norm_w)
    return c


_CACHE = {}


def kernel(x, norm_w, w_in, conv_w, conv_b, dt_bias, a_log, d_skip, ssd_norm_w,
           w_attn_out, w_ssd_out, gate_bias, w_out, final_norm_w):
    x = np.asarray(x, np.float32)
    B, S, _ = x.shape
    TP = TM = S // 2
    f = lambda a: np.ascontiguousarray(np.asarray(a, np.float32))
    pc = make_param_consts(f(norm_w)[0], f(conv_w)[0], f(conv_b)[0], f(dt_bias)[0], f(a_log)[0], f(d_skip)[0],
                           f(ssd_norm_w)[0], f(gate_bias)[0], f(final_norm_w))
    shared = {"w_in": f(w_in)[0], "w_ao": f(w_attn_out)[0], "w_so": f(w_ssd_out)[0], "w_o": f(w_out)[0]}
    shared.update(pc)
    in_maps = []
    for core in range(8):
        b, j = core // 2, core % 2
        if j == 0:
            xa = np.concatenate([np.zeros((TP, D), np.float32), x[b, 0:TM]], 0)
        else:
            xa = x[b]
        m = {"x_all": np.ascontiguousarray(xa)}
        m.update(shared)
        m.update(make_consts(TP, TM, j * TM, j == 1))
        in_maps.append(m)
    key = (TP, TM)
    if key not in _CACHE:
        _CACHE[key] = build_program(TP, TM)
    nc = _CACHE[key]
    res = run_bass_kernel_spmd(nc, in_maps, core_ids=list(range(8)))
    outp = np.empty((B, S, D), np.float32)
    for core in range(8):
        b, j = core // 2, core % 2
        outp[b, j * TM:(j + 1) * TM] = res.results[core]["out"]
    return outp
```
